# Optimizing a Trainium2 kernel written in Bass

```python
import jax
import jax.numpy as jnp
from jax import lax
import numpy as np

D_MODEL = 4096
BATCH = 4
SEQ = 2048
DEPTH = 2
DEC_BATCH = 8
DEC_SEQ = 8
PAST_LEN = 16384
PAGE_SIZE = 128

N_MIXERS = 2
N_A_LAYERS = (DEPTH + 1) // 2
N_B_LAYERS = DEPTH // 2
RMS_EPS = 1e-6

N_MEM = 256
MEM_HEADS = 4
MEM_HD = D_MODEL // 16
MEM_W = MEM_HEADS * MEM_HD
MIX_W = D_MODEL - MEM_W

A_HD = 64
A_HEADS = MIX_W // A_HD
DECAY_LORA = 96
AAA_LORA = 96
GATE_LORA = 384
A_SHIFT_W = 3 * MIX_W + DECAY_LORA + AAA_LORA + GATE_LORA
A_IN_W = A_SHIFT_W + MEM_W
A_OUT_W = MIX_W + MEM_W
LNX_EPS = 64e-5

B_HD = 128
B_GROUPS = ((128, 1), (512, 4), (2048, 16))
B_HEADS = MIX_W // B_HD
B_GH = B_HEADS // len(B_GROUPS)
B_OUT_W = B_GH * B_HD
B_IN_W = 3 * MIX_W + MEM_W
B_OUT_IN = B_OUT_W + MEM_W

D_FF = -(-(8 * D_MODEL) // (3 * 256)) * 256

kernel_name = 'rwkv7_dilated_swa_memxattn_hybrid_step'


def _rmsnorm(x, g, eps=RMS_EPS):
    xf = x.astype(jnp.float32)
    y = xf * lax.rsqrt(jnp.mean(xf * xf, axis=-1, keepdims=True) + eps)
    return (y * g.astype(jnp.float32)).astype(x.dtype)


def _mem_kv(mem, g_norm, w_kv, g_k):
    b = mem.shape[0]
    kv = (_rmsnorm(mem, g_norm) @ w_kv).reshape(b, N_MEM, 2, MEM_HEADS, MEM_HD)
    return jnp.stack([_rmsnorm(kv[:, :, 0], g_k), kv[:, :, 1]], axis=2)


def _mem_attend(qm, kv, g_q):
    b, t = qm.shape[:2]
    q = _rmsnorm(qm.reshape(b, t, MEM_HEADS, MEM_HD), g_q)
    s = jnp.einsum('bthd,bmhd->bhtm', q, kv[:, :, 0]).astype(jnp.float32) * (MEM_HD ** -0.5)
    p = jax.nn.softmax(s, axis=-1).astype(kv.dtype)
    return jnp.einsum('bhtm,bmhd->bthd', p, kv[:, :, 1]).reshape(b, t, MEM_W)


def _wkv_scan(s0, r, w, k, v, a, b):
    xs = tuple(jnp.moveaxis(z.astype(jnp.float32), 1, 0) for z in (r, w, k, v, a, b))

    def step(s, inp):
        r_t, w_t, k_t, v_t, a_t, b_t = inp
        sa = jnp.einsum('bhvk,bhk->bhv', s, a_t)
        s = s * w_t[:, :, None, :] + sa[..., None] * b_t[:, :, None, :] + v_t[..., None] * k_t[:, :, None, :]
        return s, jnp.einsum('bhvk,bhk->bhv', s, r_t)

    s, ys = lax.scan(step, s0.astype(jnp.float32), xs)
    return s, jnp.moveaxis(ys, 0, 1)


def _rwkv7(u, u_prev0, s0, mu, w0, w2, a0, a2, g2, k_k, k_a, r_k, lnx_g, lnx_b):
    bsz, t = u.shape[:2]
    u_prev = jnp.concatenate([u_prev0.astype(u.dtype), u[:, :-1]], axis=1)
    us = u + (u_prev - u) * mu
    o = 3 * MIX_W
    r, k, v = us[..., :MIX_W], us[..., MIX_W:2 * MIX_W], us[..., 2 * MIX_W:o]
    wl = us[..., o:o + DECAY_LORA]
    al = us[..., o + DECAY_LORA:o + DECAY_LORA + AAA_LORA]
    gl = us[..., o + DECAY_LORA + AAA_LORA:]
    w_log = -jax.nn.softplus(-(w0 + jnp.tanh(wl) @ w2)) - 0.5
    decay = jnp.exp(-jnp.exp(w_log.astype(jnp.float32)))
    a = jax.nn.sigmoid(a0 + al @ a2)
    g = jax.nn.sigmoid(gl) @ g2
    hs = lambda z: z.reshape(bsz, t, A_HEADS, A_HD)
    kk = hs(k * k_k).astype(jnp.float32)
    kk = kk / jnp.maximum(jnp.sqrt(jnp.sum(kk * kk, axis=-1, keepdims=True)), 1e-12)
    k = k * (1 + (a - 1) * k_a)
    ah = hs(a).astype(jnp.float32)
    s, y = _wkv_scan(s0, hs(r), hs(decay), hs(k), hs(v), -kk, kk * ah)
    mean = jnp.mean(y, axis=-1, keepdims=True)
    var = jnp.mean(jnp.square(y - mean), axis=-1, keepdims=True)
    y = ((y - mean) * lax.rsqrt(var + LNX_EPS)).reshape(bsz, t, MIX_W) * lnx_g + lnx_b
    bonus = jnp.sum(hs(r).astype(jnp.float32) * hs(k).astype(jnp.float32) * r_k, axis=-1, keepdims=True) * hs(v).astype(jnp.float32)
    y = (y + bonus.reshape(bsz, t, MIX_W)).astype(u.dtype) * g
    return y, s, u[:, -1:]


def _dswa_qkv(u, g_q, g_k):
    bsz, t = u.shape[:2]
    hs = lambda z: z.reshape(bsz, t, B_HEADS, B_HD)
    q = _rmsnorm(hs(u[..., :MIX_W]), g_q)
    k = _rmsnorm(hs(u[..., MIX_W:2 * MIX_W]), g_k)
    v = hs(u[..., 2 * MIX_W:3 * MIX_W])
    return q, k, v


def _dilated_band_attn(q, k, v, window, dil):
    bsz, t = q.shape[:2]
    span = window // dil
    sub = t // dil
    n_blk = -(-sub // span)
    pad = n_blk * span - sub

    def blocks(z):
        z = z.reshape(bsz, sub, dil, B_GH, B_HD).transpose(0, 2, 1, 3, 4)
        z = jnp.pad(z, ((0, 0), (0, 0), (0, pad), (0, 0), (0, 0)))
        return z.reshape(bsz, dil, n_blk, span, B_GH, B_HD)

    def with_prev(z):
        prev = jnp.pad(z, ((0, 0), (0, 0), (1, 0), (0, 0), (0, 0), (0, 0)))[:, :, :-1]
        return jnp.concatenate([prev, z], axis=3)

    qb = blocks(q)
    kb = with_prev(blocks(k))
    vb = with_prev(blocks(v))
    s = jnp.einsum('brnqhd,brnkhd->brnhqk', qb, kb).astype(jnp.float32) * (B_HD ** -0.5)
    qi = jnp.arange(span)[:, None] + span
    ki = jnp.arange(2 * span)[None, :]
    dist = qi - ki
    band = (dist >= 0) & (dist <= span)
    exists = (jnp.arange(n_blk)[:, None, None] * span + ki[None] - span) >= 0
    mask = band[None] & exists
    s = jnp.where(mask[None, None, :, None], s, -jnp.inf)
    lse = jax.nn.logsumexp(s, axis=-1)
    p = jnp.exp(s - lse[..., None]).astype(v.dtype)
    o = jnp.einsum('brnhqk,brnkhd->brnqhd', p, vb)
    o = o.reshape(bsz, dil, n_blk * span, B_GH, B_HD)[:, :, :sub]
    o = o.transpose(0, 2, 1, 3, 4).reshape(bsz, t, B_GH, B_HD)
    lse = lse.transpose(0, 1, 2, 4, 3).reshape(bsz, dil, n_blk * span, B_GH)[:, :, :sub]
    lse = lse.transpose(0, 2, 1, 3).reshape(bsz, t, B_GH)
    return o, lse


def _dilated_window_step(q, k, v, buf, window, dil):
    rows = buf.shape[1]
    ts = q.shape[1]
    kc = jnp.concatenate([buf[:, :, 0], k], axis=1)
    vc = jnp.concatenate([buf[:, :, 1], v], axis=1)
    span = window // dil
    idx = rows + jnp.arange(ts)[:, None] - dil * jnp.arange(span + 1)[None, :]
    valid = idx >= 0
    idx = jnp.maximum(idx, 0)
    kg = kc[:, idx]
    vg = vc[:, idx]
    s = jnp.einsum('bthd,btjhd->bthj', q, kg).astype(jnp.float32) * (B_HD ** -0.5)
    s = jnp.where(valid[None, :, None, :], s, -jnp.inf)
    lse = jax.nn.logsumexp(s, axis=-1)
    p = jnp.exp(s - lse[..., None]).astype(vg.dtype)
    o = jnp.einsum('bthj,btjhd->bthd', p, vg)
    new_buf = jnp.stack([kc[:, -rows:], vc[:, -rows:]], axis=2)
    return o, lse, new_buf


def _merge_groups(outs, lses):
    wts = jax.nn.softmax(jnp.stack(lses, axis=0), axis=0)
    o = jnp.einsum('gbth,gbthd->bthd', wts, jnp.stack(outs, axis=0).astype(jnp.float32))
    return o.reshape(o.shape[0], o.shape[1], B_OUT_W)


def _dswa_prompt(u, g_q, g_k):
    q, k, v = _dswa_qkv(u, g_q, g_k)
    outs, lses, bufs = [], [], []
    for g, (win, dil) in enumerate(B_GROUPS):
        sl = slice(g * B_GH, (g + 1) * B_GH)
        o, l = _dilated_band_attn(q[:, :, sl], k[:, :, sl], v[:, :, sl], win, dil)
        outs.append(o)
        lses.append(l)
        keep = min(win, q.shape[1])
        bufs.append(jnp.stack([k[:, -keep:, sl], v[:, -keep:, sl]], axis=2))
    return _merge_groups(outs, lses).astype(u.dtype), bufs


def _dswa_sample(u, g_q, g_k, bufs_in):
    q, k, v = _dswa_qkv(u, g_q, g_k)
    outs, lses, bufs = [], [], []
    for g, (win, dil) in enumerate(B_GROUPS):
        sl = slice(g * B_GH, (g + 1) * B_GH)
        o, l, nb = _dilated_window_step(q[:, :, sl], k[:, :, sl], v[:, :, sl], bufs_in[g], win, dil)
        outs.append(o)
        lses.append(l)
        bufs.append(nb)
    return _merge_groups(outs, lses).astype(u.dtype), bufs


def _swiglu(h, w_in, w_out):
    gu = h @ w_in
    return (jax.nn.silu(gu[..., :D_FF]) * gu[..., D_FF:]) @ w_out


def setup_inputs(seed: int = 0) -> dict:
    key = jax.random.key(seed)
    ks = iter(jax.random.split(key, 64))
    f32 = jnp.float32

    def nrm(shape, scale=1.0):
        return jax.random.normal(next(ks), shape, f32) * scale

    def gain(shape):
        return 1.0 + nrm(shape, 0.02)

    rows = [min(w, PAST_LEN) for w, _ in B_GROUPS]
    return {
        'x_prompt': nrm((BATCH, SEQ, D_MODEL)),
        'x_sample': nrm((DEC_BATCH, DEC_SEQ, D_MODEL)),
        'state_wkv': nrm((N_A_LAYERS, DEC_BATCH, A_HEADS, A_HD, A_HD)),
        'state_shift': nrm((N_A_LAYERS, DEC_BATCH, 1, A_SHIFT_W)),
        'cache_swa_kv1': nrm((N_B_LAYERS, DEC_BATCH, rows[0], 2, B_GH, B_HD)),
        'cache_swa_kv2': nrm((N_B_LAYERS, DEC_BATCH, rows[1], 2, B_GH, B_HD)),
        'cache_swa_kv3': nrm((N_B_LAYERS, DEC_BATCH, rows[2], 2, B_GH, B_HD)),
        'cache_mem_kv': nrm((DEPTH, DEC_BATCH, N_MEM, 2, MEM_HEADS, MEM_HD)),
        'mem_prompt': nrm((BATCH, N_MEM, D_MODEL)),
        'norm_mix': gain((DEPTH, D_MODEL)),
        'norm_ffn': gain((DEPTH, D_MODEL)),
        'norm_mem': gain((DEPTH, D_MODEL)),
        'w_mem_kv': nrm((DEPTH, D_MODEL, 2 * MEM_W), D_MODEL ** -0.5),
        'q_norm_mem': gain((DEPTH, MEM_HD)),
        'k_norm_mem': gain((DEPTH, MEM_HD)),
        'w_in_a': nrm((N_A_LAYERS, D_MODEL, A_IN_W), D_MODEL ** -0.5),
        'w_out_a': nrm((N_A_LAYERS, A_OUT_W, D_MODEL), A_OUT_W ** -0.5),
        'mu_a': jax.random.uniform(next(ks), (N_A_LAYERS, A_SHIFT_W), f32),
        'w0_a': -2.0 + nrm((N_A_LAYERS, MIX_W), 0.5),
        'w2_a': nrm((N_A_LAYERS, DECAY_LORA, MIX_W), 0.1),
        'a0_a': nrm((N_A_LAYERS, MIX_W), 0.5),
        'a2_a': nrm((N_A_LAYERS, AAA_LORA, MIX_W), 0.1),
        'g2_a': nrm((N_A_LAYERS, GATE_LORA, MIX_W), GATE_LORA ** -0.5),
        'kk_a': 0.85 + nrm((N_A_LAYERS, MIX_W), 0.05),
        'ka_a': 1.0 + nrm((N_A_LAYERS, MIX_W), 0.05),
        'rk_a': nrm((N_A_LAYERS, A_HEADS, A_HD), 0.1),
        'lnx_g_a': gain((N_A_LAYERS, MIX_W)),
        'lnx_b_a': nrm((N_A_LAYERS, MIX_W), 0.02),
        'w_in_b': nrm((N_B_LAYERS, D_MODEL, B_IN_W), D_MODEL ** -0.5),
        'w_out_b': nrm((N_B_LAYERS, B_OUT_IN, D_MODEL), B_OUT_IN ** -0.5),
        'q_norm_b': gain((N_B_LAYERS, B_HD)),
        'k_norm_b': gain((N_B_LAYERS, B_HD)),
        'w_ffn_in': nrm((DEPTH, D_MODEL, 2 * D_FF), D_MODEL ** -0.5),
        'w_ffn_out': nrm((DEPTH, D_FF, D_MODEL), D_FF ** -0.5),
    }


def reference(x_prompt, x_sample, state_wkv, state_shift, cache_swa_kv1, cache_swa_kv2, cache_swa_kv3,
              cache_mem_kv, mem_prompt, norm_mix, norm_ffn, norm_mem, w_mem_kv, q_norm_mem, k_norm_mem,
              w_in_a, w_out_a, mu_a, w0_a, w2_a, a0_a, a2_a, g2_a, kk_a, ka_a, rk_a, lnx_g_a, lnx_b_a,
              w_in_b, w_out_b, q_norm_b, k_norm_b, w_ffn_in, w_ffn_out):
    yp, ys = x_prompt, x_sample
    bp = x_prompt.shape[0]
    swa_in = (cache_swa_kv1, cache_swa_kv2, cache_swa_kv3)
    wkv_p, shift_p, wkv_s, shift_s, mem_p = [], [], [], [], []
    swa_p = [[] for _ in B_GROUPS]
    swa_s = [[] for _ in B_GROUPS]
    for i in range(DEPTH):
        j = i // N_MIXERS
        mkv_p = _mem_kv(mem_prompt, norm_mem[i], w_mem_kv[i], k_norm_mem[i])
        mem_p.append(mkv_p)
        mkv_s = cache_mem_kv[i]
        hp = _rmsnorm(yp, norm_mix[i])
        hs_ = _rmsnorm(ys, norm_mix[i])
        if i % N_MIXERS == 0:
            up = hp @ w_in_a[j]
            us = hs_ @ w_in_a[j]
            prm = (mu_a[j], w0_a[j], w2_a[j], a0_a[j], a2_a[j], g2_a[j], kk_a[j], ka_a[j], rk_a[j], lnx_g_a[j], lnx_b_a[j])
            zero_row = jnp.zeros((bp, 1, A_SHIFT_W), up.dtype)
            zero_s = jnp.zeros((bp, A_HEADS, A_HD, A_HD), jnp.float32)
            mp, sp, lp = _rwkv7(up[..., :A_SHIFT_W], zero_row, zero_s, *prm)
            ms, ss, ls = _rwkv7(us[..., :A_SHIFT_W], state_shift[j], state_wkv[j], *prm)
            op = jnp.concatenate([mp, _mem_attend(up[..., A_SHIFT_W:], mkv_p, q_norm_mem[i])], axis=-1)
            os_ = jnp.concatenate([ms, _mem_attend(us[..., A_SHIFT_W:], mkv_s, q_norm_mem[i])], axis=-1)
            yp = yp + op @ w_out_a[j]
            ys = ys + os_ @ w_out_a[j]
            wkv_p.append(sp)
            shift_p.append(lp)
            wkv_s.append(ss)
            shift_s.append(ls)
        else:
            up = hp @ w_in_b[j]
            us = hs_ @ w_in_b[j]
            mp, bufs_p = _dswa_prompt(up, q_norm_b[j], k_norm_b[j])
            ms, bufs_s = _dswa_sample(us, q_norm_b[j], k_norm_b[j], [c[j] for c in swa_in])
            op = jnp.concatenate([mp, _mem_attend(up[..., 3 * MIX_W:], mkv_p, q_norm_mem[i])], axis=-1)
            os_ = jnp.concatenate([ms, _mem_attend(us[..., 3 * MIX_W:], mkv_s, q_norm_mem[i])], axis=-1)
            yp = yp + op @ w_out_b[j]
            ys = ys + os_ @ w_out_b[j]
            for g in range(len(B_GROUPS)):
                swa_p[g].append(bufs_p[g])
                swa_s[g].append(bufs_s[g])
        yp = yp + _swiglu(_rmsnorm(yp, norm_ffn[i]), w_ffn_in[i], w_ffn_out[i])
        ys = ys + _swiglu(_rmsnorm(ys, norm_ffn[i]), w_ffn_in[i], w_ffn_out[i])
    return (yp, ys,
            jnp.stack(wkv_p), jnp.stack(shift_p),
            jnp.stack(swa_p[0]), jnp.stack(swa_p[1]), jnp.stack(swa_p[2]),
            jnp.stack(mem_p),
            jnp.stack(wkv_s), jnp.stack(shift_s),
            jnp.stack(swa_s[0]), jnp.stack(swa_s[1]), jnp.stack(swa_s[2]))
```

```python
import contextlib
import numpy as np
import concourse.bass as bass
import concourse.mybir as mybir
from concourse.bass_utils import run_bass_kernel_spmd

F32 = mybir.dt.float32
BF16 = mybir.dt.bfloat16
ALU = mybir.AluOpType
AF = mybir.ActivationFunctionType
AX = mybir.AxisListType

NCORES = 8
D = 4096
KC = 32
SEQ = 2048
HALF = 1024
NS = 8
NMEM = 256
MEMW = 1024
MIXW = 3072
A_SHIFT = 9792
A_IN = 10816
B_IN = 10240
DFF = 11008
EPS = 1e-6
SHC = A_SHIFT // 8

NTOK = HALF + NS
TBLK = [(0, 512), (512, 512), (1024, NS)]
HH = 24
ZC = 4608 + 192 + 384 + 512
U0C = 2048 + 1 + 2 * 9
PAN_A = [(i * 512, [128] * 4) for i in range(9)] + [(4608, [96, 96]), (4800, [128] * 3), (5184, [128] * 4)]
Q4 = [[0, 1, 2, 3], [4, 5, 6, 7]]
P04 = [[0, 4], [1, 5], [2, 6], [3, 7]]
WSPEC2 = {
    "w_in_a": (4096, ZC, 256, "q4"), "w_out_a": (4096, 4096, 1024, "full"),
    "w_in_b": (4096, 5120, 256, "q4"), "w_out_b": (2048, 4096, 1024, "full"),
    "w_fi0": (DFF, 32 * 256, 256, "full"), "w_fi1": (DFF, 32 * 256, 256, "full"),
    "w_fo0": (4096, DFF, 128, "full"), "w_fo1": (4096, DFF, 128, "full"),
}
DBG_OUT = set()
UPERM = [0, 1, 4, 5, 2, 3, 6, 7]
WSPEC = {
    "w_mem0": (4096, 2048, 1), "w_mem1": (4096, 2048, 1),
    "w_in_a": (4096, A_IN, 2), "w_out_a": (4096, 4096, 1),
    "w_in_b": (4096, B_IN, 2), "w_out_b": (2048, 4096, 1),
    "w_fi0": (4096, 2 * DFF, 4), "w_fi1": (4096, 2 * DFF, 4),
    "w_fo0": (DFF, 4096, 2), "w_fo1": (DFF, 4096, 2),
}


class Buf:
    __slots__ = ("name", "w", "r")

    def __init__(self, name=""):
        self.name = name
        self.w = None
        self.r = []


class Eng:
    def __init__(self, name, sem):
        self.name = name
        self.sem = sem
        self.cnt = 0
        self.waited = {}
        self.ops = []


class KB:
    def __init__(self, nc, es):
        self.nc = nc
        self.es = es
        self.eng = {}
        for n in ("pe", "act", "dve", "pool", "sp"):
            s = es.enter_context(nc.semaphore("s_" + n))
            self.eng[n] = Eng(n, s)
        self.dq = {}
        for q in ("sp", "pool"):
            ring = []
            for i in range(8):
                s = es.enter_context(nc.semaphore("d_%s%d" % (q, i)))
                ring.append([s, 0])
            self.dq[q] = [ring, 0]

    def buf(self, name=""):
        return Buf(name)

    def _deps(self, reads, writes):
        deps = []
        for b in reads:
            if b.w is not None:
                deps.append(b.w)
        for b in writes:
            if b.w is not None:
                deps.append(b.w)
            deps.extend(b.r)
        return deps

    def _emit_waits(self, e, deps):
        need = {}
        for (s, v) in deps:
            if s is e.sem and e.name in ("pe", "sp"):
                continue
            k = id(s)
            if e.waited.get(k, 0) >= v:
                continue
            if need.get(k, (None, 0))[1] < v:
                need[k] = (s, v)
        for k, (s, v) in need.items():
            e.waited[k] = v
            e.ops.append(lambda h, s=s, v=v: h.wait_ge(s, v))

    def _mark(self, ev, reads, writes):
        for b in reads:
            if len(b.r) > 64:
                last = {}
                for (s, v) in b.r:
                    if last.get(id(s), (None, 0))[1] < v:
                        last[id(s)] = (s, v)
                b.r = list(last.values())
            b.r.append(ev)
        for b in writes:
            b.w = ev
            b.r = []

    def op(self, en, fn, reads=(), writes=()):
        e = self.eng[en]
        self._emit_waits(e, self._deps(reads, writes))
        e.cnt += 1
        sem = e.sem
        e.ops.append(lambda h, fn=fn, sem=sem: fn(h).then_inc(sem, 1))
        ev = (sem, e.cnt)
        self._mark(ev, reads, writes)
        return ev

    def dma(self, q, out, in_, reads=(), writes=(), **kw):
        e = self.eng[q]
        ring, idx = self.dq[q]
        slot = ring[idx]
        self.dq[q][1] = (idx + 1) % len(ring)
        deps = self._deps(reads, writes)
        if slot[1] > 0:
            deps.append((slot[0], slot[1]))
        self._emit_waits(e, deps)
        slot[1] += 16
        s, v = slot[0], slot[1]
        e.ops.append(lambda h, s=s, out=out, in_=in_, kw=kw: h.dma_start(out=out, in_=in_, **kw).then_inc(s, 16))
        ev = (s, v)
        self._mark(ev, reads, writes)
        return ev

    def coll(self, fn, reads=(), writes=()):
        return self.op("pool", fn, reads, writes)

    def finish(self, out_bufs):
        e = self.eng["sp"]
        deps = []
        for b in out_bufs:
            if b.w is not None:
                deps.append(b.w)
        for q in self.dq:
            for s, v in self.dq[q][0]:
                if v > 0:
                    deps.append((s, v))
        for n in ("pe", "act", "dve", "pool"):
            en = self.eng[n]
            if en.cnt > 0:
                deps.append((en.sem, en.cnt))
        self._emit_waits(e, deps)

    def barrier(self):
        deps = []
        for q in self.dq:
            for s, v in self.dq[q][0]:
                if v > 0:
                    deps.append((s, v))
        for n in ("pe", "act", "dve", "pool", "sp"):
            en = self.eng[n]
            if en.cnt > 0:
                deps.append((en.sem, en.cnt))
        for n in ("pe", "act", "dve", "pool", "sp"):
            self._emit_waits(self.eng[n], deps)

    def replay(self):
        nc = self.nc
        allops = {n: self.eng[n].ops for n in self.eng}
        for n in self.eng:
            self.eng[n].ops = []
        self._replay(allops)

    def _replay(self, allops):
        nc = self.nc
        with nc.Block() as block:
            @block.tensor
            def _(h):
                for f in allops["pe"]:
                    f(h)

            @block.scalar
            def _(h):
                for f in allops["act"]:
                    f(h)

            @block.vector
            def _(h):
                for f in allops["dve"]:
                    f(h)

            @block.gpsimd
            def _(h):
                for f in allops["pool"]:
                    f(h)

            @block.sync
            def _(h):
                for f in allops["sp"]:
                    f(h)


class T:
    def __init__(self, t, b):
        self.t = t
        self.b = b

    def __getitem__(self, k):
        return self.t[k]


class MK:
    def __init__(self, nc, es, phases):
        self.nc = nc
        self.es = es
        self.kb = KB(nc, es)
        self.phases = phases
        self.ins = {}
        self.outs = {}
        self.scr = {}
        self.rr = 0

    def inp(self, name, shape, dt=F32):
        ap = self.nc.dram_tensor(name, list(shape), dt, kind="ExternalInput").ap()
        t = T(ap, self.kb.buf(name))
        self.ins[name] = t
        return t

    def outp(self, name, shape, dt=F32):
        ap = self.nc.dram_tensor(name, list(shape), dt, kind="ExternalOutput").ap()
        t = T(ap, self.kb.buf(name))
        self.outs[name] = t
        return t

    def dram(self, name, shape, dt):
        kind = "ExternalOutput" if name in DBG_OUT else "Internal"
        ap = self.nc.dram_tensor(name, list(shape), dt, kind=kind).ap()
        t = T(ap, self.kb.buf(name))
        self.scr[name] = t
        return t

    def sb(self, stack, name, shape, dt):
        esz = 4 if dt == F32 else 2
        n = 1
        for d in shape[1:]:
            n *= d
        nbytes = (n * esz + 31) // 32 * 32
        nf = nbytes // 4
        if self.arena_off + nf > self.arena_n:
            raise RuntimeError("SBUF arena overflow: %s needs %d B at %d of %d" % (name, nbytes, self.arena_off * 4, self.arena_n * 4))
        v = self.arena[:, self.arena_off:self.arena_off + nf]
        self.arena_off += nf
        if dt != F32:
            v = v.bitcast(dt)
        v = v[:, 0:n]
        if len(shape) == 3:
            v = v.rearrange("p (a b) -> p a b", b=shape[2])
        elif len(shape) == 4:
            v = v.rearrange("p (a b c) -> p a b c", b=shape[2], c=shape[3])
        if shape[0] < 128:
            v = v[0:shape[0]]
        return T(v, self.kb.buf(name))

    @contextlib.contextmanager
    def phase(self):
        off = self.arena_off
        yield None
        self.kb.barrier()
        self.arena_off = off

    def ps(self, stack, name, shape, dt):
        t = stack.enter_context(self.nc.psum_tensor(name, list(shape), dt))
        return T(t, self.kb.buf(name))

    def op(self, en, fn, reads=(), writes=()):
        return self.kb.op(en, fn, [x.b for x in reads], [x.b for x in writes])

    def dma(self, q, out, in_, reads=(), writes=(), **kw):
        return self.kb.dma(q, out, in_, [x.b for x in reads], [x.b for x in writes], **kw)

    def rstd(self, t, ap, scale, eps):
        self.op("act", lambda h: h.activation(ap, ap, AF.Sqrt, bias=eps, scale=scale), [t], [t])
        self.op("dve", lambda h: h.reciprocal(ap, ap), [t], [t])

    def evac_engine(self):
        self.rr += 1
        return "dve" if self.rr % 2 else "act"

    def copy(self, en, out, in_, reads, writes):
        if en == "act":
            return self.op("act", lambda h: h.activation(out, in_, AF.Copy), reads, writes)
        return self.op(en, lambda h: h.tensor_copy(out, in_), reads, writes)

    def setup_common(self):
        es = self.es
        self.arena_n = 51 * 1024
        self.arena = es.enter_context(self.nc.sbuf_tensor("arena", [128, self.arena_n], F32))
        self.arena_off = 0
        self.c_identf = self.inp("c_identf", [128, 128])
        self.identb = self.sb(es, "identb", [128, 128], BF16)
        self.identf = self.sb(es, "identf", [128, 128], F32)
        self.dma("pool", self.identb[:], self.c_identf.t, [self.c_identf], [self.identb])
        self.dma("sp", self.identf[:], self.c_identf.t, [self.c_identf], [self.identf])
        c_sel = self.inp("c_sel", [128, 2])
        self.sel = self.sb(es, "sel", [128, 2], F32)
        self.dma("sp", self.sel[:], c_sel.t, [c_sel], [self.sel])
        self.ones_f = self.sb(es, "ones_f", [128, 128], F32)
        self.ones_b = self.sb(es, "ones_b", [128, 128], BF16)
        self.op("dve", lambda h: h.memset(self.ones_f.t[:], 1.0), [], [self.ones_f])
        self.op("dve", lambda h: h.memset(self.ones_b.t[:], 1.0), [], [self.ones_b])
        self.pf = [self.ps(es, "pf%d" % i, [128, 512], F32) for i in range(6)]
        self.pbk = [self.ps(es, "pb%d" % i, [128, 1024], BF16) for i in range(2)]
        self.pfi = 0
        self.pbi = 0

    def next_pf(self):
        p = self.pf[self.pfi % getattr(self, "pf_lim", len(self.pf))]
        self.pfi += 1
        return p

    def next_pb(self):
        p = self.pbk[self.pbi % len(self.pbk)]
        self.pbi += 1
        return p

    def phase_weights(self, names):
        self.W = getattr(self, "W", {})
        for name, (K, N) in names:
            src = self.inp(name, [K, N])
            full = self.dram(name + "_bf", [K, N], BF16)
            step = max(1, (2 << 20) // (N * 4))
            r0 = 0
            while r0 < K:
                r1 = min(K, r0 + step)
                self.dma("pool", full.t[r0:r1, :], src.t[r0:r1, :], [src], [full])
                r0 = r1
            self.W[name] = ([full], K, N)

    def wslice(self, name, k0, nk, c0, nc_):
        full, kg, N = self.W[name]
        g = k0 // kg
        assert (k0 + nk - 1) // kg == g
        return full[g], full[g].t[k0 - g * kg:k0 - g * kg + nk, c0:c0 + nc_]

    def load_wpanel(self, q, dst, name, nkc, c0, ncols, kc0=0):
        full, kg, N = self.W[name]
        kpg = kg // 128 if kg % 128 == 0 else None
        if kpg is None:
            for k in range(nkc):
                r0 = (kc0 + k) * 128
                done = 0
                while done < 128:
                    g = (r0 + done) // kg
                    lo = r0 + done - g * kg
                    n = min(128 - done, kg - lo)
                    self.dma(q, dst.t[done:done + n, k, 0:ncols], full[g].t[lo:lo + n, c0:c0 + ncols], [full[g]], [dst])
                    done += n
            return
        k = 0
        while k < nkc:
            g = (kc0 + k) // kpg
            lk = (kc0 + k) - g * kpg
            n = min(nkc - k, kpg - lk)
            src = full[g].t[lk * 128:(lk + n) * 128, c0:c0 + ncols].rearrange("(k p) n -> p k n", p=128)
            self.dma(q, dst.t[:, k:k + n, 0:ncols], src, [full[g]], [dst])
            k += n

    def rms_transpose(self, st, src_ap, src_t, ntok, gain, hT, col0, tag):
        x = self.sb(st, tag + "_x", [128, D], F32)
        xb = self.sb(st, tag + "_xb", [128, D], BF16)
        junk = self.sb(st, tag + "_j", [128, D], F32)
        ss = self.sb(st, tag + "_ss", [128, 1], F32)
        for t0 in range(0, ntok, 128):
            n = min(128, ntok - t0)
            self.dma("sp", x.t[0:n, :], src_ap[t0:t0 + n, :], [src_t], [x])
            self.op("act", lambda h, n=n: h.activation(junk.t[0:n, :], x.t[0:n, :], AF.Square), [x], [junk])
            self.op("dve", lambda h, n=n: h.reduce_sum(ss.t[0:n, :], junk.t[0:n, :], AX.X), [junk], [ss])
            self.rstd(ss, ss.t[0:n, :], 1.0 / D, EPS)
            self.op("act", lambda h, n=n: h.activation(xb.t[0:n, :], x.t[0:n, :], AF.Copy, scale=ss.t[0:n, 0:1]), [x, ss], [xb])
            for k0 in range(0, KC, 8):
                pb = self.next_pb()
                for j in range(8):
                    k = k0 + j
                    self.op("pe", lambda h, pb=pb, j=j, k=k, n=n: h.transpose(pb.t[:, j * 128:j * 128 + n], xb.t[0:n, k * 128:(k + 1) * 128], self.identb.t[0:n, 0:n]),
                            [xb, self.identb], [pb])
                o = hT.t[:, k0:k0 + 8, col0 + t0:col0 + t0 + n]
                i0 = pb.t[:, :].rearrange("p (j t) -> p j t", t=128)[:, :, 0:n]
                g = gain.t[:, k0:k0 + 8].unsqueeze(2).to_broadcast([128, 8, n])
                self.op("dve", lambda h, o=o, i0=i0, g=g: h.tensor_tensor(o, i0, g, ALU.mult), [pb, gain], [hT])

    def phase_memkv(self):
        memp = self.inp("memp", [NMEM, D])
        g_mem = self.inp("g_mem", [2, 128, KC])
        g_kmem = self.inp("g_kmem", [2, 128, 256])
        o_memkv = self.outp("o_memkv", [2, NMEM, 2 * MEMW])
        self.mkz = self.dram("mkz", [2, NMEM, 2, 512], F32)
        for li in range(2):
            with self.phase() as st:
                gain = self.sb(st, "mk_gain", [128, KC], F32)
                gk = self.sb(st, "mk_gk", [128, 256], F32)
                hT = self.sb(st, "mk_hT", [128, KC, NMEM], BF16)
                self.dma("sp", gain.t[:], g_mem.t[li], [g_mem], [gain])
                self.dma("sp", gk.t[:], g_kmem.t[li], [g_kmem], [gk])
                self.rms_transpose(st, memp.t, memp, NMEM, gain, hT, 0, "mk")
                wp = [self.sb(st, "mk_w%d" % i, [128, KC, 512], BF16) for i in range(2)]
                kv = [self.sb(st, "mk_kv%d" % i, [128, 2 * MEMW], F32) for i in range(2)]
                ssq = self.sb(st, "mk_ssq", [128, 4], F32)
                sq = self.sb(st, "mk_sq", [128, 4, 256], F32)
                zk = self.sb(st, "mk_zk", [128, 2, 512], F32)
                for pi in range(4):
                    w = wp[pi % 2]
                    self.load_wpanel("sp", w, "w_mem%d" % li, KC, pi * 512, 512)
                    for tt in range(2):
                        pf = self.next_pf()
                        for k in range(KC):
                            self.op("pe", lambda h, pf=pf, k=k, tt=tt, w=w: h.matmul(pf.t[:, :], hT.t[:, k, tt * 128:(tt + 1) * 128], w.t[:, k, :], start=(k == 0), stop=(k == KC - 1)),
                                    [hT, w], [pf])
                        self.copy("dve", kv[tt].t[:, pi * 512:(pi + 1) * 512], pf.t[:, :], [pf], [kv[tt]])
                for tt in range(2):
                    k3 = kv[tt].t[:, 0:MEMW].rearrange("p (h d) -> p h d", d=256)
                    self.op("act", lambda h, k3=k3: h.activation(sq.t[:], k3, AF.Square), [kv[tt]], [sq])
                    self.op("dve", lambda h: h.reduce_sum(ssq.t[:], sq.t[:], AX.X), [sq], [ssq])
                    self.op("act", lambda h: h.activation(ssq.t[:], ssq.t[:], AF.Sqrt, bias=EPS, scale=1.0 / 256), [ssq], [ssq])
                    self.op("dve", lambda h: h.reciprocal(ssq.t[:], ssq.t[:]), [ssq], [ssq])
                    self.op("pool", lambda h, k3=k3: h.tensor_tensor(k3, k3, ssq.t[:].unsqueeze(2).to_broadcast([128, 4, 256]), ALU.mult), [kv[tt], ssq], [kv[tt]])
                    self.op("dve", lambda h, k3=k3: h.tensor_tensor(k3, k3, gk.t[:].unsqueeze(1).to_broadcast([128, 4, 256]), ALU.mult), [kv[tt], gk], [kv[tt]])
                    self.dma("sp", o_memkv.t[li, tt * 128:(tt + 1) * 128, :], kv[tt].t[:], [kv[tt]], [o_memkv])
                    for part in range(2):
                        lo = kv[tt].t[:, part * 1024:part * 1024 + 512]
                        hi = kv[tt].t[:, part * 1024 + 512:part * 1024 + 1024]
                        self.op("dve", lambda h, part=part, lo=lo: h.tensor_scalar(zk.t[:, part, :], lo, self.sel.t[:, 0:1], None, ALU.mult), [kv[tt], self.sel], [zk])
                        self.op("dve", lambda h, part=part, hi=hi: h.scalar_tensor_tensor(zk.t[:, part, :], hi, self.sel.t[:, 1:2], zk.t[:, part, :], ALU.mult, ALU.add), [kv[tt], self.sel, zk], [zk])
                    self.dma("sp", self.mkz.t[li, tt * 128:(tt + 1) * 128], zk.t[:], [zk], [self.mkz])

    def phase_shift(self):
        NT = 12
        xlast = self.inp("xlast", [NT, D])
        g_mix0 = self.inp("g_mix0", [128, KC])
        o_shift = self.outp("o_shift", [NT, SHC])
        with self.phase() as st:
            gain = self.sb(st, "sh_gain", [128, KC], F32)
            hT = self.sb(st, "sh_hT", [128, KC, NT], BF16)
            res = self.sb(st, "sh_res", [NT, SHC], F32)
            self.dma("sp", gain.t[:], g_mix0.t, [g_mix0], [gain])
            self.rms_transpose(st, xlast.t, xlast, NT, gain, hT, 0, "sh")
            PW = SHC // 3
            wp = [self.sb(st, "sh_w%d" % i, [128, KC, PW], BF16) for i in range(2)]
            for pi in range(3):
                w = wp[pi % 2]
                self.load_wpanel("sp", w, "w_ina_c", KC, pi * PW, PW)
                pf = self.next_pf()
                for k in range(KC):
                    self.op("pe", lambda h, pf=pf, k=k, w=w: h.matmul(pf.t[0:NT, 0:PW], hT.t[:, k, 0:NT], w.t[:, k, 0:PW], start=(k == 0), stop=(k == KC - 1)),
                            [hT, w], [pf])
                self.copy("dve", res.t[:, pi * PW:(pi + 1) * PW], pf.t[0:NT, 0:PW], [pf], [res])
            self.dma("sp", o_shift.t, res.t[:], [res], [o_shift])

    def allgather(self, groups, src, dst):
        self.kb.coll(lambda h, i=src.t, o=dst.t: h.collective_compute("AllGather", ALU.bypass, replica_groups=groups, ins=[i], outs=[o]),
                     [src.b], [dst.b])

    def phase_weights2(self, names):
        self.W = getattr(self, "W", {})
        for name in names:
            K, N, ru, mode = WSPEC2[name]
            nr = 4 if mode == "q4" else 8
            full = self.dram(name + "_f", [K, N], BF16)
            parts = [(name, 0, K // ru, ru)]
            if K % ru:
                parts.append((name + "_t", (K // ru) * ru, 1, K % ru))
            for (pname, row0, nu, ru_) in parts:
                rows = ru_ // nr
                src = self.inp(pname, [nu, rows, N])
                shard = self.dram(pname + "_sh", [nu, rows, N], BF16)
                half = self.dram(pname + "_h", [nu, 4 * rows, N], BF16) if mode != "q4" else None
                step = max(1, (2 << 20) // (rows * N * 4))
                for u0 in range(0, nu, step):
                    u1 = min(nu, u0 + step)
                    self.dma("pool", shard.t[u0:u1], src.t[u0:u1], [src], [shard])
                for u in range(nu):
                    base = row0 + u * ru_
                    if mode == "q4":
                        self.kb.coll(lambda h, i=shard.t[u], o=full.t[base:base + ru_, :]: h.collective_compute(
                            "AllGather", ALU.bypass, replica_groups=Q4, ins=[i], outs=[o]), [shard.b], [full.b])
                    else:
                        self.kb.coll(lambda h, i=shard.t[u], o=half.t[u]: h.collective_compute(
                            "AllGather", ALU.bypass, replica_groups=Q4, ins=[i], outs=[o]), [shard.b], [half.b])
                        for j in range(2):
                            i_ap = half.t[u, j * 2 * rows:(j + 1) * 2 * rows, :]
                            o_ap = full.t[base + j * 4 * rows:base + (j + 1) * 4 * rows, :]
                            self.kb.coll(lambda h, i=i_ap, o=o_ap: h.collective_compute(
                                "AllGather", ALU.bypass, replica_groups=P04, ins=[i], outs=[o]), [half.b], [full.b])
            self.W[name] = ([full], K, N)

    def pair_gather(self, src, dst, rows, ncols, esz):
        rp = max(128, ((2 << 20) // (ncols * esz)) // 128 * 128)
        pieces = []
        r0 = 0
        while r0 < rows:
            r1 = min(rows, r0 + rp)
            self.kb.coll(lambda h, i=src.t[r0:r1, :], o=dst.t[2 * r0:2 * r1, :]: h.collective_compute(
                "AllGather", ALU.bypass, replica_groups=P04, ins=[i], outs=[o]), [src.b], [dst.b])
            pieces.append((r0, r1))
            r0 = r1
        return pieces

    def load_wpanel_pm(self, q, dst, name, j, nkc, ncols):
        full = self.W[name][0][0]
        self.dma(q, dst.t[:, 0:nkc, 0:ncols], full.t[j * 128:(j + 1) * 128, :].rearrange("p (k c) -> p k c", c=ncols), [full], [dst])

    def gemm_fm(self, st, xT, ntok, wname, panels, nkc, sink, tag, wbufs=None, kc0=0, pm=False):
        pw = max(sum(sz) for _, sz in panels)
        if wbufs is None:
            wbufs = [self.sb(st, "%s_w%d" % (tag, i), [128, nkc, pw], BF16) for i in range(2)]
        ci = 0
        for pi, (c0, sizes) in enumerate(panels):
            w = wbufs[pi % 2]
            if pm:
                self.load_wpanel_pm("sp", w, wname, pi, nkc, sum(sizes))
            else:
                self.load_wpanel("sp", w, wname, nkc, c0, sum(sizes), kc0=kc0)
            off = 0
            for m in sizes:
                pf = self.next_pf()
                for k in range(nkc):
                    self.op("pe", lambda h, pf=pf, k=k, w=w, off=off, m=m: h.matmul(pf.t[0:m, 0:ntok], w.t[:, k, off:off + m], xT.t[:, k, 0:ntok], start=(k == 0), stop=(k == nkc - 1)),
                            [xT, w], [pf])
                sink(ci, pf, m)
                off += m
                ci += 1

    def phase_l0_prep(self):
        x_own = self.inp("x_own", [NTOK, D])
        g_mix = self.inp("g_mix", [2, 128, KC])
        self.g_mix = g_mix
        self.XT = [self.dram("XT%d" % i, [128, KC, NTOK], F32) for i in range(5)]
        hT_own = self.hT_own = self.dram("hT_own", [D, NTOK], BF16)
        self.hT_pair = self.dram("hT_pair", [2 * D, NTOK], BF16)
        with self.phase() as st:
            gain = self.sb(st, "p0_gain", [128, KC], F32)
            self.dma("sp", gain.t[:], g_mix.t[0], [g_mix], [gain])
            x = self.sb(st, "p0_x", [128, D], F32)
            xb = self.sb(st, "p0_xb", [128, D], BF16)
            junk = self.sb(st, "p0_j", [128, D], F32)
            ss = self.sb(st, "p0_ss", [128, 1], F32)
            hst = [self.sb(st, "p0_h%d" % i, [128, KC, 128], BF16) for i in range(2)]
            xst = [self.sb(st, "p0_xs%d" % i, [128, KC, 128], F32) for i in range(2)]
            for ti, t0 in enumerate(range(0, NTOK, 128)):
                n = min(128, NTOK - t0)
                hs, xs = hst[ti % 2], xst[ti % 2]
                self.dma("sp", x.t[0:n, :], x_own.t[t0:t0 + n, :], [x_own], [x])
                self.op("act", lambda h, n=n: h.activation(junk.t[0:n, :], x.t[0:n, :], AF.Square), [x], [junk])
                self.op("dve", lambda h, n=n: h.reduce_sum(ss.t[0:n, :], junk.t[0:n, :], AX.X), [junk], [ss])
                self.rstd(ss, ss.t[0:n, :], 1.0 / D, EPS)
                self.op("act", lambda h, n=n: h.activation(xb.t[0:n, :], x.t[0:n, :], AF.Copy, scale=ss.t[0:n, 0:1]), [x, ss], [xb])
                for k0 in range(0, KC, 8):
                    pb = self.next_pb()
                    for j in range(8):
                        k = k0 + j
                        self.op("pe", lambda h, pb=pb, j=j, k=k, n=n: h.transpose(pb.t[:, j * 128:j * 128 + n], xb.t[0:n, k * 128:(k + 1) * 128], self.identb.t[0:n, 0:n]),
                                [xb, self.identb], [pb])
                    o = hs.t[:, k0:k0 + 8, 0:n]
                    i0 = pb.t[:, :].rearrange("p (j t) -> p j t", t=128)[:, :, 0:n]
                    g = gain.t[:, k0:k0 + 8].unsqueeze(2).to_broadcast([128, 8, n])
                    self.op("dve", lambda h, o=o, i0=i0, g=g: h.tensor_tensor(o, i0, g, ALU.mult), [pb, gain], [hs])
                for k0 in range(0, KC, 4):
                    pf = self.next_pf()
                    for j in range(4):
                        k = k0 + j
                        self.op("pe", lambda h, pf=pf, j=j, k=k, n=n: h.transpose(pf.t[:, j * 128:j * 128 + n], x.t[0:n, k * 128:(k + 1) * 128], self.identf.t[0:n, 0:n]),
                                [x, self.identf], [pf])
                    o = xs.t[:, k0:k0 + 4, 0:n]
                    i0 = pf.t[:, :].rearrange("p (j t) -> p j t", t=128)[:, :, 0:n]
                    self.copy("act", o, i0, [pf], [xs])
                self.dma("sp", hT_own.t[:, t0:t0 + n].rearrange("(k p) t -> p k t", p=128), hs.t[:, :, 0:n], [hs], [hT_own])
                self.dma("sp", self.XT[0].t[:, :, t0:t0 + n], xs.t[:, :, 0:n], [xs], [self.XT[0]])
        self.hT_pieces = self.pair_gather(hT_own, self.hT_pair, D, NTOK, 2)

    def phase_l0_inproj(self):
        self.U0 = self.dram("U0", [45, 128, U0C], F32)
        o_shift = self.outp("o_shift0", [128, 45, 3])
        U0 = self.U0
        blocks = [(0, 0, 512, 1), (0, 512, 512, 513), (1, 0, 512, 1025), (1, 512, 512, 1537)]
        with self.phase() as st:
            xT = [self.sb(st, "ip_x%d" % i, [128, KC, 512], BF16) for i in range(2)]
            stg = [self.sb(st, "ip_s%d" % i, [128, 512], F32) for i in range(3)]
            wbufs = [self.sb(st, "ip_w%d" % i, [128, KC, 512], BF16) for i in range(2)]
            sh = self.sb(st, "ip_sh", [128, 45, 3], F32)
            self.op("dve", lambda h: h.memset(sh.t[:], 0.0), [], [sh])
            self.sidx = 0

            def run_block(bi, loads, ntok, outs):
                x = xT[bi % 2]
                for (dst0, r, t0, n) in loads:
                    for (r0, r1) in self.hT_pieces:
                        base = 2 * r0 + r * (r1 - r0)
                        src = self.hT_pair.t[base:base + (r1 - r0), t0:t0 + n].rearrange("(k p) t -> p k t", p=128)
                        self.dma("sp", x.t[:, r0 // 128:r1 // 128, dst0:dst0 + n], src, [self.hT_pair], [x])

                def sink(ci, pf, m):
                    sg = stg[self.sidx % 3]
                    self.sidx += 1
                    self.copy(self.evac_engine(), sg.t[0:m, 0:ntok], pf.t[0:m, 0:ntok], [pf], [sg])
                    for (s0, n, c0) in outs:
                        self.dma("sp", U0.t[ci, 0:m, c0:c0 + n], sg.t[0:m, s0:s0 + n], [sg], [U0])
                        j = {2048: 0, 2057: 1, 2066: 2}.get(c0 + n - 1)
                        if j is not None:
                            self.op("dve", lambda h, j=j, ci=ci, m=m, sg=sg, e=s0 + n - 1: h.tensor_copy(sh.t[0:m, ci, j:j + 1], sg.t[0:m, e:e + 1]), [sg], [sh])
                self.gemm_fm(st, x, ntok, "w_in_a", PAN_A, KC, sink, "ip", wbufs=wbufs)

            for bi, (r, t0, n, c0) in enumerate(blocks):
                run_block(bi, [(0, r, t0, n)], n, [(0, n, c0)])
            run_block(4, [(0, 0, HALF, NS), (NS, 1, HALF, NS)], 2 * NS, [(0, NS, 2050), (NS, NS, 2059)])
            self.dma("sp", o_shift.t, sh.t[:], [sh], [o_shift])

    def phase_rwkv(self):
        U0 = self.U0
        NP = 10
        pa_in = self.inp("pa", [128, NP, 12])
        pl_in = self.inp("pl", [128, 5])
        shcol = self.inp("shcol", [128, 45, 2])
        w2_in = self.inp("w2z", [96, 1536])
        a2_in = self.inp("a2z", [96, 1536])
        g2_in = self.inp("g2z", [128, 3, 1536])
        st_in = self.inp("wkv_in", [2, HH, 64, 64])
        msk_in = self.inp("c_masks", [4, 128, 128])
        o_wkv = self.outp("o_wkv", [3, HH, 64, 64])
        self.opT0 = self.dram("opT0", [2048, 2 * NTOK], BF16)
        SEGS = [(1, 2048, 0, 0), (2050, 8, 1, 2048), (2059, 8, 2, 2056)]
        with self.phase() as st:
            pa = self.sb(st, "rw_pa", [128, NP, 12], F32)
            pl = self.sb(st, "rw_pl", [128, 5], F32)
            sc = self.sb(st, "rw_sc", [128, 45, 2], F32)
            px = self.sb(st, "rw_px", [128, 2, 12], F32)
            w2 = self.sb(st, "rw_w2", [96, 1536], BF16)
            a2 = self.sb(st, "rw_a2", [96, 1536], BF16)
            g2 = self.sb(st, "rw_g2", [128, 3, 1536], BF16)
            mk = self.sb(st, "rw_mk", [128, 4, 128], F32)
            bones = self.sb(st, "rw_bo", [128, 128], F32)
            self.dma("sp", pa.t[:], pa_in.t, [pa_in], [pa])
            self.dma("sp", pl.t[:], pl_in.t, [pl_in], [pl])
            self.dma("sp", sc.t[:], shcol.t, [shcol], [sc])
            self.dma("pool", w2.t[:], w2_in.t, [w2_in], [w2])
            self.dma("pool", a2.t[:], a2_in.t, [a2_in], [a2])
            self.dma("pool", g2.t[:], g2_in.t, [g2_in], [g2])
            self.dma("sp", mk.t[:], msk_in.t.rearrange("m p c -> p m c"), [msk_in], [mk])
            self.op("dve", lambda h: h.tensor_copy(bones.t[:], mk.t[:, 2, :]), [mk], [bones])
            self.op("dve", lambda h: h.tensor_scalar(px.t[:, 0, :], pa.t[:, 3, :], -1.0, None, ALU.mult), [pa], [px])
            self.op("dve", lambda h: h.tensor_scalar(px.t[:, 1, :], pa.t[:, 6, :], -1.0, 1.0, ALU.mult, ALU.add), [pa], [px])
            MS, MI, ML = mk.t[:, 0, :], mk.t[:, 1, :], mk.t[:, 3, :]

            def load_shift(dst, ci, mu_ap, rows, tag):
                raw = self.sb(st, tag + "_raw", [128, U0C], F32) if not hasattr(self, "_raw") else self._raw
                self._raw = raw
                d = self._dtmp
                self.dma("sp", raw.t[0:rows, :], U0.t[ci, 0:rows, :], [U0], [raw])
                self.op("dve", lambda h: h.memset(raw.t[0:rows, 0:1], 0.0), [], [raw])
                self.op("dve", lambda h: h.tensor_copy(raw.t[0:rows, 2049:2050], sc.t[0:rows, ci, 0:1]), [sc], [raw])
                self.op("dve", lambda h: h.tensor_copy(raw.t[0:rows, 2058:2059], sc.t[0:rows, ci, 1:2]), [sc], [raw])
                self.op("pool", lambda h: h.tensor_tensor(d.t[0:rows, 0:U0C - 1], raw.t[0:rows, 0:U0C - 1], raw.t[0:rows, 1:U0C], ALU.subtract), [raw], [d])
                self.op("dve", lambda h: h.scalar_tensor_tensor(dst.t[0:rows, 1:U0C], d.t[0:rows, 0:U0C - 1], mu_ap, raw.t[0:rows, 1:U0C], ALU.mult, ALU.add), [d, raw, pl, pa], [dst])

            self._dtmp = self.sb(st, "rw_d", [128, U0C], F32)
            self._raw = self.sb(st, "rw_raw", [128, U0C], F32)
            lt = self.sb(st, "rw_bon", [128, U0C], F32)
            twl = self.sb(st, "rw_twl", [96, U0C], BF16)
            tal = self.sb(st, "rw_tal", [96, U0C], BF16)
            sgl = self.sb(st, "rw_sgl", [128, 3, U0C], BF16)
            load_shift(lt, 36, pl.t[0:96, 0:1], 96, "rw")
            self.op("act", lambda h: h.activation(twl.t[:, 1:U0C], lt.t[0:96, 1:U0C], AF.Tanh), [lt], [twl])
            load_shift(lt, 37, pl.t[0:96, 1:2], 96, "rw")
            self.op("act", lambda h: h.activation(tal.t[:, 1:U0C], lt.t[0:96, 1:U0C], AF.Copy), [lt], [tal])
            for j in range(3):
                load_shift(lt, 38 + j, pl.t[:, 2 + j:3 + j], 128, "rw")
                self.op("act", lambda h, j=j: h.activation(sgl.t[:, j, 1:U0C], lt.t[:, 1:U0C], AF.Sigmoid), [lt], [sgl])

            F = lambda n: self.sb(st, "rw_" + n, [128, U0C], F32)
            r, k, v, lw, L, L2, a, g, kk, kp, t1 = [F(n) for n in ("r", "k", "v", "lw", "L", "L2", "a", "g", "kk", "kp", "t1")]
            bon = lt
            ar = self.sb(st, "rw_ar", [128, 2, U0C], BF16)
            bT = self.sb(st, "rw_bT", [128, U0C], BF16)
            kT = self.sb(st, "rw_kT", [128, U0C], BF16)
            vb = self.sb(st, "rw_vb", [128, U0C], BF16)
            pc = self._raw
            opo = self.sb(st, "rw_opo", [128, 2 * NTOK], BF16)
            Tst = self.sb(st, "rw_T", [128, 64], F32)
            Tb = self.sb(st, "rw_Tb", [128, 64], BF16)
            zs = self.sb(st, "rw_zs", [64, 128], F32)
            tm = [self.sb(st, "rw_tm%d" % i, [128, 3, 128], BF16) for i in range(2)]
            sm = {n: self.sb(st, "rw_" + n, [128, 128], BF16) for n in ("N", "NT", "N2", "NT2", "X", "X2", "Aak", "Abr", "Akr", "W1", "U")}
            ytm = self.sb(st, "rw_y", [128, 128], F32)
            gs = self.sb(st, "rw_gs", [128, 8], F32)
            tsum = self.sb(st, "rw_ts", [128, 64], F32)
            wout = self.sb(st, "rw_wo", [64, 128], F32)

            def head_pair(hp):
                P = lambda i: pa.t[:, i, hp:hp + 1]
                load_shift(r, hp, P(0), 128, "rw")
                load_shift(k, 12 + hp, P(1), 128, "rw")
                load_shift(v, 24 + hp, P(2), 128, "rw")
                for c0 in range(1, U0C, 512):
                    n = min(512, U0C - c0)
                    cs = slice(c0, c0 + n)
                    pf = self.next_pf()
                    self.op("pe", lambda h, pf=pf, cs=cs, n=n: h.matmul(pf.t[:, 0:n], w2.t[:, hp * 128:(hp + 1) * 128], twl.t[:, cs], start=True, stop=True), [w2, twl], [pf])
                    self.op("act", lambda h, pf=pf, cs=cs, n=n: h.activation(lw.t[:, cs], pf.t[:, 0:n], AF.Exp, bias=px.t[:, 0, hp:hp + 1], scale=-1.0), [pf, px], [lw])
                    self.op("act", lambda h, cs=cs: h.activation(lw.t[:, cs], lw.t[:, cs], AF.Ln, bias=1.0), [lw], [lw])
                    self.op("act", lambda h, cs=cs: h.activation(lw.t[:, cs], lw.t[:, cs], AF.Exp, bias=-0.5, scale=-1.0), [lw], [lw])
                    pf = self.next_pf()
                    self.op("pe", lambda h, pf=pf, cs=cs, n=n: h.matmul(pf.t[:, 0:n], a2.t[:, hp * 128:(hp + 1) * 128], tal.t[:, cs], start=True, stop=True), [a2, tal], [pf])
                    self.op("act", lambda h, pf=pf, cs=cs, n=n: h.activation(a.t[:, cs], pf.t[:, 0:n], AF.Sigmoid, bias=P(4)), [pf, pa], [a])
                    pf = self.next_pf()
                    for j in range(3):
                        self.op("pe", lambda h, pf=pf, cs=cs, n=n, j=j: h.matmul(pf.t[:, 0:n], g2.t[:, j, hp * 128:(hp + 1) * 128], sgl.t[:, j, cs], start=(j == 0), stop=(j == 2)), [g2, sgl], [pf])
                    self.copy("dve", g.t[:, cs], pf.t[:, 0:n], [pf], [g])
                    self.op("pool", lambda h, cs=cs: h.tensor_scalar(kk.t[:, cs], k.t[:, cs], P(5), None, ALU.mult), [k, pa], [kk])
                    self.op("pool", lambda h, cs=cs: h.tensor_tensor(t1.t[:, cs], kk.t[:, cs], kk.t[:, cs], ALU.mult), [kk], [t1])
                    pf = self.next_pf()
                    self.op("pe", lambda h, pf=pf, cs=cs, n=n: h.matmul(pf.t[:, 0:n], bones.t[:], t1.t[:, cs], start=True, stop=True), [bones, t1], [pf])
                    self.op("act", lambda h, pf=pf, cs=cs, n=n: h.activation(t1.t[:, cs], pf.t[:, 0:n], AF.Sqrt), [pf], [t1])
                    self.op("dve", lambda h, cs=cs: h.tensor_scalar(t1.t[:, cs], t1.t[:, cs], 1e-12, None, ALU.max), [t1], [t1])
                    self.op("dve", lambda h, cs=cs: h.reciprocal(t1.t[:, cs], t1.t[:, cs]), [t1], [t1])
                    self.op("pool", lambda h, cs=cs: h.tensor_tensor(kk.t[:, cs], kk.t[:, cs], t1.t[:, cs], ALU.mult), [kk, t1], [kk])
                    self.op("dve", lambda h, cs=cs: h.tensor_scalar(t1.t[:, cs], a.t[:, cs], P(6), px.t[:, 1, hp:hp + 1], ALU.mult, ALU.add), [a, pa, px], [t1])
                    self.op("pool", lambda h, cs=cs: h.tensor_tensor(kp.t[:, cs], k.t[:, cs], t1.t[:, cs], ALU.mult), [k, t1], [kp])
                    self.op("dve", lambda h, cs=cs: h.scalar_tensor_tensor(t1.t[:, cs], r.t[:, cs], P(7), kp.t[:, cs], ALU.mult, ALU.mult), [r, kp, pa], [t1])
                    pf = self.next_pf()
                    self.op("pe", lambda h, pf=pf, cs=cs, n=n: h.matmul(pf.t[:, 0:n], bones.t[:], t1.t[:, cs], start=True, stop=True), [bones, t1], [pf])
                    self.op("dve", lambda h, pf=pf, cs=cs, n=n: h.tensor_tensor(bon.t[:, cs], pf.t[:, 0:n], v.t[:, cs], ALU.mult), [pf, v], [bon])
                self.op("dve", lambda h: h.tensor_scalar(L.t[:, 1:U0C], lw.t[:, 1:U0C], -1.0, None, ALU.mult), [lw], [L])
                src, dst = L, L2
                sh_ = 1
                while sh_ < 128:
                    s3 = src.t[:, 1:2049].rearrange("p (c t) -> p c t", t=128)
                    d3 = dst.t[:, 1:2049].rearrange("p (c t) -> p c t", t=128)
                    self.op("dve", lambda h, s3=s3, d3=d3, sh_=sh_: h.tensor_tensor(d3[:, :, sh_:128], s3[:, :, sh_:128], s3[:, :, 0:128 - sh_], ALU.add), [src], [dst])
                    self.op("pool", lambda h, s3=s3, d3=d3, sh_=sh_: h.tensor_copy(d3[:, :, 0:sh_], s3[:, :, 0:sh_]), [src], [dst])
                    if sh_ < 8:
                        for c0 in (2050, 2059):
                            self.op("dve", lambda h, c0=c0, sh_=sh_, src=src, dst=dst: h.tensor_tensor(dst.t[:, c0 + sh_:c0 + 8], src.t[:, c0 + sh_:c0 + 8], src.t[:, c0:c0 + 8 - sh_], ALU.add), [src], [dst])
                            self.op("pool", lambda h, c0=c0, sh_=sh_, src=src, dst=dst: h.tensor_copy(dst.t[:, c0:c0 + sh_], src.t[:, c0:c0 + sh_]), [src], [dst])
                    else:
                        for c0 in (2050, 2059):
                            self.op("pool", lambda h, c0=c0, src=src, dst=dst: h.tensor_copy(dst.t[:, c0:c0 + 8], src.t[:, c0:c0 + 8]), [src], [dst])
                    src, dst = dst, src
                    sh_ *= 2
                Lc = src
                A_ = slice(1, U0C)
                self.op("act", lambda h: h.activation(pc.t[:, A_], Lc.t[:, A_], AF.Exp), [Lc], [pc])
                self.op("dve", lambda h: h.tensor_tensor(ar.t[:, 1, A_], r.t[:, A_], pc.t[:, A_], ALU.mult), [r, pc], [ar])
                self.op("act", lambda h: h.activation(t1.t[:, A_], Lc.t[:, A_], AF.Exp, scale=-1.0), [Lc], [t1])
                self.op("dve", lambda h: h.tensor_tensor(kT.t[:, A_], kp.t[:, A_], t1.t[:, A_], ALU.mult), [kp, t1], [kT])
                self.op("pool", lambda h: h.tensor_tensor(kp.t[:, A_], kk.t[:, A_], a.t[:, A_], ALU.mult), [kk, a], [kp])
                self.op("dve", lambda h: h.tensor_tensor(bT.t[:, A_], kp.t[:, A_], t1.t[:, A_], ALU.mult), [kp, t1], [bT])
                self.op("pool", lambda h: h.tensor_tensor(dst.t[:, A_], Lc.t[:, A_], lw.t[:, A_], ALU.add), [Lc, lw], [dst])
                self.op("act", lambda h: h.activation(t1.t[:, A_], dst.t[:, A_], AF.Exp), [dst], [t1])
                self.op("dve", lambda h: h.scalar_tensor_tensor(ar.t[:, 0, A_], kk.t[:, A_], -1.0, t1.t[:, A_], ALU.mult, ALU.mult), [kk, t1], [ar])
                self.op("act", lambda h: h.activation(vb.t[:, A_], v.t[:, A_], AF.Copy), [v], [vb])

                for (c00, ntok, si, oc0) in SEGS:
                    if si == 0:
                        self.op("dve", lambda h: h.memset(Tst.t[:], 0.0), [], [Tst])
                    else:
                        self.dma("sp", zs.t[:].rearrange("v (h k) -> v h k", h=2), st_in.t[si - 1, 2 * hp:2 * hp + 2].rearrange("h v k -> v h k"), [st_in], [zs])
                        pf = self.next_pf()
                        self.op("pe", lambda h, pf=pf: h.transpose(pf.t[:, 0:64], zs.t[:, :], self.identf.t[0:64, 0:64]), [zs, self.identf], [pf])
                        self.copy("dve", Tst.t[:], pf.t[:, 0:64], [pf], [Tst])
                    self.op("act", lambda h: h.activation(Tb.t[:], Tst.t[:], AF.Copy), [Tst], [Tb])
                    for ch in range(0, ntok, 128):
                        C = min(128, ntok - ch)
                        cs = slice(c00 + ch, c00 + ch + C)
                        tmt = tm[(ch // 128) % 2]
                        pb = self.next_pb()
                        for j, srcT in enumerate((vb, bT, kT)):
                            self.op("pe", lambda h, pb=pb, j=j, srcT=srcT, cs=cs, C=C: h.transpose(pb.t[0:C, j * 128:(j + 1) * 128], srcT.t[:, cs], self.identb.t[:, :]), [srcT, self.identb], [pb])
                        self.copy("act", tmt.t[0:C, :, :], pb.t[0:C, 0:384].rearrange("p (j c) -> p j c", c=128), [pb], [tmt])
                        for hh in range(2):
                            hs = slice(hh * 64, hh * 64 + 64)
                            Vh, Bh, Kh = tmt.t[0:C, 0, hs], tmt.t[0:C, 1, :], tmt.t[0:C, 2, :]
                            arh = ar.t[hs, :, cs]
                            p1, p2, p3 = self.next_pf(), self.next_pf(), self.next_pf()
                            self.op("pe", lambda h, p1=p1, arh=arh, hs=hs, cs=cs, C=C: h.matmul(p1.t[0:C, 0:2 * C].rearrange("p (a t) -> p a t", a=2), bT.t[hs, cs], arh, start=True, stop=True), [bT, ar], [p1])
                            self.op("pe", lambda h, p2=p2, arh=arh, hs=hs, cs=cs, C=C: h.matmul(p2.t[0:C, 0:2 * C].rearrange("p (a t) -> p a t", a=2), kT.t[hs, cs], arh, start=True, stop=True), [kT, ar], [p2])
                            self.op("pe", lambda h, p3=p3, hs=hs, cs=cs, C=C: h.matmul(p3.t[0:C, 0:C], ar.t[hs, 0, cs], bT.t[hs, cs], start=True, stop=True), [bT, ar], [p3])
                            N_, NT_, X_ = sm["N"], sm["NT"], sm["X"]
                            self.op("dve", lambda h, p1=p1, C=C: h.tensor_tensor(sm["N"].t[0:C, 0:C], p1.t[0:C, 0:C], MS[0:C, 0:C], ALU.mult), [p1, mk], [sm["N"]])
                            self.op("dve", lambda h, p1=p1, C=C: h.tensor_tensor(sm["Abr"].t[0:C, 0:C], p1.t[0:C, C:2 * C], MI[0:C, 0:C], ALU.mult), [p1, mk], [sm["Abr"]])
                            self.op("dve", lambda h, p2=p2, C=C: h.tensor_tensor(sm["Aak"].t[0:C, 0:C], p2.t[0:C, 0:C], MS[0:C, 0:C], ALU.mult), [p2, mk], [sm["Aak"]])
                            self.op("dve", lambda h, p2=p2, C=C: h.tensor_tensor(sm["Akr"].t[0:C, 0:C], p2.t[0:C, C:2 * C], MI[0:C, 0:C], ALU.mult), [p2, mk], [sm["Akr"]])
                            self.op("dve", lambda h, p3=p3, C=C: h.tensor_tensor(sm["NT"].t[0:C, 0:C], p3.t[0:C, 0:C], ML[0:C, 0:C], ALU.mult), [p3, mk], [sm["NT"]])
                            self.op("pool", lambda h, C=C: h.tensor_tensor(sm["X"].t[0:C, 0:C], sm["N"].t[0:C, 0:C], self.identb.t[0:C, 0:C], ALU.add), [sm["N"], self.identb], [sm["X"]])
                            pw, pwT, X, pw2, pwT2, X2 = sm["N"], sm["NT"], sm["X"], sm["N2"], sm["NT2"], sm["X2"]
                            nst = 0
                            while (1 << (nst + 1)) < C:
                                nst += 1
                            for it in range(nst):
                                q1, q2 = self.next_pf(), self.next_pf()
                                self.op("pe", lambda h, q1=q1, pw=pw, pwT=pwT, C=C: h.matmul(q1.t[0:C, 0:C], pwT.t[0:C, 0:C], pw.t[0:C, 0:C], start=True, stop=True), [pw, pwT], [q1])
                                self.op("pe", lambda h, q2=q2, pw=pw, pwT=pwT, C=C: h.matmul(q2.t[0:C, 0:C], pw.t[0:C, 0:C], pwT.t[0:C, 0:C], start=True, stop=True), [pw, pwT], [q2])
                                self.copy("dve", pw2.t[0:C, 0:C], q1.t[0:C, 0:C], [q1], [pw2])
                                self.copy("act", pwT2.t[0:C, 0:C], q2.t[0:C, 0:C], [q2], [pwT2])
                                q3 = self.next_pf()
                                self.op("pe", lambda h, q3=q3, X=X, C=C: h.matmul(q3.t[0:C, 0:C], self.identb.t[0:C, 0:C], X.t[0:C, 0:C], start=True, stop=False), [X, self.identb], [q3])
                                self.op("pe", lambda h, q3=q3, X=X, pwT2=pwT2, C=C: h.matmul(q3.t[0:C, 0:C], pwT2.t[0:C, 0:C], X.t[0:C, 0:C], start=False, stop=True), [X, pwT2], [q3])
                                self.copy("dve", X2.t[0:C, 0:C], q3.t[0:C, 0:C], [q3], [X2])
                                pw, pw2 = pw2, pw
                                pwT, pwT2 = pwT2, pwT
                                X, X2 = X2, X
                            q = self.next_pf()
                            self.op("pe", lambda h, q=q, hs=hs, cs=cs, C=C: h.matmul(q.t[0:C, 0:64], ar.t[hs, 0, cs], Tb.t[hs, :], start=True, stop=False), [ar, Tb], [q])
                            self.op("pe", lambda h, q=q, Vh=Vh, C=C: h.matmul(q.t[0:C, 0:64], sm["Aak"].t[0:C, 0:C], Vh, start=False, stop=True), [sm["Aak"], tmt], [q])
                            self.copy("act", sm["W1"].t[0:C, 0:64], q.t[0:C, 0:64], [q], [sm["W1"]])
                            q = self.next_pf()
                            self.op("pe", lambda h, q=q, X=X, C=C: h.matmul(q.t[0:C, 0:64], X.t[0:C, 0:C], sm["W1"].t[0:C, 0:64], start=True, stop=True), [X, sm["W1"]], [q])
                            self.copy("dve", sm["U"].t[0:C, 0:64], q.t[0:C, 0:64], [q], [sm["U"]])
                            q = self.next_pf()
                            self.op("pe", lambda h, q=q, hs=hs, cs=cs, C=C: h.matmul(q.t[0:C, 0:64], ar.t[hs, 1, cs], Tb.t[hs, :], start=True, stop=False), [ar, Tb], [q])
                            self.op("pe", lambda h, q=q, C=C: h.matmul(q.t[0:C, 0:64], sm["Abr"].t[0:C, 0:C], sm["U"].t[0:C, 0:64], start=False, stop=False), [sm["Abr"], sm["U"]], [q])
                            self.op("pe", lambda h, q=q, Vh=Vh, C=C: h.matmul(q.t[0:C, 0:64], sm["Akr"].t[0:C, 0:C], Vh, start=False, stop=True), [sm["Akr"], tmt], [q])
                            self.copy("act", ytm.t[0:C, hs], q.t[0:C, 0:64], [q], [ytm])
                            q = self.next_pf()
                            self.op("pe", lambda h, q=q, Bh=Bh, C=C: h.matmul(q.t[:, 0:64], Bh, sm["U"].t[0:C, 0:64], start=True, stop=False), [tmt, sm["U"]], [q])
                            self.op("pe", lambda h, q=q, Kh=Kh, Vh=Vh: h.matmul(q.t[:, 0:64], Kh, Vh, start=False, stop=True), [tmt], [q])
                            self.op("dve", lambda h, q=q, hs=hs: h.tensor_tensor(tsum.t[hs, :], q.t[hs, 0:64], Tst.t[hs, :], ALU.add), [q, Tst], [tsum])
                            pcol = c00 + ch + C - 1
                            self.op("act", lambda h, hs=hs, pcol=pcol: h.activation(Tst.t[hs, :], tsum.t[hs, :], AF.Copy, scale=pc.t[hs, pcol:pcol + 1]), [tsum, pc], [Tst])
                            self.op("dve", lambda h, hs=hs: h.tensor_copy(Tb.t[hs, :], Tst.t[hs, :]), [Tst], [Tb])
                        y3 = ytm.t[0:C, :].rearrange("p (h v) -> p h v", h=2)
                        self.op("dve", lambda h, y3=y3, C=C: h.reduce_sum(gs.t[0:C, 0:2], y3, AX.X), [ytm], [gs])
                        self.op("act", lambda h, C=C: h.activation(gs.t[0:C, 0:2], gs.t[0:C, 0:2], AF.Copy, scale=1.0 / 64), [gs], [gs])
                        self.op("dve", lambda h, y3=y3, C=C: h.tensor_tensor(y3, y3, gs.t[0:C, 0:2].unsqueeze(2).to_broadcast([C, 2, 64]), ALU.subtract), [ytm, gs], [ytm])
                        t3 = self._dtmp.t[0:C, 0:128].rearrange("p (h v) -> p h v", h=2)
                        self.op("act", lambda h, y3=y3, t3=t3: h.activation(t3, y3, AF.Square), [ytm], [self._dtmp])
                        self.op("dve", lambda h, t3=t3, C=C: h.reduce_sum(gs.t[0:C, 2:4], t3, AX.X), [self._dtmp], [gs])
                        self.op("act", lambda h, C=C: h.activation(gs.t[0:C, 2:4], gs.t[0:C, 2:4], AF.Sqrt, bias=64e-5, scale=1.0 / 64), [gs], [gs])
                        self.op("dve", lambda h, C=C: h.reciprocal(gs.t[0:C, 2:4], gs.t[0:C, 2:4]), [gs], [gs])
                        self.op("pool", lambda h, y3=y3, C=C: h.tensor_tensor(y3, y3, gs.t[0:C, 2:4].unsqueeze(2).to_broadcast([C, 2, 64]), ALU.mult), [ytm, gs], [ytm])
                        q = self.next_pf()
                        self.op("pe", lambda h, q=q, C=C: h.transpose(q.t[:, 0:C], ytm.t[0:C, :], self.identf.t[0:C, 0:C]), [ytm, self.identf], [q])
                        self.op("dve", lambda h, q=q, cs=cs, C=C: h.tensor_scalar(t1.t[:, cs], q.t[:, 0:C], P(8), P(9), ALU.mult, ALU.add), [q, pa], [t1])
                        self.op("pool", lambda h, cs=cs: h.tensor_tensor(t1.t[:, cs], t1.t[:, cs], bon.t[:, cs], ALU.add), [t1, bon], [t1])
                        oc = oc0 + ch
                        self.op("dve", lambda h, cs=cs, oc=oc, C=C: h.tensor_tensor(opo.t[:, oc:oc + C], t1.t[:, cs], g.t[:, cs], ALU.mult), [t1, g], [opo])
                    q = self.next_pf()
                    self.op("pe", lambda h, q=q: h.transpose(q.t[0:64, 0:128], Tst.t[:, :], self.identf.t[:, :]), [Tst, self.identf], [q])
                    self.copy("dve", wout.t[:, :], q.t[0:64, 0:128], [q], [wout])
                    self.dma("sp", o_wkv.t[si, 2 * hp:2 * hp + 2].rearrange("h v k -> v h k"), wout.t[:].rearrange("v (h k) -> v h k", h=2), [wout], [o_wkv])
                self.dma("sp", self.opT0.t[hp * 128:(hp + 1) * 128, :], opo.t[:], [opo], [self.opT0])

            for hp in range(12):
                head_pair(hp)

    def phase_memattn(self, li, qsrc, qc0, segs, opT, row0):
        cmem = self.ins["cmem"] if "cmem" in self.ins else self.inp("cmem", [2, 2, NMEM, 2, 2, 256])
        gq_in = self.ins["g_qmem"] if "g_qmem" in self.ins else self.inp("g_qmem", [128, 2, 2])
        with self.phase() as st:
            gq = self.sb(st, "ma_gq", [128, 2, 2], F32)
            self.dma("sp", gq.t[:], gq_in.t, [gq_in], [gq])
            kf = self.sb(st, "ma_kf", [128, 2, 256], F32)
            vf = self.sb(st, "ma_vf", [128, 2, 256], F32)
            kb_ = self.sb(st, "ma_kb", [128, 2, 256], BF16)
            vb_ = self.sb(st, "ma_vb", [128, 2, 256], BF16)
            KT = self.sb(st, "ma_KT", [128, 2, 256], BF16)
            qf = self.sb(st, "ma_qf", [128, 2, 512], F32)
            sq = self.sb(st, "ma_sq", [128, 2, 512], F32)
            rs = self.sb(st, "ma_rs", [128, 512], F32)
            qn = self.sb(st, "ma_qn", [128, 2, 512], BF16)
            E = self.sb(st, "ma_E", [128, 2, 512], BF16)
            rd = self.sb(st, "ma_rd", [128, 512], F32)
            ost = [self.sb(st, "ma_o%d" % i, [128, 512], BF16) for i in range(2)]
            self._oi = 0

            def block(hl, c0, n, d0):
                for dc in range(2):
                    self.dma("sp", qf.t[:, dc, 0:n], qsrc.t[qc0 + 2 * hl + dc, :, c0:c0 + n], [qsrc], [qf])
                self.op("act", lambda h: h.activation(sq.t[:, :, 0:n], qf.t[:, :, 0:n], AF.Square), [qf], [sq])
                ps = self.next_pf()
                for dc in range(2):
                    self.op("pe", lambda h, dc=dc: h.matmul(ps.t[:, 0:n], self.ones_f.t[:, :], sq.t[:, dc, 0:n], start=(dc == 0), stop=(dc == 1)), [self.ones_f, sq], [ps])
                self.op("act", lambda h: h.activation(rs.t[:, 0:n], ps.t[:, 0:n], AF.Sqrt, bias=EPS, scale=1.0 / 256), [ps], [rs])
                self.op("dve", lambda h: h.reciprocal(rs.t[:, 0:n], rs.t[:, 0:n]), [rs], [rs])
                for dc in range(2):
                    self.op("dve", lambda h, dc=dc: h.scalar_tensor_tensor(qn.t[:, dc, 0:n], qf.t[:, dc, 0:n], gq.t[:, li, dc:dc + 1], rs.t[:, 0:n], ALU.mult, ALU.mult), [qf, gq, rs], [qn])
                for mb in range(2):
                    pss = self.next_pf()
                    for dc in range(2):
                        self.op("pe", lambda h, mb=mb, dc=dc, pss=pss: h.matmul(pss.t[:, 0:n], KT.t[:, dc, mb * 128:(mb + 1) * 128], qn.t[:, dc, 0:n], start=(dc == 0), stop=(dc == 1)), [KT, qn], [pss])
                    self.op("act", lambda h, mb=mb, pss=pss: h.activation(E.t[:, mb, 0:n], pss.t[:, 0:n], AF.Exp, scale=1.0 / 16), [pss], [E])
                pd = self.next_pf()
                for mb in range(2):
                    self.op("pe", lambda h, mb=mb: h.matmul(pd.t[:, 0:n], self.ones_b.t[:, :], E.t[:, mb, 0:n], start=(mb == 0), stop=(mb == 1)), [self.ones_b, E], [pd])
                self.op("dve", lambda h: h.reciprocal(rd.t[:, 0:n], pd.t[:, 0:n]), [pd], [rd])
                for dc2 in range(2):
                    pn = self.next_pf()
                    for mb in range(2):
                        self.op("pe", lambda h, mb=mb, dc2=dc2, pn=pn: h.matmul(pn.t[:, 0:n], vb_.t[:, mb, dc2 * 128:(dc2 + 1) * 128], E.t[:, mb, 0:n], start=(mb == 0), stop=(mb == 1)), [vb_, E], [pn])
                    o = ost[self._oi % 2]
                    self._oi += 1
                    self.op("dve", lambda h, pn=pn, o=o: h.tensor_tensor(o.t[:, 0:n], pn.t[:, 0:n], rd.t[:, 0:n], ALU.mult), [pn, rd], [o])
                    r0 = row0 + hl * 256 + dc2 * 128
                    self.dma("sp", opT.t[r0:r0 + 128, d0:d0 + n], o.t[:, 0:n], [o], [opT])

            def kvset(kvi, hl):
                if kvi == 0:
                    ksrc, vsrc, srcT = self.mkz.t[li, :, 0, hl * 256:(hl + 1) * 256], self.mkz.t[li, :, 1, hl * 256:(hl + 1) * 256], self.mkz
                else:
                    ksrc, vsrc, srcT = cmem.t[li, kvi - 1, :, 0, hl, :], cmem.t[li, kvi - 1, :, 1, hl, :], cmem
                self.dma("sp", kf.t[:], ksrc.rearrange("(mb p) d -> p mb d", p=128), [srcT], [kf])
                self.dma("sp", vf.t[:], vsrc.rearrange("(mb p) d -> p mb d", p=128), [srcT], [vf])
                self.copy("act", kb_.t[:], kf.t[:], [kf], [kb_])
                self.copy("dve", vb_.t[:], vf.t[:], [vf], [vb_])
                pb = self.next_pb()
                for dc in range(2):
                    for mb in range(2):
                        self.op("pe", lambda h, dc=dc, mb=mb: h.transpose(pb.t[:, (dc * 2 + mb) * 128:(dc * 2 + mb + 1) * 128], kb_.t[:, mb, dc * 128:(dc + 1) * 128], self.identb.t[:, :]), [kb_, self.identb], [pb])
                self.copy("dve", KT.t[:].rearrange("p a b -> p (a b)"), pb.t[:, 0:512], [pb], [KT])
                for (c0, n, kv_i, d0) in segs:
                    if kv_i != kvi:
                        continue
                    for o in range(0, n, 512):
                        block(hl, c0 + o, min(512, n - o), d0 + o)

            for kvi in range(3):
                for hl in range(2):
                    kvset(kvi, hl)

    def load_own(self, x, tmp, src, nk, t0, n, srcT):
        ca, cb = (t0, HALF + t0) if t0 < HALF else (2 * HALF, 2 * HALF + NS)
        self.dma("sp", x.t[:, 0:nk, 0:n], src.t[:, ca:ca + n].rearrange("(k p) t -> p k t", p=128), [srcT], [x])
        self.dma("pool", tmp.t[:, 0:nk, 0:n], src.t[:, cb:cb + n].rearrange("(k p) t -> p k t", p=128), [srcT], [tmp])
        self.op("dve", lambda h: h.tensor_scalar(x.t[:, 0:nk, 0:n], x.t[:, 0:nk, 0:n], self.sel.t[:, 0:1], None, ALU.mult), [x, self.sel], [x])
        self.op("dve", lambda h: h.scalar_tensor_tensor(x.t[:, 0:nk, 0:n], tmp.t[:, 0:nk, 0:n], self.sel.t[:, 1:2], x.t[:, 0:nk, 0:n], ALU.mult, ALU.add), [x, tmp, self.sel], [x])

    def phase_outproj(self, opT, nrows, wname, xin, xout, tag):
        opP = self.dram(tag + "_opP", [2 * nrows, 2 * NTOK], BF16)
        self.pair_gather(opT, opP, nrows, 2 * NTOK, 2)
        nk = 2 * nrows // 128
        with self.phase() as st:
            xa = [self.sb(st, tag + "_x%d" % i, [128, nk, 512], BF16) for i in range(2)]
            xt = self.sb(st, tag + "_xt", [128, nk, 512], BF16)
            wbufs = [self.sb(st, tag + "_w%d" % i, [128, nk, 512], BF16) for i in range(2)]
            res = [self.sb(st, tag + "_r%d" % i, [128, 512], F32) for i in range(3)]
            xr = [self.sb(st, tag + "_xr%d" % i, [128, 512], F32) for i in range(3)]
            self._ri = 0
            for bi, (t0, n) in enumerate(TBLK):
                x = xa[bi % 2]
                self.load_own(x, xt, opP, nk, t0, n, opP)

                def sink(ci, pf, m, t0=t0, n=n):
                    r, xx = res[self._ri % 3], xr[self._ri % 3]
                    self._ri += 1
                    self.dma("pool", xx.t[:, 0:n], xin.t[:, ci, t0:t0 + n], [xin], [xx])
                    self.op("dve", lambda h: h.tensor_tensor(r.t[:, 0:n], pf.t[:, 0:n], xx.t[:, 0:n], ALU.add), [pf, xx], [r])
                    self.dma("sp", xout.t[:, ci, t0:t0 + n], r.t[:, 0:n], [r], [xout])
                self.gemm_fm(st, x, n, wname, [(i * 512, [128] * 4) for i in range(8)], nk, sink, tag, wbufs=wbufs)

    def phase_ffn(self, li, xin, xout):
        g_in = self.ins["g_ffn"] if "g_ffn" in self.ins else self.inp("g_ffn", [2, 128, KC])
        NJ = DFF // 128
        TB = [(0, 256), (256, 256), (512, 256), (768, 256), (1024, NS)]
        with self.phase() as st:
            gain = self.sb(st, "ff_g", [128, KC], F32)
            self.dma("sp", gain.t[:], g_in.t[li], [g_in], [gain])
            x = self.sb(st, "ff_x", [128, KC, 256], F32)
            hT = self.sb(st, "ff_h", [128, KC, 256], BF16)
            aT = self.sb(st, "ff_a", [128, NJ, 256], BF16)
            sq = [self.sb(st, "ff_sq%d" % i, [128, 256], F32) for i in range(2)]
            rs = self.sb(st, "ff_rs", [128, 256], F32)
            sg = [self.sb(st, "ff_sg%d" % i, [128, 256], F32) for i in range(2)]
            res = [self.sb(st, "ff_r%d" % i, [128, 256], F32) for i in range(3)]
            w1 = [self.sb(st, "ff_w1%d" % i, [128, KC, 256], BF16) for i in range(2)]
            w2 = [self.sb(st, "ff_w2%d" % i, [128, NJ, 128], BF16) for i in range(2)]
            self._ri = 0
            for (t0, n) in TB:
                self.dma("sp", x.t[:, :, 0:n], xin.t[:, :, t0:t0 + n], [xin], [x])
                ps = self.next_pf()
                for k in range(KC):
                    q = sq[k % 2]
                    self.op("act", lambda h, k=k, q=q, n=n: h.activation(q.t[:, 0:n], x.t[:, k, 0:n], AF.Square), [x], [q])
                    self.op("pe", lambda h, k=k, q=q, n=n, ps=ps: h.matmul(ps.t[:, 0:n], self.ones_f.t[:, :], q.t[:, 0:n], start=(k == 0), stop=(k == KC - 1)), [self.ones_f, q], [ps])
                self.op("act", lambda h, n=n, ps=ps: h.activation(rs.t[:, 0:n], ps.t[:, 0:n], AF.Sqrt, bias=EPS, scale=1.0 / D), [ps], [rs])
                self.op("dve", lambda h, n=n: h.reciprocal(rs.t[:, 0:n], rs.t[:, 0:n]), [rs], [rs])
                for k in range(KC):
                    self.op("dve", lambda h, k=k, n=n: h.scalar_tensor_tensor(hT.t[:, k, 0:n], x.t[:, k, 0:n], gain.t[:, k:k + 1], rs.t[:, 0:n], ALU.mult, ALU.mult), [x, gain, rs], [hT])

                def sink1(ci, pf, m, n=n):
                    j, up = ci // 2, ci % 2
                    s_ = sg[j % 2]
                    if not up:
                        self.op("act", lambda h: h.activation(s_.t[:, 0:n], pf.t[:, 0:n], AF.Silu), [pf], [s_])
                    else:
                        self.op("dve", lambda h: h.tensor_tensor(aT.t[:, j, 0:n], s_.t[:, 0:n], pf.t[:, 0:n], ALU.mult), [s_, pf], [aT])
                self.gemm_fm(st, hT, n, "w_fi%d" % li, [(j * 256, [128, 128]) for j in range(NJ)], KC, sink1, "ff1", wbufs=w1, pm=True)

                def sink2(ci, pf, m, t0=t0, n=n):
                    r = res[self._ri % 3]
                    self._ri += 1
                    self.op("dve", lambda h: h.tensor_tensor(r.t[:, 0:n], pf.t[:, 0:n], x.t[:, ci, 0:n], ALU.add), [pf, x], [r])
                    self.dma("sp", xout.t[:, ci, t0:t0 + n], r.t[:, 0:n], [r], [xout])
                self.gemm_fm(st, aT, n, "w_fo%d" % li, [(i * 128, [128]) for i in range(KC)], NJ, sink2, "ff2", wbufs=w2, pm=True)

    def phase_l1_prep(self, xin):
        TB = [(0, 256), (256, 256), (512, 256), (768, 256), (1024, NS)]
        with self.phase() as st:
            gain = self.sb(st, "p1_g", [128, KC], F32)
            self.dma("sp", gain.t[:], self.g_mix.t[1], [self.g_mix], [gain])
            x = self.sb(st, "p1_x", [128, KC, 256], F32)
            hTs = [self.sb(st, "p1_h%d" % i, [128, KC, 256], BF16) for i in range(2)]
            sq = [self.sb(st, "p1_sq%d" % i, [128, 256], F32) for i in range(2)]
            rs = self.sb(st, "p1_rs", [128, 256], F32)
            for bi, (t0, n) in enumerate(TB):
                hT = hTs[bi % 2]
                self.dma("sp", x.t[:, :, 0:n], xin.t[:, :, t0:t0 + n], [xin], [x])
                ps = self.next_pf()
                for k in range(KC):
                    q = sq[k % 2]
                    self.op("act", lambda h, k=k, q=q, n=n: h.activation(q.t[:, 0:n], x.t[:, k, 0:n], AF.Square), [x], [q])
                    self.op("pe", lambda h, k=k, q=q, n=n, ps=ps: h.matmul(ps.t[:, 0:n], self.ones_f.t[:, :], q.t[:, 0:n], start=(k == 0), stop=(k == KC - 1)), [self.ones_f, q], [ps])
                self.op("act", lambda h, n=n, ps=ps: h.activation(rs.t[:, 0:n], ps.t[:, 0:n], AF.Sqrt, bias=EPS, scale=1.0 / D), [ps], [rs])
                self.op("dve", lambda h, n=n: h.reciprocal(rs.t[:, 0:n], rs.t[:, 0:n]), [rs], [rs])
                for k in range(KC):
                    self.op("dve", lambda h, k=k, n=n, hT=hT: h.scalar_tensor_tensor(hT.t[:, k, 0:n], x.t[:, k, 0:n], gain.t[:, k:k + 1], rs.t[:, 0:n], ALU.mult, ALU.mult), [x, gain, rs], [hT])
                self.dma("sp", self.hT_own.t[:, t0:t0 + n].rearrange("(k p) t -> p k t", p=128), hT.t[:, :, 0:n], [hT], [self.hT_own])
        self.hT_pieces = self.pair_gather(self.hT_own, self.hT_pair, D, NTOK, 2)

    def phase_l1_inproj(self):
        g_in = self.inp("g_qkb", [128, 2])
        NT2 = 2 * NTOK
        self.QT = self.dram("QT", [12, 128, NT2], BF16)
        self.KTb = self.dram("KTb", [12, 128, NT2], BF16)
        self.KTf = self.dram("KTf", [12, 128, NT2], F32)
        self.Vtm = self.dram("Vtm", [3, NT2, 512], F32)
        self.U1m = self.dram("U1m", [4, 128, NT2], F32)
        blocks = [(0, 0, 512, 0), (0, 512, 512, 512), (1, 0, 512, 1024), (1, 512, 512, 1536)]
        with self.phase() as st:
            gqk = self.sb(st, "i1_g", [128, 2], F32)
            self.dma("sp", gqk.t[:], g_in.t, [g_in], [gqk])
            xT = [self.sb(st, "i1_x%d" % i, [128, KC, 512], BF16) for i in range(2)]
            wbufs = [self.sb(st, "i1_w%d" % i, [128, KC, 512], BF16) for i in range(2)]
            sq = [self.sb(st, "i1_sq%d" % i, [128, 512], F32) for i in range(2)]
            rs = [self.sb(st, "i1_rs%d" % i, [128, 512], F32) for i in range(2)]
            o16 = [self.sb(st, "i1_ob%d" % i, [128, 512], BF16) for i in range(3)]
            o32 = [self.sb(st, "i1_of%d" % i, [128, 512], F32) for i in range(3)]
            self._ci = 0

            def normed(qk, hd, pf, ntok, c0):
                i_ = self._ci
                self._ci += 1
                s_, r_ = sq[i_ % 2], rs[i_ % 2]
                self.op("act", lambda h: h.activation(s_.t[:, 0:ntok], pf.t[:, 0:ntok], AF.Square), [pf], [s_])
                ps = self.next_pf()
                self.op("pe", lambda h: h.matmul(ps.t[:, 0:ntok], self.ones_f.t[:, :], s_.t[:, 0:ntok], start=True, stop=True), [self.ones_f, s_], [ps])
                self.op("act", lambda h: h.activation(r_.t[:, 0:ntok], ps.t[:, 0:ntok], AF.Sqrt, bias=EPS, scale=1.0 / 128), [ps], [r_])
                self.op("dve", lambda h: h.reciprocal(r_.t[:, 0:ntok], r_.t[:, 0:ntok]), [r_], [r_])
                ob = o16[i_ % 3]
                if qk == 0:
                    self.op("dve", lambda h: h.scalar_tensor_tensor(ob.t[:, 0:ntok], pf.t[:, 0:ntok], gqk.t[:, 0:1], r_.t[:, 0:ntok], ALU.mult, ALU.mult), [pf, gqk, r_], [ob])
                    self.dma("sp", self.QT.t[hd, :, c0:c0 + ntok], ob.t[:, 0:ntok], [ob], [self.QT])
                else:
                    of = o32[i_ % 3]
                    self.op("dve", lambda h: h.scalar_tensor_tensor(of.t[:, 0:ntok], pf.t[:, 0:ntok], gqk.t[:, 1:2], r_.t[:, 0:ntok], ALU.mult, ALU.mult), [pf, gqk, r_], [of])
                    self.dma("sp", self.KTf.t[hd, :, c0:c0 + ntok], of.t[:, 0:ntok], [of], [self.KTf])
                    self.copy("act", ob.t[:, 0:ntok], of.t[:, 0:ntok], [of], [ob])
                    self.dma("sp", self.KTb.t[hd, :, c0:c0 + ntok], ob.t[:, 0:ntok], [ob], [self.KTb])

            def run_block(bi, loads, ntok, c0):
                x = xT[bi % 2]
                for (dst0, r, t0, n) in loads:
                    for (r0, r1) in self.hT_pieces:
                        base = 2 * r0 + r * (r1 - r0)
                        src = self.hT_pair.t[base:base + (r1 - r0), t0:t0 + n].rearrange("(k p) t -> p k t", p=128)
                        self.dma("sp", x.t[:, r0 // 128:r1 // 128, dst0:dst0 + n], src, [self.hT_pair], [x])
                for pi in range(10):
                    w = wbufs[pi % 2]
                    self.load_wpanel("sp", w, "w_in_b", KC, pi * 512, 512)
                    if 6 <= pi < 9:
                        g = pi - 6
                        for tb in range(0, ntok, 128):
                            nt = min(128, ntok - tb)
                            pf = self.next_pf()
                            for k in range(KC):
                                self.op("pe", lambda h, pf=pf, k=k, w=w, tb=tb, nt=nt: h.matmul(pf.t[0:nt, 0:512], x.t[:, k, tb:tb + nt], w.t[:, k, 0:512], start=(k == 0), stop=(k == KC - 1)), [x, w], [pf])
                            of = o32[self._ci % 3]
                            self._ci += 1
                            self.copy(self.evac_engine(), of.t[0:nt, :], pf.t[0:nt, 0:512], [pf], [of])
                            self.dma("sp", self.Vtm.t[g, c0 + tb:c0 + tb + nt, :], of.t[0:nt, :], [of], [self.Vtm])
                    else:
                        for m in range(4):
                            pf = self.next_pf()
                            for k in range(KC):
                                self.op("pe", lambda h, pf=pf, k=k, w=w, m=m: h.matmul(pf.t[:, 0:ntok], w.t[:, k, m * 128:(m + 1) * 128], x.t[:, k, 0:ntok], start=(k == 0), stop=(k == KC - 1)), [x, w], [pf])
                            if pi == 9:
                                of = o32[self._ci % 3]
                                self._ci += 1
                                self.copy(self.evac_engine(), of.t[:, 0:ntok], pf.t[:, 0:ntok], [pf], [of])
                                self.dma("sp", self.U1m.t[m, :, c0:c0 + ntok], of.t[:, 0:ntok], [of], [self.U1m])
                            else:
                                normed(0 if pi < 3 else 1, (pi % 3) * 4 + m, pf, ntok, c0)

            for bi, (r, t0, n, c0) in enumerate(blocks):
                run_block(bi, [(0, r, t0, n)], n, c0)
            run_block(4, [(0, 0, HALF, NS), (NS, 1, HALF, NS)], 2 * NS, 2048)

    def phase_dswa_prompt(self):
        cm_in = self.inp("c_amask", [128, 256])
        self.opT1 = self.dram("opT1", [1024, 2 * NTOK], BF16)
        SC = 128 ** -0.5
        with self.phase() as st:
            mk = self.sb(st, "dp_mk", [128, 256], BF16)
            self.dma("pool", mk.t[:], cm_in.t, [cm_in], [mk])
            NUM = self.sb(st, "dp_num", [128, 2048], F32)
            DEN = self.sb(st, "dp_den", [128, 2048], F32)
            qts = [self.sb(st, "dp_q%d" % i, [128, 2048], BF16) for i in range(2)]
            kts = [self.sb(st, "dp_k%d" % i, [128, 2048], BF16) for i in range(2)]
            vf = self.sb(st, "dp_vf", [128, 16, 128], F32)
            vbs = [self.sb(st, "dp_v%d" % i, [128, 16, 128], BF16) for i in range(2)]
            ers = [self.sb(st, "dp_er%d" % i, [128, 256], BF16) for i in range(2)]
            Es = [self.sb(st, "dp_E%d" % i, [128, 256], BF16) for i in range(2)]
            ob = self.sb(st, "dp_ob", [128, 2048], BF16)
            rd = self.sb(st, "dp_rd", [128, 2048], F32)
            self._ci = 0

            def block(g, q_, k_, v_, dil, nb, r, n):
                qv = q_.t[:, :].rearrange("p (n j r) -> p r n j", j=128, r=dil)
                kv = k_.t[:, :].rearrange("p (n j r) -> p r n j", j=128, r=dil)
                NUMv = NUM.t[:, :].rearrange("p (n j r) -> p r n j", j=128, r=dil)[:, r, n, :]
                DENv = DEN.t[:, :].rearrange("p (n j r) -> p r n j", j=128, r=dil)[:, r, n, :]
                er, E = ers[self._ci % 2], Es[self._ci % 2]
                self._ci += 1
                w = 256 if n > 0 else 128
                ps = self.next_pf()
                self.op("pe", lambda h: h.matmul(ps.t[:, 0:128], kv[:, r, n, :], qv[:, r, n, :], start=True, stop=True), [k_, q_], [ps])
                if n > 0:
                    self.op("pe", lambda h: h.matmul(ps.t[:, 128:256], kv[:, r, n - 1, :], qv[:, r, n, :], start=True, stop=True), [k_, q_], [ps])
                self.op("act", lambda h: h.activation(er.t[:, 0:w], ps.t[:, 0:w], AF.Exp, scale=SC), [ps], [er])
                self.op("dve", lambda h: h.tensor_tensor(E.t[:, 0:w], er.t[:, 0:w], mk.t[:, 0:w], ALU.mult), [er, mk], [E])
                pn, pd = self.next_pf(), self.next_pf()
                bi = r * nb + n
                self.op("pe", lambda h: h.matmul(pn.t[:, 0:128], v_.t[:, bi, :], E.t[:, 0:128], start=True, stop=(n == 0)), [v_, E], [pn])
                if n > 0:
                    self.op("pe", lambda h: h.matmul(pn.t[:, 0:128], v_.t[:, bi - 1, :], E.t[:, 128:256], start=False, stop=True), [v_, E], [pn])
                self.op("pe", lambda h: h.matmul(pd.t[:, 0:128], self.ones_b.t[:, :], E.t[:, 0:128], start=True, stop=(n == 0)), [self.ones_b, E], [pd])
                if n > 0:
                    self.op("pe", lambda h: h.matmul(pd.t[:, 0:128], self.ones_b.t[:, :], E.t[:, 128:256], start=False, stop=True), [self.ones_b, E], [pd])
                if g == 0:
                    self.op("act", lambda h: h.activation(NUMv, pn.t[:, 0:128], AF.Copy), [pn], [NUM])
                    self.op("dve", lambda h: h.tensor_copy(DENv, pd.t[:, 0:128]), [pd], [DEN])
                else:
                    self.op("dve", lambda h: h.tensor_tensor(NUMv, NUMv, pn.t[:, 0:128], ALU.add), [pn, NUM], [NUM])
                    self.op("dve", lambda h: h.tensor_tensor(DENv, DENv, pd.t[:, 0:128], ALU.add), [pd, DEN], [DEN])

            def head(i, g, cnt):
                hd = g * 4 + i
                dil = (1, 4, 16)[g]
                nb = 16 // dil
                q_, k_, v_ = qts[cnt % 2], kts[cnt % 2], vbs[cnt % 2]
                self.dma("sp", q_.t[:, :], self.QT.t[hd, :, 0:2048], [self.QT], [q_])
                self.dma("sp", k_.t[:, :], self.KTb.t[hd, :, 0:2048], [self.KTb], [k_])
                for r in range(dil):
                    src = self.Vtm.t[g, 0:2048, i * 128:(i + 1) * 128].rearrange("(n j r) d -> r j n d", j=128, r=dil)[r]
                    self.dma("sp", vf.t[:, r * nb:(r + 1) * nb, :], src, [self.Vtm], [vf])
                self.copy("act", v_.t[:], vf.t[:], [vf], [v_])
                for r in range(dil):
                    for n in range(nb):
                        block(g, q_, k_, v_, dil, nb, r, n)

            cnt = 0
            for i in range(4):
                for g in range(3):
                    head(i, g, cnt)
                    cnt += 1
                self.op("dve", lambda h: h.reciprocal(rd.t[:], DEN.t[:]), [DEN], [rd])
                self.op("dve", lambda h: h.tensor_tensor(ob.t[:], NUM.t[:], rd.t[:], ALU.mult), [NUM, rd], [ob])
                self.dma("sp", self.opT1.t[i * 128:(i + 1) * 128, 0:2048], ob.t[:], [ob], [self.opT1])

    def phase_kv_out_prompt(self):
        KEEP = (128, 512, 2048)
        outs = [self.outp("o_kvp%d" % g, [KEEP[g], 2, 4, 128]) for g in range(3)]
        with self.phase() as st:
            kf = self.sb(st, "ko_kf", [128, 2048], F32)
            ots = [self.sb(st, "ko_o%d" % i, [128, 4, 128], F32) for i in range(2)]
            cnt = 0
            for g in range(3):
                keep = KEEP[g]
                for r0 in range(0, keep, 512):
                    r1 = min(keep, r0 + 512)
                    self.dma("sp", outs[g].t[r0:r1, 1, :, :], self.Vtm.t[g, 2048 - keep + r0:2048 - keep + r1, :].rearrange("t (h d) -> t h d", d=128), [self.Vtm], [outs[g]])
                for i in range(4):
                    hd = g * 4 + i
                    self.dma("sp", kf.t[:, 0:keep], self.KTf.t[hd, :, 2048 - keep:2048], [self.KTf], [kf])
                    for tb in range(0, keep, 512):
                        nb_ = min(4, (keep - tb) // 128)
                        pf = self.next_pf()
                        for j in range(nb_):
                            self.op("pe", lambda h, pf=pf, j=j, tb=tb: h.transpose(pf.t[:, j * 128:(j + 1) * 128], kf.t[:, tb + j * 128:tb + (j + 1) * 128], self.identf.t[:, :]), [kf, self.identf], [pf])
                        o = ots[cnt % 2]
                        cnt += 1
                        self.copy(self.evac_engine(), o.t[:, 0:nb_, :], pf.t[:, 0:nb_ * 128].rearrange("p (j d) -> p j d", d=128), [pf], [o])
                        self.dma("sp", outs[g].t[tb:tb + nb_ * 128, 0, i, :].rearrange("(j p) d -> p j d", p=128), o.t[:, 0:nb_, :], [o], [outs[g]])

    def phase_dswa_sample(self):
        ROWS = (128, 512, 2048)
        MOFF = (0, 1, 5)
        csw = [self.inp("cswa%d" % g, [2, ROWS[g], 2, 4, 128]) for g in range(3)]
        sm_in = self.inp("c_smask", [128, 24, 8])
        outs = [self.outp("o_kvs%d" % g, [2, ROWS[g], 2, 4, 128]) for g in range(3)]
        SC = 128 ** -0.5
        self.pf_lim = 4
        accn, accd = self.pf[4], self.pf[5]
        with self.phase() as st:
            sm = self.sb(st, "ds_sm", [128, 24, 8], BF16)
            self.dma("pool", sm.t[:], sm_in.t, [sm_in], [sm])
            kf = self.sb(st, "ds_kf", [128, 16, 128], F32)
            vf = self.sb(st, "ds_vf", [128, 16, 128], F32)
            kb_ = self.sb(st, "ds_kb", [128, 16, 128], BF16)
            vb_ = self.sb(st, "ds_vb", [128, 16, 128], BF16)
            kTs = [self.sb(st, "ds_kT%d" % i, [128, 128], BF16) for i in range(2)]
            q8 = self.sb(st, "ds_q8", [128, 8], BF16)
            k8 = self.sb(st, "ds_k8", [128, 8], BF16)
            k8f = self.sb(st, "ds_k8f", [128, 8], F32)
            v8f = self.sb(st, "ds_v8f", [8, 128], F32)
            v8 = self.sb(st, "ds_v8", [8, 128], BF16)
            ers = [self.sb(st, "ds_er%d" % i, [128, 8], BF16) for i in range(2)]
            Es = [self.sb(st, "ds_E%d" % i, [128, 8], BF16) for i in range(2)]
            rd = self.sb(st, "ds_rd", [128, 8], F32)
            ob = self.sb(st, "ds_ob", [128, 8], BF16)
            k8o = self.sb(st, "ds_k8o", [8, 128], F32)
            self._ci = 0

            def group(s, i, g):
                hd = g * 4 + i
                nblk = ROWS[g] // 128
                c8 = 2048 + 8 * s
                self.dma("sp", kf.t[:, 0:nblk, :], csw[g].t[s, :, 0, i, :].rearrange("(n p) d -> p n d", p=128), [csw[g]], [kf])
                self.dma("sp", vf.t[:, 0:nblk, :], csw[g].t[s, :, 1, i, :].rearrange("(n p) d -> p n d", p=128), [csw[g]], [vf])
                self.copy("act", kb_.t[:, 0:nblk, :], kf.t[:, 0:nblk, :], [kf], [kb_])
                self.copy("dve", vb_.t[:, 0:nblk, :], vf.t[:, 0:nblk, :], [vf], [vb_])
                self.dma("sp", q8.t[:], self.QT.t[hd, :, c8:c8 + 8], [self.QT], [q8])
                self.dma("sp", k8.t[:], self.KTb.t[hd, :, c8:c8 + 8], [self.KTb], [k8])
                self.dma("sp", k8f.t[:], self.KTf.t[hd, :, c8:c8 + 8], [self.KTf], [k8f])
                self.dma("sp", v8f.t[:], self.Vtm.t[g, c8:c8 + 8, i * 128:(i + 1) * 128], [self.Vtm], [v8f])
                self.copy("dve", v8.t[:], v8f.t[:], [v8f], [v8])
                for blk in range(nblk):
                    kT = kTs[self._ci % 2]
                    er, E = ers[self._ci % 2], Es[self._ci % 2]
                    self._ci += 1
                    first = (g == 0 and blk == 0)
                    pb = self.next_pb()
                    self.op("pe", lambda h, pb=pb, blk=blk: h.transpose(pb.t[:, 0:128], kb_.t[:, blk, :], self.identb.t[:, :]), [kb_, self.identb], [pb])
                    self.copy("dve", kT.t[:], pb.t[:, 0:128], [pb], [kT])
                    ps = self.next_pf()
                    self.op("pe", lambda h, ps=ps, kT=kT: h.matmul(ps.t[:, 0:8], kT.t[:, :], q8.t[:, :], start=True, stop=True), [kT, q8], [ps])
                    self.op("act", lambda h, ps=ps, er=er: h.activation(er.t[:], ps.t[:, 0:8], AF.Exp, scale=SC), [ps], [er])
                    self.op("dve", lambda h, er=er, E=E, blk=blk: h.tensor_tensor(E.t[:], er.t[:], sm.t[:, MOFF[g] + blk, :], ALU.mult), [er, sm], [E])
                    self.op("pe", lambda h, E=E, blk=blk, first=first: h.matmul(accn.t[:, 0:8], vb_.t[:, blk, :], E.t[:], start=first, stop=False), [vb_, E], [accn])
                    self.op("pe", lambda h, E=E, first=first: h.matmul(accd.t[:, 0:8], self.ones_b.t[:, :], E.t[:], start=first, stop=False), [self.ones_b, E], [accd])
                er, E = ers[self._ci % 2], Es[self._ci % 2]
                self._ci += 1
                last = (g == 2)
                ps = self.next_pf()
                self.op("pe", lambda h: h.matmul(ps.t[0:8, 0:8], k8.t[:, :], q8.t[:, :], start=True, stop=True), [k8, q8], [ps])
                self.op("act", lambda h: h.activation(er.t[0:8, :], ps.t[0:8, 0:8], AF.Exp, scale=SC), [ps], [er])
                self.op("dve", lambda h: h.tensor_tensor(E.t[0:8, :], er.t[0:8, :], sm.t[0:8, 21 + g, :], ALU.mult), [er, sm], [E])
                self.op("pe", lambda h: h.matmul(accn.t[:, 0:8], v8.t[0:8, :], E.t[0:8, :], start=False, stop=last), [v8, E], [accn])
                self.op("pe", lambda h: h.matmul(accd.t[:, 0:8], self.ones_b.t[0:8, :], E.t[0:8, :], start=False, stop=last), [self.ones_b, E], [accd])
                rows = ROWS[g]
                pt = self.next_pf()
                self.op("pe", lambda h: h.transpose(pt.t[0:8, 0:128], k8f.t[:, 0:8], self.identf.t[:, :]), [k8f, self.identf], [pt])
                self.copy("dve", k8o.t[:], pt.t[0:8, 0:128], [pt], [k8o])
                self.dma("sp", outs[g].t[s, rows - 8:rows, 0, i, :], k8o.t[:], [k8o], [outs[g]])

            for s in range(2):
                for g in range(3):
                    rows = ROWS[g]
                    for r0 in range(8, rows, 512):
                        if "ds_nocopy" in self.phases:
                            break
                        r1 = min(rows, r0 + 512)
                        self.dma("sp", outs[g].t[s, r0 - 8:r1 - 8], csw[g].t[s, r0:r1], [csw[g]], [outs[g]])
                    c8 = 2048 + 8 * s
                    self.dma("sp", outs[g].t[s, rows - 8:rows, 1, :, :], self.Vtm.t[g, c8:c8 + 8, :].rearrange("t (h d) -> t h d", d=128), [self.Vtm], [outs[g]])
                for i in range(4):
                    if "ds_noattn" in self.phases:
                        break
                    for g in range(3):
                        group(s, i, g)
                    self.op("dve", lambda h: h.reciprocal(rd.t[:], accd.t[:, 0:8]), [accd], [rd])
                    self.op("dve", lambda h: h.tensor_tensor(ob.t[:], accn.t[:, 0:8], rd.t[:], ALU.mult), [accn, rd], [ob])
                    c8 = 2048 + 8 * s
                    self.dma("sp", self.opT1.t[i * 128:(i + 1) * 128, c8:c8 + 8], ob.t[:], [ob], [self.opT1])
        self.pf_lim = 6

    def phase_final(self, xin):
        o_y = self.outp("o_y", [NTOK, D])
        with self.phase() as st:
            xs = [self.sb(st, "fy_x%d" % i, [128, KC, 128], F32) for i in range(2)]
            yts = [self.sb(st, "fy_y%d" % i, [128, D], F32) for i in range(2)]
            for bi, t0 in enumerate(range(0, NTOK, 128)):
                n = min(128, NTOK - t0)
                x, yt = xs[bi % 2], yts[bi % 2]
                self.dma("sp", x.t[:, :, 0:n], xin.t[:, :, t0:t0 + n], [xin], [x])
                for k0 in range(0, KC, 4):
                    pf = self.next_pf()
                    for j in range(4):
                        self.op("pe", lambda h, pf=pf, j=j, k0=k0, n=n, x=x: h.transpose(pf.t[0:n, j * 128:(j + 1) * 128], x.t[:, k0 + j, 0:n], self.identf.t[:, :]), [x, self.identf], [pf])
                    self.copy(self.evac_engine(), yt.t[0:n, k0 * 128:(k0 + 4) * 128], pf.t[0:n, 0:512], [pf], [yt])
                self.dma("sp", o_y.t[t0:t0 + n, :], yt.t[0:n, :], [yt], [o_y])

    def build(self):
        ph = self.phases
        self.setup_common()
        nol0 = "nol0" in ph
        wn = []
        if not nol0:
            self.phase_weights([("w_mem0", (4096, 2048)), ("w_mem1", (4096, 2048))])
            wn = ["w_in_a"]
        if "l0out" in ph:
            wn += ["w_out_a"]
        if "ffn0" in ph:
            wn += ["w_fi0", "w_fo0"]
        if "l1" in ph:
            wn += ["w_in_b"]
        if "l1out" in ph:
            wn += ["w_out_b", "w_fi1", "w_fo1"]
        first, rest = ([w for w in wn if w == "w_in_a"], [w for w in wn if w != "w_in_a"]) if not nol0 else (wn, [])
        self.phase_weights2(first)
        self.kb.barrier()
        if not nol0:
            self.phase_memkv()
            self.phase_l0_prep()
            self.phase_l0_inproj()
            self.phase_weights2(rest)
            self.phase_rwkv()
        else:
            self.g_mix = self.inp("g_mix", [2, 128, KC])
            self.XT = [self.dram("XT%d" % i, [128, KC, NTOK], F32) for i in range(5)]
            self.hT_own = self.dram("hT_own", [D, NTOK], BF16)
            self.hT_pair = self.dram("hT_pair", [2 * D, NTOK], BF16)
        if "l0out" in ph:
            SEG = [(1, 2048, 0, 0), (2050, 8, 1, 2048), (2059, 8, 2, 2056)]
            self.phase_memattn(0, self.U0, 41, SEG, self.opT0, 1536)
            self.phase_outproj(self.opT0, 2048, "w_out_a", self.XT[0], self.XT[1], "o0")
        if "ffn0" in ph:
            self.phase_ffn(0, self.XT[1], self.XT[2])
        if "l1" in ph:
            self.phase_l1_prep(self.XT[2])
            self.phase_l1_inproj()
            if "skipkvo" not in ph:
                self.phase_kv_out_prompt()
            if "skipdp" not in ph:
                self.phase_dswa_prompt()
            else:
                self.opT1 = self.dram("opT1", [1024, 2 * NTOK], BF16)
            if "skipds" not in ph:
                self.phase_dswa_sample()
        if "l1out" in ph:
            SEG = [(0, 2048, 0, 0), (2048, 8, 1, 2048), (2056, 8, 2, 2056)]
            self.phase_memattn(1, self.U1m, 0, SEG, self.opT1, 512)
            self.phase_outproj(self.opT1, 1024, "w_out_b", self.XT[2], self.XT[3], "o1")
            self.phase_ffn(1, self.XT[3], self.XT[4])
            self.phase_final(self.XT[4])
        self.kb.finish([t.b for t in self.outs.values()] + [t.b for n, t in self.scr.items() if n in DBG_OUT])
        self.kb.replay()


def _wshard(w, G, c):
    K, N = w.shape
    rows = K // (8 * G)
    return np.ascontiguousarray(w.reshape(G, 8, rows, N)[:, c])


def _cz(c):
    return c % 4, c // 4


def _wsh2(w, name, c):
    K, N, ru, mode = WSPEC2[name]
    nr = 4 if mode == "q4" else 8
    sel = (c % 4) if mode == "q4" else UPERM[c]
    out = {}
    nu = K // ru
    out[name] = np.ascontiguousarray(w[:nu * ru].reshape(nu, nr, ru // nr, N)[:, sel])
    if K % ru:
        rt = K % ru
        out[name + "_t"] = np.ascontiguousarray(w[nu * ru:].reshape(1, nr, rt // nr, N)[:, sel])
    return out


def _pm_in(w):
    return np.ascontiguousarray(w.reshape(KC, 128, DFF // 128, 256).transpose(2, 1, 0, 3).reshape(DFF, KC * 256))


def _pm_out(w):
    return np.ascontiguousarray(w.reshape(DFF // 128, 128, KC, 128).transpose(2, 1, 0, 3).reshape(4096, DFF))


def _gather_order(nrows, ncols, esz):
    rp = max(128, ((2 << 20) // (ncols * esz)) // 128 * 128)
    rk, lr = [], []
    r0 = 0
    while r0 < nrows:
        r1 = min(nrows, r0 + rp)
        for r in range(2):
            rk.append(np.full(r1 - r0, r))
            lr.append(np.arange(r0, r1))
        r0 = r1
    return np.concatenate(rk), np.concatenate(lr)


def _cols_a(z):
    r = np.arange(z * 1536, (z + 1) * 1536)
    return np.concatenate([r, 3072 + r, 6144 + r, np.arange(9216, 9792), 9792 + np.arange(z * 512, (z + 1) * 512)])


def _cols_b(z):
    heads = np.array([g * 8 + 4 * z + i for g in range(3) for i in range(4)])
    hc = (heads[:, None] * 128 + np.arange(128)[None, :]).reshape(-1)
    return np.concatenate([hc, 3072 + hc, 6144 + hc, 9216 + np.arange(z * 512, (z + 1) * 512)])


def _smask():
    m = np.zeros((128, 24, 8), np.float32)
    t = np.arange(8)[None, :]
    off = 0
    for g, (win, dil) in enumerate(((128, 1), (512, 4), (2048, 16))):
        rows = win
        for blk in range(rows // 128):
            p = (blk * 128 + np.arange(128))[:, None]
            m[:, off + blk, :] = (((rows + t - p) % dil) == 0) & (p >= t + rows - win)
        off += rows // 128
        tp = np.arange(8)[:, None]
        m[0:8, 21 + g, :] = (((t - tp) % dil) == 0) & (tp <= t)
    return m


def _chunk_rows(v, z):
    cols = _cols_a(z)
    vv = v[cols] if v.shape[0] >= A_IN else np.concatenate([v, np.zeros(A_IN - v.shape[0], v.dtype)])[cols]
    out = np.zeros((45, 128), np.float32)
    out[0:36] = vv[0:4608].reshape(36, 128)
    out[36, :96] = vv[4608:4704]
    out[37, :96] = vv[4704:4800]
    out[38:41] = vv[4800:5184].reshape(3, 128)
    out[41:45] = vv[5184:5696].reshape(4, 128)
    return out


_WITH_RWKV = [False]


def _host_l0in(inp, c):
    b, z = _cz(c)
    m = {}
    m.update(_wsh2(np.ascontiguousarray(inp["w_in_a"][0][:, _cols_a(z)]), "w_in_a", c))
    m["x_own"] = np.ascontiguousarray(np.concatenate([inp["x_prompt"][b, z * HALF:(z + 1) * HALF], inp["x_sample"][2 * b + z]], axis=0))
    m["g_mix"] = np.ascontiguousarray(inp["norm_mix"].reshape(2, KC, 128).transpose(0, 2, 1))
    sh = np.stack([_chunk_rows(inp["state_shift"][0, 2 * b + j, 0], z) for j in range(2)], axis=-1)
    m["shcol"] = np.ascontiguousarray(sh.transpose(1, 0, 2))
    ch = slice(z * 1536, (z + 1) * 1536)
    f = lambda v: v[ch].reshape(12, 128).T
    mu = inp["mu_a"][0]
    pa = np.stack([f(mu[0:3072]), f(mu[3072:6144]), f(mu[6144:9216]), f(inp["w0_a"][0]), f(inp["a0_a"][0]), f(inp["kk_a"][0]),
                   f(inp["ka_a"][0]), f(inp["rk_a"][0].reshape(-1)), f(inp["lnx_g_a"][0]), f(inp["lnx_b_a"][0])], axis=1)
    m["pa"] = np.ascontiguousarray(pa.astype(np.float32))
    pl = np.zeros((128, 5), np.float32)
    pl[:96, 0] = mu[9216:9312]
    pl[:96, 1] = mu[9312:9408]
    pl[:, 2:5] = mu[9408:9792].reshape(3, 128).T
    m["pl"] = pl
    m["w2z"] = np.ascontiguousarray(inp["w2_a"][0][:, ch])
    m["a2z"] = np.ascontiguousarray(inp["a2_a"][0][:, ch])
    m["g2z"] = np.ascontiguousarray(inp["g2_a"][0][:, ch].reshape(3, 128, 1536).transpose(1, 0, 2))
    m["wkv_in"] = np.ascontiguousarray(inp["state_wkv"][0, 2 * b:2 * b + 2, z * HH:(z + 1) * HH])
    ii = np.arange(128)
    strict = (ii[:, None] < ii[None, :]).astype(np.float32)
    incl = (ii[:, None] <= ii[None, :]).astype(np.float32)
    bones = (ii[:, None] // 64 == ii[None, :] // 64).astype(np.float32)
    lower = (ii[:, None] > ii[None, :]).astype(np.float32)
    m["c_masks"] = np.stack([strict, incl, bones, lower])
    return m


def make_in_maps(inp, phases):
    maps = []
    shared = {}
    if "l0out" in phases:
        rk, lr = _gather_order(2048, 2 * NTOK, 2)
        perm = np.where(lr < 1536, rk * 1536 + lr, 3072 + rk * 512 + (lr - 1536))
        shared["w_out_a"] = np.ascontiguousarray(inp["w_out_a"][0][perm])
    if "ffn0" in phases:
        jj = np.arange(DFF).reshape(DFF // 128, 128)
        cols = np.concatenate([jj, DFF + jj], axis=1).reshape(-1)
        shared["w_fi0"] = _pm_in(inp["w_ffn_in"][0][:, cols])
        shared["w_fo0"] = _pm_out(inp["w_ffn_out"][0])
    if "l1out" in phases:
        rk, lr = _gather_order(1024, 2 * NTOK, 2)
        perm = np.where(lr < 512, rk * 512 + lr, 1024 + rk * 512 + (lr - 512))
        shared["w_out_b"] = np.ascontiguousarray(inp["w_out_b"][0][perm])
        jj = np.arange(DFF).reshape(DFF // 128, 128)
        cols = np.concatenate([jj, DFF + jj], axis=1).reshape(-1)
        shared["w_fi1"] = _pm_in(inp["w_ffn_in"][1][:, cols])
        shared["w_fo1"] = _pm_out(inp["w_ffn_out"][1])
    ii = np.arange(128)
    amask = np.concatenate([(ii[:, None] <= ii[None, :]), (ii[:, None] >= ii[None, :])], axis=1).astype(np.float32)
    smask = _smask()
    for c in range(NCORES):
        b, z = _cz(c)
        m = {"c_identf": np.eye(128, dtype=np.float32)}
        if "l1" in phases:
            m.update(_wsh2(np.ascontiguousarray(inp["w_in_b"][0][:, _cols_b(z)]), "w_in_b", c))
            m["g_qkb"] = np.ascontiguousarray(np.stack([inp["q_norm_b"][0], inp["k_norm_b"][0]], axis=1))
            m["c_amask"] = amask
            m["c_smask"] = smask
            for g, nm in enumerate(("cache_swa_kv1", "cache_swa_kv2", "cache_swa_kv3")):
                m["cswa%d" % g] = np.ascontiguousarray(inp[nm][0, 2 * b:2 * b + 2, :, :, 4 * z:4 * z + 4, :])
        m["c_sel"] = np.ascontiguousarray(np.broadcast_to(np.array([1.0 - z, float(z)], np.float32), (128, 2)))
        m["w_mem0"] = np.ascontiguousarray(inp["w_mem_kv"][0])
        m["w_mem1"] = np.ascontiguousarray(inp["w_mem_kv"][1])
        m["memp"] = np.ascontiguousarray(inp["mem_prompt"][b])
        m["g_mem"] = np.ascontiguousarray(inp["norm_mem"].reshape(2, KC, 128).transpose(0, 2, 1))
        m["g_kmem"] = np.ascontiguousarray(np.broadcast_to(inp["k_norm_mem"][:, None, :], (2, 128, 256)))
        m.update(_host_l0in(inp, c))
        if "l0out" in phases:
            m["cmem"] = np.ascontiguousarray(inp["cache_mem_kv"][:, 2 * b:2 * b + 2, :, :, 2 * z:2 * z + 2, :])
            m["g_qmem"] = np.ascontiguousarray(inp["q_norm_mem"].reshape(2, 2, 128).transpose(2, 0, 1))
        if "ffn0" in phases:
            m["g_ffn"] = np.ascontiguousarray(inp["norm_ffn"].reshape(2, KC, 128).transpose(0, 2, 1))
        for k, w in shared.items():
            m.update(_wsh2(w, k, c))
        maps.append(m)
    return maps


def run(inp, phases):
    nc = bass.Bass("TRN2", target_bir_lowering=False)
    with contextlib.ExitStack() as es:
        mk = MK(nc, es, phases)
        mk.build()
    maps = make_in_maps(inp, phases)
    maps = [{k: v for k, v in m.items() if k in mk.ins} for m in maps]
    missing = [k for k in mk.ins if k not in maps[0]]
    assert not missing, missing
    res = run_bass_kernel_spmd(nc, maps, core_ids=list(range(NCORES)))
    return res.results


ALL_PHASES = ("l0out", "ffn0", "l1", "l1out")

_OUT_SHAPES = [
    (4, 2048, 4096), (8, 8, 4096), (1, 4, 48, 64, 64), (1, 4, 1, A_SHIFT),
    (1, 4, 128, 2, 8, 128), (1, 4, 512, 2, 8, 128), (1, 4, 2048, 2, 8, 128),
    (2, 4, NMEM, 2, 4, 256),
    (1, 8, 48, 64, 64), (1, 8, 1, A_SHIFT),
    (1, 8, 128, 2, 8, 128), (1, 8, 512, 2, 8, 128), (1, 8, 2048, 2, 8, 128),
]


def _unchunk(a):
    return np.concatenate([a[0:36].reshape(-1), a[36, :96], a[37, :96], a[38:41].reshape(-1), a[41:45].reshape(-1)])


def kernel(**inputs):
    inp = {k: np.asarray(v) for k, v in inputs.items()}
    res = run(inp, ALL_PHASES)
    outs = [np.zeros(sh, np.float32) for sh in _OUT_SHAPES]
    for b in range(4):
        outs[7][:, b] = res[b]["o_memkv"].reshape(2, NMEM, 2, 4, 256)
        for z in range(2):
            r = res[z * 4 + b]
            hsl = slice(z * HH, (z + 1) * HH)
            outs[2][0, b, hsl] = r["o_wkv"][0]
            outs[8][0, 2 * b, hsl] = r["o_wkv"][1]
            outs[8][0, 2 * b + 1, hsl] = r["o_wkv"][2]
            o = r["o_shift0"].transpose(1, 0, 2)
            cols = _cols_a(z)
            sel = cols < A_SHIFT
            for j, dst in enumerate((outs[3][0, b, 0], outs[9][0, 2 * b, 0], outs[9][0, 2 * b + 1, 0])):
                dst[cols[sel]] = _unchunk(o[:, :, j])[sel]
            for g in range(3):
                outs[4 + g][0, b, :, :, 4 * z:4 * z + 4, :] = r["o_kvp%d" % g]
                outs[10 + g][0, 2 * b, :, :, 4 * z:4 * z + 4, :] = r["o_kvs%d" % g][0]
                outs[10 + g][0, 2 * b + 1, :, :, 4 * z:4 * z + 4, :] = r["o_kvs%d" % g][1]
            outs[0][b, z * HALF:(z + 1) * HALF] = r["o_y"][:HALF]
            outs[1][2 * b + z] = r["o_y"][HALF:]
    return tuple(outs)
```

```python
import contextlib
import numpy as np
import concourse.bass as bass
import concourse.mybir as mybir
from concourse.bass_utils import run_bass_kernel_spmd

F32 = mybir.dt.float32
BF16 = mybir.dt.bfloat16
ALU = mybir.AluOpType
AF = mybir.ActivationFunctionType
AX = mybir.AxisListType

NCORES = 8
D = 4096
KC = 32
SEQ = 2048
HALF = 1024
NS = 8
NMEM = 256
MEMW = 1024
MIXW = 3072
A_SHIFT = 9792
A_IN = 10816
B_IN = 10240
DFF = 11008
EPS = 1e-6
SHC = A_SHIFT // 8

NTOK = HALF + NS
TBLK = [(0, 512), (512, 512), (1024, NS)]
HH = 24
ZC = 4608 + 192 + 384 + 512
U0C = 2048 + 1 + 2 * 9
PAN_A = [(i * 512, [128] * 4) for i in range(9)] + [(4608, [96, 96]), (4800, [128] * 3), (5184, [128] * 4)]
Q4 = [[0, 1, 2, 3], [4, 5, 6, 7]]
P04 = [[0, 4], [1, 5], [2, 6], [3, 7]]
WSPEC2 = {
    "w_in_a": (4096, ZC, 256, "q4"), "w_out_a": (4096, 4096, 1024, "full"),
    "w_in_b": (4096, 5120, 256, "q4"), "w_out_b": (2048, 4096, 1024, "full"),
    "w_fi0": (DFF, 32 * 256, 256, "full"), "w_fi1": (DFF, 32 * 256, 256, "full"),
    "w_fo0": (4096, DFF, 128, "full"), "w_fo1": (4096, DFF, 128, "full"),
}
DBG_OUT = set()
UPERM = [0, 1, 4, 5, 2, 3, 6, 7]
WSPEC = {
    "w_mem0": (4096, 2048, 1), "w_mem1": (4096, 2048, 1),
    "w_in_a": (4096, A_IN, 2), "w_out_a": (4096, 4096, 1),
    "w_in_b": (4096, B_IN, 2), "w_out_b": (2048, 4096, 1),
    "w_fi0": (4096, 2 * DFF, 4), "w_fi1": (4096, 2 * DFF, 4),
    "w_fo0": (DFF, 4096, 2), "w_fo1": (DFF, 4096, 2),
}


class Buf:
    __slots__ = ("name", "w", "r")

    def __init__(self, name=""):
        self.name = name
        self.w = None
        self.r = []


class Eng:
    def __init__(self, name, sem):
        self.name = name
        self.sem = sem
        self.cnt = 0
        self.waited = {}
        self.ops = []


class KB:
    def __init__(self, nc, es):
        self.nc = nc
        self.es = es
        self.eng = {}
        for n in ("pe", "act", "dve", "pool", "sp"):
            s = es.enter_context(nc.semaphore("s_" + n))
            self.eng[n] = Eng(n, s)
        self.dq = {}
        for q in ("sp", "pool"):
            ring = []
            for i in range(8):
                s = es.enter_context(nc.semaphore("d_%s%d" % (q, i)))
                ring.append([s, 0])
            self.dq[q] = [ring, 0]

    def buf(self, name=""):
        return Buf(name)

    def _deps(self, reads, writes):
        deps = []
        for b in reads:
            if b.w is not None:
                deps.append(b.w)
        for b in writes:
            if b.w is not None:
                deps.append(b.w)
            deps.extend(b.r)
        return deps

    def _emit_waits(self, e, deps):
        need = {}
        for (s, v) in deps:
            if s is e.sem and e.name in ("pe", "sp"):
                continue
            k = id(s)
            if e.waited.get(k, 0) >= v:
                continue
            if need.get(k, (None, 0))[1] < v:
                need[k] = (s, v)
        for k, (s, v) in need.items():
            e.waited[k] = v
            e.ops.append(lambda h, s=s, v=v: h.wait_ge(s, v))

    def _mark(self, ev, reads, writes):
        for b in reads:
            if len(b.r) > 64:
                last = {}
                for (s, v) in b.r:
                    if last.get(id(s), (None, 0))[1] < v:
                        last[id(s)] = (s, v)
                b.r = list(last.values())
            b.r.append(ev)
        for b in writes:
            b.w = ev
            b.r = []

    def op(self, en, fn, reads=(), writes=()):
        e = self.eng[en]
        self._emit_waits(e, self._deps(reads, writes))
        e.cnt += 1
        sem = e.sem
        e.ops.append(lambda h, fn=fn, sem=sem: fn(h).then_inc(sem, 1))
        ev = (sem, e.cnt)
        self._mark(ev, reads, writes)
        return ev

    def dma(self, q, out, in_, reads=(), writes=(), **kw):
        e = self.eng[q]
        ring, idx = self.dq[q]
        slot = ring[idx]
        self.dq[q][1] = (idx + 1) % len(ring)
        deps = self._deps(reads, writes)
        if slot[1] > 0:
            deps.append((slot[0], slot[1]))
        self._emit_waits(e, deps)
        slot[1] += 16
        s, v = slot[0], slot[1]
        e.ops.append(lambda h, s=s, out=out, in_=in_, kw=kw: h.dma_start(out=out, in_=in_, **kw).then_inc(s, 16))
        ev = (s, v)
        self._mark(ev, reads, writes)
        return ev

    def coll(self, fn, reads=(), writes=()):
        return self.op("pool", fn, reads, writes)

    def finish(self, out_bufs):
        e = self.eng["sp"]
        deps = []
        for b in out_bufs:
            if b.w is not None:
                deps.append(b.w)
        for q in self.dq:
            for s, v in self.dq[q][0]:
                if v > 0:
                    deps.append((s, v))
        for n in ("pe", "act", "dve", "pool"):
            en = self.eng[n]
            if en.cnt > 0:
                deps.append((en.sem, en.cnt))
        self._emit_waits(e, deps)

    def barrier(self):
        deps = []
        for q in self.dq:
            for s, v in self.dq[q][0]:
                if v > 0:
                    deps.append((s, v))
        for n in ("pe", "act", "dve", "pool", "sp"):
            en = self.eng[n]
            if en.cnt > 0:
                deps.append((en.sem, en.cnt))
        for n in ("pe", "act", "dve", "pool", "sp"):
            self._emit_waits(self.eng[n], deps)

    def replay(self):
        nc = self.nc
        allops = {n: self.eng[n].ops for n in self.eng}
        for n in self.eng:
            self.eng[n].ops = []
        self._replay(allops)

    def _replay(self, allops):
        nc = self.nc
        with nc.Block() as block:
            @block.tensor
            def _(h):
                for f in allops["pe"]:
                    f(h)

            @block.scalar
            def _(h):
                for f in allops["act"]:
                    f(h)

            @block.vector
            def _(h):
                for f in allops["dve"]:
                    f(h)

            @block.gpsimd
            def _(h):
                for f in allops["pool"]:
                    f(h)

            @block.sync
            def _(h):
                for f in allops["sp"]:
                    f(h)


class T:
    def __init__(self, t, b):
        self.t = t
        self.b = b

    def __getitem__(self, k):
        return self.t[k]


class MK:
    def __init__(self, nc, es, phases):
        self.nc = nc
        self.es = es
        self.kb = KB(nc, es)
        self.phases = phases
        self.ins = {}
        self.outs = {}
        self.scr = {}
        self.rr = 0

    def inp(self, name, shape, dt=F32):
        ap = self.nc.dram_tensor(name, list(shape), dt, kind="ExternalInput").ap()
        t = T(ap, self.kb.buf(name))
        self.ins[name] = t
        return t

    def outp(self, name, shape, dt=F32):
        ap = self.nc.dram_tensor(name, list(shape), dt, kind="ExternalOutput").ap()
        t = T(ap, self.kb.buf(name))
        self.outs[name] = t
        return t

    def dram(self, name, shape, dt):
        kind = "ExternalOutput" if name in DBG_OUT else "Internal"
        ap = self.nc.dram_tensor(name, list(shape), dt, kind=kind).ap()
        t = T(ap, self.kb.buf(name))
        self.scr[name] = t
        return t

    def sb(self, stack, name, shape, dt):
        esz = 4 if dt == F32 else 2
        n = 1
        for d in shape[1:]:
            n *= d
        nbytes = (n * esz + 31) // 32 * 32
        nf = nbytes // 4
        if self.arena_off + nf > self.arena_n:
            raise RuntimeError("SBUF arena overflow: %s needs %d B at %d of %d" % (name, nbytes, self.arena_off * 4, self.arena_n * 4))
        v = self.arena[:, self.arena_off:self.arena_off + nf]
        self.arena_off += nf
        if dt != F32:
            v = v.bitcast(dt)
        v = v[:, 0:n]
        if len(shape) == 3:
            v = v.rearrange("p (a b) -> p a b", b=shape[2])
        elif len(shape) == 4:
            v = v.rearrange("p (a b c) -> p a b c", b=shape[2], c=shape[3])
        if shape[0] < 128:
            v = v[0:shape[0]]
        return T(v, self.kb.buf(name))

    @contextlib.contextmanager
    def phase(self):
        off = self.arena_off
        yield None
        self.kb.barrier()
        self.arena_off = off

    def ps(self, stack, name, shape, dt):
        t = stack.enter_context(self.nc.psum_tensor(name, list(shape), dt))
        return T(t, self.kb.buf(name))

    def op(self, en, fn, reads=(), writes=()):
        return self.kb.op(en, fn, [x.b for x in reads], [x.b for x in writes])

    def dma(self, q, out, in_, reads=(), writes=(), **kw):
        return self.kb.dma(q, out, in_, [x.b for x in reads], [x.b for x in writes], **kw)

    def rstd(self, t, ap, scale, eps):
        self.op("act", lambda h: h.activation(ap, ap, AF.Sqrt, bias=eps, scale=scale), [t], [t])
        self.op("dve", lambda h: h.reciprocal(ap, ap), [t], [t])

    def evac_engine(self):
        self.rr += 1
        return "dve" if self.rr % 2 else "act"

    def copy(self, en, out, in_, reads, writes):
        if en == "act":
            return self.op("act", lambda h: h.activation(out, in_, AF.Copy), reads, writes)
        return self.op(en, lambda h: h.tensor_copy(out, in_), reads, writes)

    def setup_common(self):
        es = self.es
        self.arena_n = 51 * 1024
        self.arena = es.enter_context(self.nc.sbuf_tensor("arena", [128, self.arena_n], F32))
        self.arena_off = 0
        self.c_identf = self.inp("c_identf", [128, 128])
        self.identb = self.sb(es, "identb", [128, 128], BF16)
        self.identf = self.sb(es, "identf", [128, 128], F32)
        self.dma("pool", self.identb[:], self.c_identf.t, [self.c_identf], [self.identb])
        self.dma("sp", self.identf[:], self.c_identf.t, [self.c_identf], [self.identf])
        c_sel = self.inp("c_sel", [128, 2])
        self.sel = self.sb(es, "sel", [128, 2], F32)
        self.dma("sp", self.sel[:], c_sel.t, [c_sel], [self.sel])
        self.ones_f = self.sb(es, "ones_f", [128, 128], F32)
        self.ones_b = self.sb(es, "ones_b", [128, 128], BF16)
        self.op("dve", lambda h: h.memset(self.ones_f.t[:], 1.0), [], [self.ones_f])
        self.op("dve", lambda h: h.memset(self.ones_b.t[:], 1.0), [], [self.ones_b])
        self.pf = [self.ps(es, "pf%d" % i, [128, 512], F32) for i in range(6)]
        self.pbk = [self.ps(es, "pb%d" % i, [128, 1024], BF16) for i in range(2)]
        self.pfi = 0
        self.pbi = 0

    def next_pf(self):
        p = self.pf[self.pfi % getattr(self, "pf_lim", len(self.pf))]
        self.pfi += 1
        return p

    def next_pb(self):
        p = self.pbk[self.pbi % len(self.pbk)]
        self.pbi += 1
        return p

    def phase_weights(self, names):
        self.W = getattr(self, "W", {})
        for name, (K, N) in names:
            src = self.inp(name, [K, N])
            full = self.dram(name + "_bf", [K, N], BF16)
            step = max(1, (2 << 20) // (N * 4))
            r0 = 0
            while r0 < K:
                r1 = min(K, r0 + step)
                self.dma("pool", full.t[r0:r1, :], src.t[r0:r1, :], [src], [full])
                r0 = r1
            self.W[name] = ([full], K, N)

    def wslice(self, name, k0, nk, c0, nc_):
        full, kg, N = self.W[name]
        g = k0 // kg
        assert (k0 + nk - 1) // kg == g
        return full[g], full[g].t[k0 - g * kg:k0 - g * kg + nk, c0:c0 + nc_]

    def load_wpanel(self, q, dst, name, nkc, c0, ncols, kc0=0):
        full, kg, N = self.W[name]
        kpg = kg // 128 if kg % 128 == 0 else None
        if kpg is None:
            for k in range(nkc):
                r0 = (kc0 + k) * 128
                done = 0
                while done < 128:
                    g = (r0 + done) // kg
                    lo = r0 + done - g * kg
                    n = min(128 - done, kg - lo)
                    self.dma(q, dst.t[done:done + n, k, 0:ncols], full[g].t[lo:lo + n, c0:c0 + ncols], [full[g]], [dst])
                    done += n
            return
        k = 0
        while k < nkc:
            g = (kc0 + k) // kpg
            lk = (kc0 + k) - g * kpg
            n = min(nkc - k, kpg - lk)
            src = full[g].t[lk * 128:(lk + n) * 128, c0:c0 + ncols].rearrange("(k p) n -> p k n", p=128)
            self.dma(q, dst.t[:, k:k + n, 0:ncols], src, [full[g]], [dst])
            k += n

    def rms_transpose(self, st, src_ap, src_t, ntok, gain, hT, col0, tag):
        x = self.sb(st, tag + "_x", [128, D], F32)
        xb = self.sb(st, tag + "_xb", [128, D], BF16)
        junk = self.sb(st, tag + "_j", [128, D], F32)
        ss = self.sb(st, tag + "_ss", [128, 1], F32)
        for t0 in range(0, ntok, 128):
            n = min(128, ntok - t0)
            self.dma("sp", x.t[0:n, :], src_ap[t0:t0 + n, :], [src_t], [x])
            self.op("act", lambda h, n=n: h.activation(junk.t[0:n, :], x.t[0:n, :], AF.Square), [x], [junk])
            self.op("dve", lambda h, n=n: h.reduce_sum(ss.t[0:n, :], junk.t[0:n, :], AX.X), [junk], [ss])
            self.rstd(ss, ss.t[0:n, :], 1.0 / D, EPS)
            self.op("act", lambda h, n=n: h.activation(xb.t[0:n, :], x.t[0:n, :], AF.Copy, scale=ss.t[0:n, 0:1]), [x, ss], [xb])
            for k0 in range(0, KC, 8):
                pb = self.next_pb()
                for j in range(8):
                    k = k0 + j
                    self.op("pe", lambda h, pb=pb, j=j, k=k, n=n: h.transpose(pb.t[:, j * 128:j * 128 + n], xb.t[0:n, k * 128:(k + 1) * 128], self.identb.t[0:n, 0:n]),
                            [xb, self.identb], [pb])
                o = hT.t[:, k0:k0 + 8, col0 + t0:col0 + t0 + n]
                i0 = pb.t[:, :].rearrange("p (j t) -> p j t", t=128)[:, :, 0:n]
                g = gain.t[:, k0:k0 + 8].unsqueeze(2).to_broadcast([128, 8, n])
                self.op("dve", lambda h, o=o, i0=i0, g=g: h.tensor_tensor(o, i0, g, ALU.mult), [pb, gain], [hT])

    def phase_memkv(self):
        memp = self.inp("memp", [NMEM, D])
        g_mem = self.inp("g_mem", [2, 128, KC])
        g_kmem = self.inp("g_kmem", [2, 128, 256])
        o_memkv = self.outp("o_memkv", [2, NMEM, 2 * MEMW])
        self.mkz = self.dram("mkz", [2, NMEM, 2, 512], F32)
        for li in range(2):
            with self.phase() as st:
                gain = self.sb(st, "mk_gain", [128, KC], F32)
                gk = self.sb(st, "mk_gk", [128, 256], F32)
                hT = self.sb(st, "mk_hT", [128, KC, NMEM], BF16)
                self.dma("sp", gain.t[:], g_mem.t[li], [g_mem], [gain])
                self.dma("sp", gk.t[:], g_kmem.t[li], [g_kmem], [gk])
                self.rms_transpose(st, memp.t, memp, NMEM, gain, hT, 0, "mk")
                wp = [self.sb(st, "mk_w%d" % i, [128, KC, 512], BF16) for i in range(2)]
                kv = [self.sb(st, "mk_kv%d" % i, [128, 2 * MEMW], F32) for i in range(2)]
                ssq = self.sb(st, "mk_ssq", [128, 4], F32)
                sq = self.sb(st, "mk_sq", [128, 4, 256], F32)
                zk = self.sb(st, "mk_zk", [128, 2, 512], F32)
                for pi in range(4):
                    w = wp[pi % 2]
                    self.load_wpanel("sp", w, "w_mem%d" % li, KC, pi * 512, 512)
                    for tt in range(2):
                        pf = self.next_pf()
                        for k in range(KC):
                            self.op("pe", lambda h, pf=pf, k=k, tt=tt, w=w: h.matmul(pf.t[:, :], hT.t[:, k, tt * 128:(tt + 1) * 128], w.t[:, k, :], start=(k == 0), stop=(k == KC - 1)),
                                    [hT, w], [pf])
                        self.copy("dve", kv[tt].t[:, pi * 512:(pi + 1) * 512], pf.t[:, :], [pf], [kv[tt]])
                for tt in range(2):
                    k3 = kv[tt].t[:, 0:MEMW].rearrange("p (h d) -> p h d", d=256)
                    self.op("act", lambda h, k3=k3: h.activation(sq.t[:], k3, AF.Square), [kv[tt]], [sq])
                    self.op("dve", lambda h: h.reduce_sum(ssq.t[:], sq.t[:], AX.X), [sq], [ssq])
                    self.op("act", lambda h: h.activation(ssq.t[:], ssq.t[:], AF.Sqrt, bias=EPS, scale=1.0 / 256), [ssq], [ssq])
                    self.op("dve", lambda h: h.reciprocal(ssq.t[:], ssq.t[:]), [ssq], [ssq])
                    self.op("pool", lambda h, k3=k3: h.tensor_tensor(k3, k3, ssq.t[:].unsqueeze(2).to_broadcast([128, 4, 256]), ALU.mult), [kv[tt], ssq], [kv[tt]])
                    self.op("dve", lambda h, k3=k3: h.tensor_tensor(k3, k3, gk.t[:].unsqueeze(1).to_broadcast([128, 4, 256]), ALU.mult), [kv[tt], gk], [kv[tt]])
                    self.dma("sp", o_memkv.t[li, tt * 128:(tt + 1) * 128, :], kv[tt].t[:], [kv[tt]], [o_memkv])
                    for part in range(2):
                        lo = kv[tt].t[:, part * 1024:part * 1024 + 512]
                        hi = kv[tt].t[:, part * 1024 + 512:part * 1024 + 1024]
                        self.op("dve", lambda h, part=part, lo=lo: h.tensor_scalar(zk.t[:, part, :], lo, self.sel.t[:, 0:1], None, ALU.mult), [kv[tt], self.sel], [zk])
                        self.op("dve", lambda h, part=part, hi=hi: h.scalar_tensor_tensor(zk.t[:, part, :], hi, self.sel.t[:, 1:2], zk.t[:, part, :], ALU.mult, ALU.add), [kv[tt], self.sel, zk], [zk])
                    self.dma("sp", self.mkz.t[li, tt * 128:(tt + 1) * 128], zk.t[:], [zk], [self.mkz])

    def phase_shift(self):
        NT = 12
        xlast = self.inp("xlast", [NT, D])
        g_mix0 = self.inp("g_mix0", [128, KC])
        o_shift = self.outp("o_shift", [NT, SHC])
        with self.phase() as st:
            gain = self.sb(st, "sh_gain", [128, KC], F32)
            hT = self.sb(st, "sh_hT", [128, KC, NT], BF16)
            res = self.sb(st, "sh_res", [NT, SHC], F32)
            self.dma("sp", gain.t[:], g_mix0.t, [g_mix0], [gain])
            self.rms_transpose(st, xlast.t, xlast, NT, gain, hT, 0, "sh")
            PW = SHC // 3
            wp = [self.sb(st, "sh_w%d" % i, [128, KC, PW], BF16) for i in range(2)]
            for pi in range(3):
                w = wp[pi % 2]
                self.load_wpanel("sp", w, "w_ina_c", KC, pi * PW, PW)
                pf = self.next_pf()
                for k in range(KC):
                    self.op("pe", lambda h, pf=pf, k=k, w=w: h.matmul(pf.t[0:NT, 0:PW], hT.t[:, k, 0:NT], w.t[:, k, 0:PW], start=(k == 0), stop=(k == KC - 1)),
                            [hT, w], [pf])
                self.copy("dve", res.t[:, pi * PW:(pi + 1) * PW], pf.t[0:NT, 0:PW], [pf], [res])
            self.dma("sp", o_shift.t, res.t[:], [res], [o_shift])

    def allgather(self, groups, src, dst):
        self.kb.coll(lambda h, i=src.t, o=dst.t: h.collective_compute("AllGather", ALU.bypass, replica_groups=groups, ins=[i], outs=[o]),
                     [src.b], [dst.b])

    def phase_weights2(self, names):
        self.W = getattr(self, "W", {})
        for name in names:
            K, N, ru, mode = WSPEC2[name]
            nr = 4 if mode == "q4" else 8
            full = self.dram(name + "_f", [K, N], BF16)
            parts = [(name, 0, K // ru, ru)]
            if K % ru:
                parts.append((name + "_t", (K // ru) * ru, 1, K % ru))
            for (pname, row0, nu, ru_) in parts:
                rows = ru_ // nr
                src = self.inp(pname, [nu, rows, N])
                shard = self.dram(pname + "_sh", [nu, rows, N], BF16)
                half = self.dram(pname + "_h", [nu, 4 * rows, N], BF16) if mode != "q4" else None
                step = max(1, (2 << 20) // (rows * N * 4))
                for u0 in range(0, nu, step):
                    u1 = min(nu, u0 + step)
                    self.dma("pool", shard.t[u0:u1], src.t[u0:u1], [src], [shard])
                for u in range(nu):
                    base = row0 + u * ru_
                    if mode == "q4":
                        self.kb.coll(lambda h, i=shard.t[u], o=full.t[base:base + ru_, :]: h.collective_compute(
                            "AllGather", ALU.bypass, replica_groups=Q4, ins=[i], outs=[o]), [shard.b], [full.b])
                    else:
                        self.kb.coll(lambda h, i=shard.t[u], o=half.t[u]: h.collective_compute(
                            "AllGather", ALU.bypass, replica_groups=Q4, ins=[i], outs=[o]), [shard.b], [half.b])
                        for j in range(2):
                            i_ap = half.t[u, j * 2 * rows:(j + 1) * 2 * rows, :]
                            o_ap = full.t[base + j * 4 * rows:base + (j + 1) * 4 * rows, :]
                            self.kb.coll(lambda h, i=i_ap, o=o_ap: h.collective_compute(
                                "AllGather", ALU.bypass, replica_groups=P04, ins=[i], outs=[o]), [half.b], [full.b])
            self.W[name] = ([full], K, N)

    def pair_gather(self, src, dst, rows, ncols, esz):
        rp = max(128, ((2 << 20) // (ncols * esz)) // 128 * 128)
        pieces = []
        r0 = 0
        while r0 < rows:
            r1 = min(rows, r0 + rp)
            self.kb.coll(lambda h, i=src.t[r0:r1, :], o=dst.t[2 * r0:2 * r1, :]: h.collective_compute(
                "AllGather", ALU.bypass, replica_groups=P04, ins=[i], outs=[o]), [src.b], [dst.b])
            pieces.append((r0, r1))
            r0 = r1
        return pieces

    def load_wpanel_pm(self, q, dst, name, j, nkc, ncols):
        full = self.W[name][0][0]
        self.dma(q, dst.t[:, 0:nkc, 0:ncols], full.t[j * 128:(j + 1) * 128, :].rearrange("p (k c) -> p k c", c=ncols), [full], [dst])

    def gemm_fm(self, st, xT, ntok, wname, panels, nkc, sink, tag, wbufs=None, kc0=0, pm=False):
        pw = max(sum(sz) for _, sz in panels)
        if wbufs is None:
            wbufs = [self.sb(st, "%s_w%d" % (tag, i), [128, nkc, pw], BF16) for i in range(2)]
        ci = 0
        for pi, (c0, sizes) in enumerate(panels):
            w = wbufs[pi % 2]
            if pm:
                self.load_wpanel_pm("sp", w, wname, pi, nkc, sum(sizes))
            else:
                self.load_wpanel("sp", w, wname, nkc, c0, sum(sizes), kc0=kc0)
            off = 0
            for m in sizes:
                pf = self.next_pf()
                for k in range(nkc):
                    self.op("pe", lambda h, pf=pf, k=k, w=w, off=off, m=m: h.matmul(pf.t[0:m, 0:ntok], w.t[:, k, off:off + m], xT.t[:, k, 0:ntok], start=(k == 0), stop=(k == nkc - 1)),
                            [xT, w], [pf])
                sink(ci, pf, m)
                off += m
                ci += 1

    def phase_l0_prep(self):
        x_own = self.inp("x_own", [NTOK, D])
        g_mix = self.inp("g_mix", [2, 128, KC])
        self.g_mix = g_mix
        self.XT = [self.dram("XT%d" % i, [128, KC, NTOK], F32) for i in range(5)]
        hT_own = self.hT_own = self.dram("hT_own", [D, NTOK], BF16)
        self.hT_pair = self.dram("hT_pair", [2 * D, NTOK], BF16)
        with self.phase() as st:
            gain = self.sb(st, "p0_gain", [128, KC], F32)
            self.dma("sp", gain.t[:], g_mix.t[0], [g_mix], [gain])
            x = self.sb(st, "p0_x", [128, D], F32)
            xb = self.sb(st, "p0_xb", [128, D], BF16)
            junk = self.sb(st, "p0_j", [128, D], F32)
            ss = self.sb(st, "p0_ss", [128, 1], F32)
            hst = [self.sb(st, "p0_h%d" % i, [128, KC, 128], BF16) for i in range(2)]
            xst = [self.sb(st, "p0_xs%d" % i, [128, KC, 128], F32) for i in range(2)]
            for ti, t0 in enumerate(range(0, NTOK, 128)):
                n = min(128, NTOK - t0)
                hs, xs = hst[ti % 2], xst[ti % 2]
                self.dma("sp", x.t[0:n, :], x_own.t[t0:t0 + n, :], [x_own], [x])
                self.op("act", lambda h, n=n: h.activation(junk.t[0:n, :], x.t[0:n, :], AF.Square), [x], [junk])
                self.op("dve", lambda h, n=n: h.reduce_sum(ss.t[0:n, :], junk.t[0:n, :], AX.X), [junk], [ss])
                self.rstd(ss, ss.t[0:n, :], 1.0 / D, EPS)
                self.op("act", lambda h, n=n: h.activation(xb.t[0:n, :], x.t[0:n, :], AF.Copy, scale=ss.t[0:n, 0:1]), [x, ss], [xb])
                for k0 in range(0, KC, 8):
                    pb = self.next_pb()
                    for j in range(8):
                        k = k0 + j
                        self.op("pe", lambda h, pb=pb, j=j, k=k, n=n: h.transpose(pb.t[:, j * 128:j * 128 + n], xb.t[0:n, k * 128:(k + 1) * 128], self.identb.t[0:n, 0:n]),
                                [xb, self.identb], [pb])
                    o = hs.t[:, k0:k0 + 8, 0:n]
                    i0 = pb.t[:, :].rearrange("p (j t) -> p j t", t=128)[:, :, 0:n]
                    g = gain.t[:, k0:k0 + 8].unsqueeze(2).to_broadcast([128, 8, n])
                    self.op("dve", lambda h, o=o, i0=i0, g=g: h.tensor_tensor(o, i0, g, ALU.mult), [pb, gain], [hs])
                for k0 in range(0, KC, 4):
                    pf = self.next_pf()
                    for j in range(4):
                        k = k0 + j
                        self.op("pe", lambda h, pf=pf, j=j, k=k, n=n: h.transpose(pf.t[:, j * 128:j * 128 + n], x.t[0:n, k * 128:(k + 1) * 128], self.identf.t[0:n, 0:n]),
                                [x, self.identf], [pf])
                    o = xs.t[:, k0:k0 + 4, 0:n]
                    i0 = pf.t[:, :].rearrange("p (j t) -> p j t", t=128)[:, :, 0:n]
                    self.copy("act", o, i0, [pf], [xs])
                self.dma("sp", hT_own.t[:, t0:t0 + n].rearrange("(k p) t -> p k t", p=128), hs.t[:, :, 0:n], [hs], [hT_own])
                self.dma("sp", self.XT[0].t[:, :, t0:t0 + n], xs.t[:, :, 0:n], [xs], [self.XT[0]])
        self.hT_pieces = self.pair_gather(hT_own, self.hT_pair, D, NTOK, 2)

    def phase_l0_inproj(self):
        self.U0 = self.dram("U0", [45, 128, U0C], F32)
        o_shift = self.outp("o_shift0", [128, 45, 3])
        U0 = self.U0
        blocks = [(0, 0, 512, 1), (0, 512, 512, 513), (1, 0, 512, 1025), (1, 512, 512, 1537)]
        with self.phase() as st:
            xT = [self.sb(st, "ip_x%d" % i, [128, KC, 512], BF16) for i in range(2)]
            stg = [self.sb(st, "ip_s%d" % i, [128, 512], F32) for i in range(3)]
            wbufs = [self.sb(st, "ip_w%d" % i, [128, KC, 512], BF16) for i in range(2)]
            sh = self.sb(st, "ip_sh", [128, 45, 3], F32)
            self.op("dve", lambda h: h.memset(sh.t[:], 0.0), [], [sh])
            self.sidx = 0

            def run_block(bi, loads, ntok, outs):
                x = xT[bi % 2]
                for (dst0, r, t0, n) in loads:
                    for (r0, r1) in self.hT_pieces:
                        base = 2 * r0 + r * (r1 - r0)
                        src = self.hT_pair.t[base:base + (r1 - r0), t0:t0 + n].rearrange("(k p) t -> p k t", p=128)
                        self.dma("sp", x.t[:, r0 // 128:r1 // 128, dst0:dst0 + n], src, [self.hT_pair], [x])

                def sink(ci, pf, m):
                    sg = stg[self.sidx % 3]
                    self.sidx += 1
                    self.copy(self.evac_engine(), sg.t[0:m, 0:ntok], pf.t[0:m, 0:ntok], [pf], [sg])
                    for (s0, n, c0) in outs:
                        self.dma("sp", U0.t[ci, 0:m, c0:c0 + n], sg.t[0:m, s0:s0 + n], [sg], [U0])
                        j = {2048: 0, 2057: 1, 2066: 2}.get(c0 + n - 1)
                        if j is not None:
                            self.op("dve", lambda h, j=j, ci=ci, m=m, sg=sg, e=s0 + n - 1: h.tensor_copy(sh.t[0:m, ci, j:j + 1], sg.t[0:m, e:e + 1]), [sg], [sh])
                self.gemm_fm(st, x, ntok, "w_in_a", PAN_A, KC, sink, "ip", wbufs=wbufs)

            for bi, (r, t0, n, c0) in enumerate(blocks):
                run_block(bi, [(0, r, t0, n)], n, [(0, n, c0)])
            run_block(4, [(0, 0, HALF, NS), (NS, 1, HALF, NS)], 2 * NS, [(0, NS, 2050), (NS, NS, 2059)])
            self.dma("sp", o_shift.t, sh.t[:], [sh], [o_shift])

    def phase_rwkv(self):
        U0 = self.U0
        NP = 10
        pa_in = self.inp("pa", [128, NP, 12])
        pl_in = self.inp("pl", [128, 5])
        shcol = self.inp("shcol", [128, 45, 2])
        w2_in = self.inp("w2z", [96, 1536])
        a2_in = self.inp("a2z", [96, 1536])
        g2_in = self.inp("g2z", [128, 3, 1536])
        st_in = self.inp("wkv_in", [2, HH, 64, 64])
        msk_in = self.inp("c_masks", [4, 128, 128])
        o_wkv = self.outp("o_wkv", [3, HH, 64, 64])
        self.opT0 = self.dram("opT0", [2048, 2 * NTOK], BF16)
        SEGS = [(1, 2048, 0, 0), (2050, 8, 1, 2048), (2059, 8, 2, 2056)]
        with self.phase() as st:
            pa = self.sb(st, "rw_pa", [128, NP, 12], F32)
            pl = self.sb(st, "rw_pl", [128, 5], F32)
            sc = self.sb(st, "rw_sc", [128, 45, 2], F32)
            px = self.sb(st, "rw_px", [128, 2, 12], F32)
            w2 = self.sb(st, "rw_w2", [96, 1536], BF16)
            a2 = self.sb(st, "rw_a2", [96, 1536], BF16)
            g2 = self.sb(st, "rw_g2", [128, 3, 1536], BF16)
            mk = self.sb(st, "rw_mk", [128, 4, 128], F32)
            bones = self.sb(st, "rw_bo", [128, 128], F32)
            self.dma("sp", pa.t[:], pa_in.t, [pa_in], [pa])
            self.dma("sp", pl.t[:], pl_in.t, [pl_in], [pl])
            self.dma("sp", sc.t[:], shcol.t, [shcol], [sc])
            self.dma("pool", w2.t[:], w2_in.t, [w2_in], [w2])
            self.dma("pool", a2.t[:], a2_in.t, [a2_in], [a2])
            self.dma("pool", g2.t[:], g2_in.t, [g2_in], [g2])
            self.dma("sp", mk.t[:], msk_in.t.rearrange("m p c -> p m c"), [msk_in], [mk])
            self.op("dve", lambda h: h.tensor_copy(bones.t[:], mk.t[:, 2, :]), [mk], [bones])
            self.op("dve", lambda h: h.tensor_scalar(px.t[:, 0, :], pa.t[:, 3, :], -1.0, None, ALU.mult), [pa], [px])
            self.op("dve", lambda h: h.tensor_scalar(px.t[:, 1, :], pa.t[:, 6, :], -1.0, 1.0, ALU.mult, ALU.add), [pa], [px])
            MS, MI, ML = mk.t[:, 0, :], mk.t[:, 1, :], mk.t[:, 3, :]

            def load_shift(dst, ci, mu_ap, rows, tag):
                raw = self.sb(st, tag + "_raw", [128, U0C], F32) if not hasattr(self, "_raw") else self._raw
                self._raw = raw
                d = self._dtmp
                self.dma("sp", raw.t[0:rows, :], U0.t[ci, 0:rows, :], [U0], [raw])
                self.op("dve", lambda h: h.memset(raw.t[0:rows, 0:1], 0.0), [], [raw])
                self.op("dve", lambda h: h.tensor_copy(raw.t[0:rows, 2049:2050], sc.t[0:rows, ci, 0:1]), [sc], [raw])
                self.op("dve", lambda h: h.tensor_copy(raw.t[0:rows, 2058:2059], sc.t[0:rows, ci, 1:2]), [sc], [raw])
                self.op("dve", lambda h: h.tensor_tensor(d.t[0:rows, 0:U0C - 1], raw.t[0:rows, 0:U0C - 1], raw.t[0:rows, 1:U0C], ALU.subtract), [raw], [d])
                self.op("dve", lambda h: h.scalar_tensor_tensor(dst.t[0:rows, 1:U0C], d.t[0:rows, 0:U0C - 1], mu_ap, raw.t[0:rows, 1:U0C], ALU.mult, ALU.add), [d, raw, pl, pa], [dst])

            self._dtmp = self.sb(st, "rw_d", [128, U0C], F32)
            self._raw = self.sb(st, "rw_raw", [128, U0C], F32)
            lt = self.sb(st, "rw_bon", [128, U0C], F32)
            twl = self.sb(st, "rw_twl", [96, U0C], BF16)
            tal = self.sb(st, "rw_tal", [96, U0C], BF16)
            sgl = self.sb(st, "rw_sgl", [128, 3, U0C], BF16)
            load_shift(lt, 36, pl.t[0:96, 0:1], 96, "rw")
            self.op("act", lambda h: h.activation(twl.t[:, 1:U0C], lt.t[0:96, 1:U0C], AF.Tanh), [lt], [twl])
            load_shift(lt, 37, pl.t[0:96, 1:2], 96, "rw")
            self.op("act", lambda h: h.activation(tal.t[:, 1:U0C], lt.t[0:96, 1:U0C], AF.Copy), [lt], [tal])
            for j in range(3):
                load_shift(lt, 38 + j, pl.t[:, 2 + j:3 + j], 128, "rw")
                self.op("act", lambda h, j=j: h.activation(sgl.t[:, j, 1:U0C], lt.t[:, 1:U0C], AF.Sigmoid), [lt], [sgl])

            F = lambda n: self.sb(st, "rw_" + n, [128, U0C], F32)
            r, k, v, lw, L, L2, a, g, kk, kp, t1 = [F(n) for n in ("r", "k", "v", "lw", "L", "L2", "a", "g", "kk", "kp", "t1")]
            bon = lt
            ar = self.sb(st, "rw_ar", [128, 2, U0C], BF16)
            bT = self.sb(st, "rw_bT", [128, U0C], BF16)
            kT = self.sb(st, "rw_kT", [128, U0C], BF16)
            vb = self.sb(st, "rw_vb", [128, U0C], BF16)
            pc = self._raw
            opo = self.sb(st, "rw_opo", [128, 2 * NTOK], BF16)
            Tst = self.sb(st, "rw_T", [128, 64], F32)
            Tb = self.sb(st, "rw_Tb", [128, 64], BF16)
            zs = self.sb(st, "rw_zs", [64, 128], F32)
            tm = [self.sb(st, "rw_tm%d" % i, [128, 3, 128], BF16) for i in range(2)]
            sm = {n: self.sb(st, "rw_" + n, [128, 128], BF16) for n in ("N", "NT", "N2", "NT2", "X", "X2", "Aak", "Abr", "Akr", "W1", "U")}
            ytm = self.sb(st, "rw_y", [128, 128], F32)
            gs = self.sb(st, "rw_gs", [128, 8], F32)
            tsum = self.sb(st, "rw_ts", [128, 64], F32)
            wout = self.sb(st, "rw_wo", [64, 128], F32)

            def head_pair(hp):
                P = lambda i: pa.t[:, i, hp:hp + 1]
                load_shift(r, hp, P(0), 128, "rw")
                load_shift(k, 12 + hp, P(1), 128, "rw")
                load_shift(v, 24 + hp, P(2), 128, "rw")
                for c0 in range(1, U0C, 512):
                    n = min(512, U0C - c0)
                    cs = slice(c0, c0 + n)
                    pf = self.next_pf()
                    self.op("pe", lambda h, pf=pf, cs=cs, n=n: h.matmul(pf.t[:, 0:n], w2.t[:, hp * 128:(hp + 1) * 128], twl.t[:, cs], start=True, stop=True), [w2, twl], [pf])
                    self.op("act", lambda h, pf=pf, cs=cs, n=n: h.activation(lw.t[:, cs], pf.t[:, 0:n], AF.Exp, bias=px.t[:, 0, hp:hp + 1], scale=-1.0), [pf, px], [lw])
                    self.op("act", lambda h, cs=cs: h.activation(lw.t[:, cs], lw.t[:, cs], AF.Ln, bias=1.0), [lw], [lw])
                    self.op("act", lambda h, cs=cs: h.activation(lw.t[:, cs], lw.t[:, cs], AF.Exp, bias=-0.5, scale=-1.0), [lw], [lw])
                    pf = self.next_pf()
                    self.op("pe", lambda h, pf=pf, cs=cs, n=n: h.matmul(pf.t[:, 0:n], a2.t[:, hp * 128:(hp + 1) * 128], tal.t[:, cs], start=True, stop=True), [a2, tal], [pf])
                    self.op("act", lambda h, pf=pf, cs=cs, n=n: h.activation(a.t[:, cs], pf.t[:, 0:n], AF.Sigmoid, bias=P(4)), [pf, pa], [a])
                    pf = self.next_pf()
                    for j in range(3):
                        self.op("pe", lambda h, pf=pf, cs=cs, n=n, j=j: h.matmul(pf.t[:, 0:n], g2.t[:, j, hp * 128:(hp + 1) * 128], sgl.t[:, j, cs], start=(j == 0), stop=(j == 2)), [g2, sgl], [pf])
                    self.copy("dve", g.t[:, cs], pf.t[:, 0:n], [pf], [g])
                    self.op("dve", lambda h, cs=cs: h.tensor_scalar(kk.t[:, cs], k.t[:, cs], P(5), None, ALU.mult), [k, pa], [kk])
                    self.op("dve", lambda h, cs=cs: h.tensor_tensor(t1.t[:, cs], kk.t[:, cs], kk.t[:, cs], ALU.mult), [kk], [t1])
                    pf = self.next_pf()
                    self.op("pe", lambda h, pf=pf, cs=cs, n=n: h.matmul(pf.t[:, 0:n], bones.t[:], t1.t[:, cs], start=True, stop=True), [bones, t1], [pf])
                    self.op("act", lambda h, pf=pf, cs=cs, n=n: h.activation(t1.t[:, cs], pf.t[:, 0:n], AF.Sqrt), [pf], [t1])
                    self.op("dve", lambda h, cs=cs: h.tensor_scalar(t1.t[:, cs], t1.t[:, cs], 1e-12, None, ALU.max), [t1], [t1])
                    self.op("dve", lambda h, cs=cs: h.reciprocal(t1.t[:, cs], t1.t[:, cs]), [t1], [t1])
                    self.op("dve", lambda h, cs=cs: h.tensor_tensor(kk.t[:, cs], kk.t[:, cs], t1.t[:, cs], ALU.mult), [kk, t1], [kk])
                    self.op("dve", lambda h, cs=cs: h.tensor_scalar(t1.t[:, cs], a.t[:, cs], P(6), px.t[:, 1, hp:hp + 1], ALU.mult, ALU.add), [a, pa, px], [t1])
                    self.op("dve", lambda h, cs=cs: h.tensor_tensor(kp.t[:, cs], k.t[:, cs], t1.t[:, cs], ALU.mult), [k, t1], [kp])
                    self.op("dve", lambda h, cs=cs: h.scalar_tensor_tensor(t1.t[:, cs], r.t[:, cs], P(7), kp.t[:, cs], ALU.mult, ALU.mult), [r, kp, pa], [t1])
                    pf = self.next_pf()
                    self.op("pe", lambda h, pf=pf, cs=cs, n=n: h.matmul(pf.t[:, 0:n], bones.t[:], t1.t[:, cs], start=True, stop=True), [bones, t1], [pf])
                    self.op("dve", lambda h, pf=pf, cs=cs, n=n: h.tensor_tensor(bon.t[:, cs], pf.t[:, 0:n], v.t[:, cs], ALU.mult), [pf, v], [bon])
                self.op("dve", lambda h: h.tensor_scalar(L.t[:, 1:U0C], lw.t[:, 1:U0C], -1.0, None, ALU.mult), [lw], [L])
                src, dst = L, L2
                sh_ = 1
                while sh_ < 128:
                    s3 = src.t[:, 1:2049].rearrange("p (c t) -> p c t", t=128)
                    d3 = dst.t[:, 1:2049].rearrange("p (c t) -> p c t", t=128)
                    self.op("dve", lambda h, s3=s3, d3=d3, sh_=sh_: h.tensor_tensor(d3[:, :, sh_:128], s3[:, :, sh_:128], s3[:, :, 0:128 - sh_], ALU.add), [src], [dst])
                    self.op("dve", lambda h, s3=s3, d3=d3, sh_=sh_: h.tensor_copy(d3[:, :, 0:sh_], s3[:, :, 0:sh_]), [src], [dst])
                    if sh_ < 8:
                        for c0 in (2050, 2059):
                            self.op("dve", lambda h, c0=c0, sh_=sh_, src=src, dst=dst: h.tensor_tensor(dst.t[:, c0 + sh_:c0 + 8], src.t[:, c0 + sh_:c0 + 8], src.t[:, c0:c0 + 8 - sh_], ALU.add), [src], [dst])
                            self.op("dve", lambda h, c0=c0, sh_=sh_, src=src, dst=dst: h.tensor_copy(dst.t[:, c0:c0 + sh_], src.t[:, c0:c0 + sh_]), [src], [dst])
                    else:
                        for c0 in (2050, 2059):
                            self.op("dve", lambda h, c0=c0, src=src, dst=dst: h.tensor_copy(dst.t[:, c0:c0 + 8], src.t[:, c0:c0 + 8]), [src], [dst])
                    src, dst = dst, src
                    sh_ *= 2
                Lc = src
                A_ = slice(1, U0C)
                self.op("act", lambda h: h.activation(pc.t[:, A_], Lc.t[:, A_], AF.Exp), [Lc], [pc])
                self.op("dve", lambda h: h.tensor_tensor(ar.t[:, 1, A_], r.t[:, A_], pc.t[:, A_], ALU.mult), [r, pc], [ar])
                self.op("act", lambda h: h.activation(t1.t[:, A_], Lc.t[:, A_], AF.Exp, scale=-1.0), [Lc], [t1])
                self.op("dve", lambda h: h.tensor_tensor(kT.t[:, A_], kp.t[:, A_], t1.t[:, A_], ALU.mult), [kp, t1], [kT])
                self.op("dve", lambda h: h.tensor_tensor(kp.t[:, A_], kk.t[:, A_], a.t[:, A_], ALU.mult), [kk, a], [kp])
                self.op("dve", lambda h: h.tensor_tensor(bT.t[:, A_], kp.t[:, A_], t1.t[:, A_], ALU.mult), [kp, t1], [bT])
                self.op("dve", lambda h: h.tensor_tensor(dst.t[:, A_], Lc.t[:, A_], lw.t[:, A_], ALU.add), [Lc, lw], [dst])
                self.op("act", lambda h: h.activation(t1.t[:, A_], dst.t[:, A_], AF.Exp), [dst], [t1])
                self.op("dve", lambda h: h.scalar_tensor_tensor(ar.t[:, 0, A_], kk.t[:, A_], -1.0, t1.t[:, A_], ALU.mult, ALU.mult), [kk, t1], [ar])
                self.op("act", lambda h: h.activation(vb.t[:, A_], v.t[:, A_], AF.Copy), [v], [vb])

                for (c00, ntok, si, oc0) in SEGS:
                    if si == 0:
                        self.op("dve", lambda h: h.memset(Tst.t[:], 0.0), [], [Tst])
                    else:
                        self.dma("sp", zs.t[:].rearrange("v (h k) -> v h k", h=2), st_in.t[si - 1, 2 * hp:2 * hp + 2].rearrange("h v k -> v h k"), [st_in], [zs])
                        pf = self.next_pf()
                        self.op("pe", lambda h, pf=pf: h.transpose(pf.t[:, 0:64], zs.t[:, :], self.identf.t[0:64, 0:64]), [zs, self.identf], [pf])
                        self.copy("dve", Tst.t[:], pf.t[:, 0:64], [pf], [Tst])
                    self.op("act", lambda h: h.activation(Tb.t[:], Tst.t[:], AF.Copy), [Tst], [Tb])
                    for ch in range(0, ntok, 128):
                        C = min(128, ntok - ch)
                        cs = slice(c00 + ch, c00 + ch + C)
                        tmt = tm[(ch // 128) % 2]
                        pb = self.next_pb()
                        for j, srcT in enumerate((vb, bT, kT)):
                            self.op("pe", lambda h, pb=pb, j=j, srcT=srcT, cs=cs, C=C: h.transpose(pb.t[0:C, j * 128:(j + 1) * 128], srcT.t[:, cs], self.identb.t[:, :]), [srcT, self.identb], [pb])
                        self.copy("act", tmt.t[0:C, :, :], pb.t[0:C, 0:384].rearrange("p (j c) -> p j c", c=128), [pb], [tmt])
                        for hh in range(2):
                            hs = slice(hh * 64, hh * 64 + 64)
                            Vh, Bh, Kh = tmt.t[0:C, 0, hs], tmt.t[0:C, 1, :], tmt.t[0:C, 2, :]
                            arh = ar.t[hs, :, cs]
                            p1, p2, p3 = self.next_pf(), self.next_pf(), self.next_pf()
                            self.op("pe", lambda h, p1=p1, arh=arh, hs=hs, cs=cs, C=C: h.matmul(p1.t[0:C, 0:2 * C].rearrange("p (a t) -> p a t", a=2), bT.t[hs, cs], arh, start=True, stop=True), [bT, ar], [p1])
                            self.op("pe", lambda h, p2=p2, arh=arh, hs=hs, cs=cs, C=C: h.matmul(p2.t[0:C, 0:2 * C].rearrange("p (a t) -> p a t", a=2), kT.t[hs, cs], arh, start=True, stop=True), [kT, ar], [p2])
                            self.op("pe", lambda h, p3=p3, hs=hs, cs=cs, C=C: h.matmul(p3.t[0:C, 0:C], ar.t[hs, 0, cs], bT.t[hs, cs], start=True, stop=True), [bT, ar], [p3])
                            N_, NT_, X_ = sm["N"], sm["NT"], sm["X"]
                            self.op("dve", lambda h, p1=p1, C=C: h.tensor_tensor(sm["N"].t[0:C, 0:C], p1.t[0:C, 0:C], MS[0:C, 0:C], ALU.mult), [p1, mk], [sm["N"]])
                            self.op("dve", lambda h, p1=p1, C=C: h.tensor_tensor(sm["Abr"].t[0:C, 0:C], p1.t[0:C, C:2 * C], MI[0:C, 0:C], ALU.mult), [p1, mk], [sm["Abr"]])
                            self.op("dve", lambda h, p2=p2, C=C: h.tensor_tensor(sm["Aak"].t[0:C, 0:C], p2.t[0:C, 0:C], MS[0:C, 0:C], ALU.mult), [p2, mk], [sm["Aak"]])
                            self.op("dve", lambda h, p2=p2, C=C: h.tensor_tensor(sm["Akr"].t[0:C, 0:C], p2.t[0:C, C:2 * C], MI[0:C, 0:C], ALU.mult), [p2, mk], [sm["Akr"]])
                            self.op("dve", lambda h, p3=p3, C=C: h.tensor_tensor(sm["NT"].t[0:C, 0:C], p3.t[0:C, 0:C], ML[0:C, 0:C], ALU.mult), [p3, mk], [sm["NT"]])
                            self.op("dve", lambda h, C=C: h.tensor_tensor(sm["X"].t[0:C, 0:C], sm["N"].t[0:C, 0:C], self.identb.t[0:C, 0:C], ALU.add), [sm["N"], self.identb], [sm["X"]])
                            pw, pwT, X, pw2, pwT2, X2 = sm["N"], sm["NT"], sm["X"], sm["N2"], sm["NT2"], sm["X2"]
                            nst = 0
                            while (1 << (nst + 1)) < C:
                                nst += 1
                            for it in range(nst):
                                q1, q2 = self.next_pf(), self.next_pf()
                                self.op("pe", lambda h, q1=q1, pw=pw, pwT=pwT, C=C: h.matmul(q1.t[0:C, 0:C], pwT.t[0:C, 0:C], pw.t[0:C, 0:C], start=True, stop=True), [pw, pwT], [q1])
                                self.op("pe", lambda h, q2=q2, pw=pw, pwT=pwT, C=C: h.matmul(q2.t[0:C, 0:C], pw.t[0:C, 0:C], pwT.t[0:C, 0:C], start=True, stop=True), [pw, pwT], [q2])
                                self.copy("dve", pw2.t[0:C, 0:C], q1.t[0:C, 0:C], [q1], [pw2])
                                self.copy("act", pwT2.t[0:C, 0:C], q2.t[0:C, 0:C], [q2], [pwT2])
                                q3 = self.next_pf()
                                self.op("pe", lambda h, q3=q3, X=X, C=C: h.matmul(q3.t[0:C, 0:C], self.identb.t[0:C, 0:C], X.t[0:C, 0:C], start=True, stop=False), [X, self.identb], [q3])
                                self.op("pe", lambda h, q3=q3, X=X, pwT2=pwT2, C=C: h.matmul(q3.t[0:C, 0:C], pwT2.t[0:C, 0:C], X.t[0:C, 0:C], start=False, stop=True), [X, pwT2], [q3])
                                self.copy("dve", X2.t[0:C, 0:C], q3.t[0:C, 0:C], [q3], [X2])
                                pw, pw2 = pw2, pw
                                pwT, pwT2 = pwT2, pwT
                                X, X2 = X2, X
                            q = self.next_pf()
                            self.op("pe", lambda h, q=q, hs=hs, cs=cs, C=C: h.matmul(q.t[0:C, 0:64], ar.t[hs, 0, cs], Tb.t[hs, :], start=True, stop=False), [ar, Tb], [q])
                            self.op("pe", lambda h, q=q, Vh=Vh, C=C: h.matmul(q.t[0:C, 0:64], sm["Aak"].t[0:C, 0:C], Vh, start=False, stop=True), [sm["Aak"], tmt], [q])
                            self.copy("act", sm["W1"].t[0:C, 0:64], q.t[0:C, 0:64], [q], [sm["W1"]])
                            q = self.next_pf()
                            self.op("pe", lambda h, q=q, X=X, C=C: h.matmul(q.t[0:C, 0:64], X.t[0:C, 0:C], sm["W1"].t[0:C, 0:64], start=True, stop=True), [X, sm["W1"]], [q])
                            self.copy("dve", sm["U"].t[0:C, 0:64], q.t[0:C, 0:64], [q], [sm["U"]])
                            q = self.next_pf()
                            self.op("pe", lambda h, q=q, hs=hs, cs=cs, C=C: h.matmul(q.t[0:C, 0:64], ar.t[hs, 1, cs], Tb.t[hs, :], start=True, stop=False), [ar, Tb], [q])
                            self.op("pe", lambda h, q=q, C=C: h.matmul(q.t[0:C, 0:64], sm["Abr"].t[0:C, 0:C], sm["U"].t[0:C, 0:64], start=False, stop=False), [sm["Abr"], sm["U"]], [q])
                            self.op("pe", lambda h, q=q, Vh=Vh, C=C: h.matmul(q.t[0:C, 0:64], sm["Akr"].t[0:C, 0:C], Vh, start=False, stop=True), [sm["Akr"], tmt], [q])
                            self.copy("act", ytm.t[0:C, hs], q.t[0:C, 0:64], [q], [ytm])
                            q = self.next_pf()
                            self.op("pe", lambda h, q=q, Bh=Bh, C=C: h.matmul(q.t[:, 0:64], Bh, sm["U"].t[0:C, 0:64], start=True, stop=False), [tmt, sm["U"]], [q])
                            self.op("pe", lambda h, q=q, Kh=Kh, Vh=Vh: h.matmul(q.t[:, 0:64], Kh, Vh, start=False, stop=True), [tmt], [q])
                            self.op("dve", lambda h, q=q, hs=hs: h.tensor_tensor(tsum.t[hs, :], q.t[hs, 0:64], Tst.t[hs, :], ALU.add), [q, Tst], [tsum])
                            pcol = c00 + ch + C - 1
                            self.op("act", lambda h, hs=hs, pcol=pcol: h.activation(Tst.t[hs, :], tsum.t[hs, :], AF.Copy, scale=pc.t[hs, pcol:pcol + 1]), [tsum, pc], [Tst])
                            self.op("dve", lambda h, hs=hs: h.tensor_copy(Tb.t[hs, :], Tst.t[hs, :]), [Tst], [Tb])
                        y3 = ytm.t[0:C, :].rearrange("p (h v) -> p h v", h=2)
                        self.op("dve", lambda h, y3=y3, C=C: h.reduce_sum(gs.t[0:C, 0:2], y3, AX.X), [ytm], [gs])
                        self.op("act", lambda h, C=C: h.activation(gs.t[0:C, 0:2], gs.t[0:C, 0:2], AF.Copy, scale=1.0 / 64), [gs], [gs])
                        self.op("dve", lambda h, y3=y3, C=C: h.tensor_tensor(y3, y3, gs.t[0:C, 0:2].unsqueeze(2).to_broadcast([C, 2, 64]), ALU.subtract), [ytm, gs], [ytm])
                        t3 = self._dtmp.t[0:C, 0:128].rearrange("p (h v) -> p h v", h=2)
                        self.op("act", lambda h, y3=y3, t3=t3: h.activation(t3, y3, AF.Square), [ytm], [self._dtmp])
                        self.op("dve", lambda h, t3=t3, C=C: h.reduce_sum(gs.t[0:C, 2:4], t3, AX.X), [self._dtmp], [gs])
                        self.op("act", lambda h, C=C: h.activation(gs.t[0:C, 2:4], gs.t[0:C, 2:4], AF.Sqrt, bias=64e-5, scale=1.0 / 64), [gs], [gs])
                        self.op("dve", lambda h, C=C: h.reciprocal(gs.t[0:C, 2:4], gs.t[0:C, 2:4]), [gs], [gs])
                        self.op("dve", lambda h, y3=y3, C=C: h.tensor_tensor(y3, y3, gs.t[0:C, 2:4].unsqueeze(2).to_broadcast([C, 2, 64]), ALU.mult), [ytm, gs], [ytm])
                        q = self.next_pf()
                        self.op("pe", lambda h, q=q, C=C: h.transpose(q.t[:, 0:C], ytm.t[0:C, :], self.identf.t[0:C, 0:C]), [ytm, self.identf], [q])
                        self.op("dve", lambda h, q=q, cs=cs, C=C: h.tensor_scalar(t1.t[:, cs], q.t[:, 0:C], P(8), P(9), ALU.mult, ALU.add), [q, pa], [t1])
                        self.op("dve", lambda h, cs=cs: h.tensor_tensor(t1.t[:, cs], t1.t[:, cs], bon.t[:, cs], ALU.add), [t1, bon], [t1])
                        oc = oc0 + ch
                        self.op("dve", lambda h, cs=cs, oc=oc, C=C: h.tensor_tensor(opo.t[:, oc:oc + C], t1.t[:, cs], g.t[:, cs], ALU.mult), [t1, g], [opo])
                    q = self.next_pf()
                    self.op("pe", lambda h, q=q: h.transpose(q.t[0:64, 0:128], Tst.t[:, :], self.identf.t[:, :]), [Tst, self.identf], [q])
                    self.copy("dve", wout.t[:, :], q.t[0:64, 0:128], [q], [wout])
                    self.dma("sp", o_wkv.t[si, 2 * hp:2 * hp + 2].rearrange("h v k -> v h k"), wout.t[:].rearrange("v (h k) -> v h k", h=2), [wout], [o_wkv])
                self.dma("sp", self.opT0.t[hp * 128:(hp + 1) * 128, :], opo.t[:], [opo], [self.opT0])

            for hp in range(12):
                head_pair(hp)

    def phase_memattn(self, li, qsrc, qc0, segs, opT, row0):
        cmem = self.ins["cmem"] if "cmem" in self.ins else self.inp("cmem", [2, 2, NMEM, 2, 2, 256])
        gq_in = self.ins["g_qmem"] if "g_qmem" in self.ins else self.inp("g_qmem", [128, 2, 2])
        with self.phase() as st:
            gq = self.sb(st, "ma_gq", [128, 2, 2], F32)
            self.dma("sp", gq.t[:], gq_in.t, [gq_in], [gq])
            kf = self.sb(st, "ma_kf", [128, 2, 256], F32)
            vf = self.sb(st, "ma_vf", [128, 2, 256], F32)
            kb_ = self.sb(st, "ma_kb", [128, 2, 256], BF16)
            vb_ = self.sb(st, "ma_vb", [128, 2, 256], BF16)
            KT = self.sb(st, "ma_KT", [128, 2, 256], BF16)
            qf = self.sb(st, "ma_qf", [128, 2, 512], F32)
            sq = self.sb(st, "ma_sq", [128, 2, 512], F32)
            rs = self.sb(st, "ma_rs", [128, 512], F32)
            qn = self.sb(st, "ma_qn", [128, 2, 512], BF16)
            E = self.sb(st, "ma_E", [128, 2, 512], BF16)
            rd = self.sb(st, "ma_rd", [128, 512], F32)
            ost = [self.sb(st, "ma_o%d" % i, [128, 512], BF16) for i in range(2)]
            self._oi = 0

            def block(hl, c0, n, d0):
                for dc in range(2):
                    self.dma("sp", qf.t[:, dc, 0:n], qsrc.t[qc0 + 2 * hl + dc, :, c0:c0 + n], [qsrc], [qf])
                self.op("act", lambda h: h.activation(sq.t[:, :, 0:n], qf.t[:, :, 0:n], AF.Square), [qf], [sq])
                ps = self.next_pf()
                for dc in range(2):
                    self.op("pe", lambda h, dc=dc: h.matmul(ps.t[:, 0:n], self.ones_f.t[:, :], sq.t[:, dc, 0:n], start=(dc == 0), stop=(dc == 1)), [self.ones_f, sq], [ps])
                self.op("act", lambda h: h.activation(rs.t[:, 0:n], ps.t[:, 0:n], AF.Sqrt, bias=EPS, scale=1.0 / 256), [ps], [rs])
                self.op("dve", lambda h: h.reciprocal(rs.t[:, 0:n], rs.t[:, 0:n]), [rs], [rs])
                for dc in range(2):
                    self.op("dve", lambda h, dc=dc: h.scalar_tensor_tensor(qn.t[:, dc, 0:n], qf.t[:, dc, 0:n], gq.t[:, li, dc:dc + 1], rs.t[:, 0:n], ALU.mult, ALU.mult), [qf, gq, rs], [qn])
                for mb in range(2):
                    pss = self.next_pf()
                    for dc in range(2):
                        self.op("pe", lambda h, mb=mb, dc=dc, pss=pss: h.matmul(pss.t[:, 0:n], KT.t[:, dc, mb * 128:(mb + 1) * 128], qn.t[:, dc, 0:n], start=(dc == 0), stop=(dc == 1)), [KT, qn], [pss])
                    self.op("act", lambda h, mb=mb, pss=pss: h.activation(E.t[:, mb, 0:n], pss.t[:, 0:n], AF.Exp, scale=1.0 / 16), [pss], [E])
                pd = self.next_pf()
                for mb in range(2):
                    self.op("pe", lambda h, mb=mb: h.matmul(pd.t[:, 0:n], self.ones_b.t[:, :], E.t[:, mb, 0:n], start=(mb == 0), stop=(mb == 1)), [self.ones_b, E], [pd])
                self.op("dve", lambda h: h.reciprocal(rd.t[:, 0:n], pd.t[:, 0:n]), [pd], [rd])
                for dc2 in range(2):
                    pn = self.next_pf()
                    for mb in range(2):
                        self.op("pe", lambda h, mb=mb, dc2=dc2, pn=pn: h.matmul(pn.t[:, 0:n], vb_.t[:, mb, dc2 * 128:(dc2 + 1) * 128], E.t[:, mb, 0:n], start=(mb == 0), stop=(mb == 1)), [vb_, E], [pn])
                    o = ost[self._oi % 2]
                    self._oi += 1
                    self.op("dve", lambda h, pn=pn, o=o: h.tensor_tensor(o.t[:, 0:n], pn.t[:, 0:n], rd.t[:, 0:n], ALU.mult), [pn, rd], [o])
                    r0 = row0 + hl * 256 + dc2 * 128
                    self.dma("sp", opT.t[r0:r0 + 128, d0:d0 + n], o.t[:, 0:n], [o], [opT])

            def kvset(kvi, hl):
                if kvi == 0:
                    ksrc, vsrc, srcT = self.mkz.t[li, :, 0, hl * 256:(hl + 1) * 256], self.mkz.t[li, :, 1, hl * 256:(hl + 1) * 256], self.mkz
                else:
                    ksrc, vsrc, srcT = cmem.t[li, kvi - 1, :, 0, hl, :], cmem.t[li, kvi - 1, :, 1, hl, :], cmem
                self.dma("sp", kf.t[:], ksrc.rearrange("(mb p) d -> p mb d", p=128), [srcT], [kf])
                self.dma("sp", vf.t[:], vsrc.rearrange("(mb p) d -> p mb d", p=128), [srcT], [vf])
                self.copy("act", kb_.t[:], kf.t[:], [kf], [kb_])
                self.copy("dve", vb_.t[:], vf.t[:], [vf], [vb_])
                pb = self.next_pb()
                for dc in range(2):
                    for mb in range(2):
                        self.op("pe", lambda h, dc=dc, mb=mb: h.transpose(pb.t[:, (dc * 2 + mb) * 128:(dc * 2 + mb + 1) * 128], kb_.t[:, mb, dc * 128:(dc + 1) * 128], self.identb.t[:, :]), [kb_, self.identb], [pb])
                self.copy("dve", KT.t[:].rearrange("p a b -> p (a b)"), pb.t[:, 0:512], [pb], [KT])
                for (c0, n, kv_i, d0) in segs:
                    if kv_i != kvi:
                        continue
                    for o in range(0, n, 512):
                        block(hl, c0 + o, min(512, n - o), d0 + o)

            for kvi in range(3):
                for hl in range(2):
                    kvset(kvi, hl)

    def load_own(self, x, tmp, src, nk, t0, n, srcT):
        ca, cb = (t0, HALF + t0) if t0 < HALF else (2 * HALF, 2 * HALF + NS)
        self.dma("sp", x.t[:, 0:nk, 0:n], src.t[:, ca:ca + n].rearrange("(k p) t -> p k t", p=128), [srcT], [x])
        self.dma("pool", tmp.t[:, 0:nk, 0:n], src.t[:, cb:cb + n].rearrange("(k p) t -> p k t", p=128), [srcT], [tmp])
        self.op("dve", lambda h: h.tensor_scalar(x.t[:, 0:nk, 0:n], x.t[:, 0:nk, 0:n], self.sel.t[:, 0:1], None, ALU.mult), [x, self.sel], [x])
        self.op("dve", lambda h: h.scalar_tensor_tensor(x.t[:, 0:nk, 0:n], tmp.t[:, 0:nk, 0:n], self.sel.t[:, 1:2], x.t[:, 0:nk, 0:n], ALU.mult, ALU.add), [x, tmp, self.sel], [x])

    def phase_outproj(self, opT, nrows, wname, xin, xout, tag):
        opP = self.dram(tag + "_opP", [2 * nrows, 2 * NTOK], BF16)
        self.pair_gather(opT, opP, nrows, 2 * NTOK, 2)
        nk = 2 * nrows // 128
        with self.phase() as st:
            xa = [self.sb(st, tag + "_x%d" % i, [128, nk, 512], BF16) for i in range(2)]
            xt = self.sb(st, tag + "_xt", [128, nk, 512], BF16)
            wbufs = [self.sb(st, tag + "_w%d" % i, [128, nk, 512], BF16) for i in range(2)]
            res = [self.sb(st, tag + "_r%d" % i, [128, 512], F32) for i in range(3)]
            xr = [self.sb(st, tag + "_xr%d" % i, [128, 512], F32) for i in range(3)]
            self._ri = 0
            for bi, (t0, n) in enumerate(TBLK):
                x = xa[bi % 2]
                self.load_own(x, xt, opP, nk, t0, n, opP)

                def sink(ci, pf, m, t0=t0, n=n):
                    r, xx = res[self._ri % 3], xr[self._ri % 3]
                    self._ri += 1
                    self.dma("pool", xx.t[:, 0:n], xin.t[:, ci, t0:t0 + n], [xin], [xx])
                    self.op("dve", lambda h: h.tensor_tensor(r.t[:, 0:n], pf.t[:, 0:n], xx.t[:, 0:n], ALU.add), [pf, xx], [r])
                    self.dma("sp", xout.t[:, ci, t0:t0 + n], r.t[:, 0:n], [r], [xout])
                self.gemm_fm(st, x, n, wname, [(i * 512, [128] * 4) for i in range(8)], nk, sink, tag, wbufs=wbufs)

    def phase_ffn(self, li, xin, xout):
        g_in = self.ins["g_ffn"] if "g_ffn" in self.ins else self.inp("g_ffn", [2, 128, KC])
        NJ = DFF // 128
        TB = [(0, 256), (256, 256), (512, 256), (768, 256), (1024, NS)]
        with self.phase() as st:
            gain = self.sb(st, "ff_g", [128, KC], F32)
            self.dma("sp", gain.t[:], g_in.t[li], [g_in], [gain])
            x = self.sb(st, "ff_x", [128, KC, 256], F32)
            hT = self.sb(st, "ff_h", [128, KC, 256], BF16)
            aT = self.sb(st, "ff_a", [128, NJ, 256], BF16)
            sq = [self.sb(st, "ff_sq%d" % i, [128, 256], F32) for i in range(2)]
            rs = self.sb(st, "ff_rs", [128, 256], F32)
            sg = [self.sb(st, "ff_sg%d" % i, [128, 256], F32) for i in range(2)]
            res = [self.sb(st, "ff_r%d" % i, [128, 256], F32) for i in range(3)]
            w1 = [self.sb(st, "ff_w1%d" % i, [128, KC, 256], BF16) for i in range(2)]
            w2 = [self.sb(st, "ff_w2%d" % i, [128, NJ, 128], BF16) for i in range(2)]
            self._ri = 0
            for (t0, n) in TB:
                self.dma("sp", x.t[:, :, 0:n], xin.t[:, :, t0:t0 + n], [xin], [x])
                ps = self.next_pf()
                for k in range(KC):
                    q = sq[k % 2]
                    self.op("act", lambda h, k=k, q=q, n=n: h.activation(q.t[:, 0:n], x.t[:, k, 0:n], AF.Square), [x], [q])
                    self.op("pe", lambda h, k=k, q=q, n=n, ps=ps: h.matmul(ps.t[:, 0:n], self.ones_f.t[:, :], q.t[:, 0:n], start=(k == 0), stop=(k == KC - 1)), [self.ones_f, q], [ps])
                self.op("act", lambda h, n=n, ps=ps: h.activation(rs.t[:, 0:n], ps.t[:, 0:n], AF.Sqrt, bias=EPS, scale=1.0 / D), [ps], [rs])
                self.op("dve", lambda h, n=n: h.reciprocal(rs.t[:, 0:n], rs.t[:, 0:n]), [rs], [rs])
                for k in range(KC):
                    self.op("dve", lambda h, k=k, n=n: h.scalar_tensor_tensor(hT.t[:, k, 0:n], x.t[:, k, 0:n], gain.t[:, k:k + 1], rs.t[:, 0:n], ALU.mult, ALU.mult), [x, gain, rs], [hT])

                def sink1(ci, pf, m, n=n):
                    j, up = ci // 2, ci % 2
                    s_ = sg[j % 2]
                    if not up:
                        self.op("act", lambda h: h.activation(s_.t[:, 0:n], pf.t[:, 0:n], AF.Silu), [pf], [s_])
                    else:
                        self.op("dve", lambda h: h.tensor_tensor(aT.t[:, j, 0:n], s_.t[:, 0:n], pf.t[:, 0:n], ALU.mult), [s_, pf], [aT])
                self.gemm_fm(st, hT, n, "w_fi%d" % li, [(j * 256, [128, 128]) for j in range(NJ)], KC, sink1, "ff1", wbufs=w1, pm=True)

                def sink2(ci, pf, m, t0=t0, n=n):
                    r = res[self._ri % 3]
                    self._ri += 1
                    self.op("dve", lambda h: h.tensor_tensor(r.t[:, 0:n], pf.t[:, 0:n], x.t[:, ci, 0:n], ALU.add), [pf, x], [r])
                    self.dma("sp", xout.t[:, ci, t0:t0 + n], r.t[:, 0:n], [r], [xout])
                self.gemm_fm(st, aT, n, "w_fo%d" % li, [(i * 128, [128]) for i in range(KC)], NJ, sink2, "ff2", wbufs=w2, pm=True)

    def phase_l1_prep(self, xin):
        TB = [(0, 256), (256, 256), (512, 256), (768, 256), (1024, NS)]
        with self.phase() as st:
            gain = self.sb(st, "p1_g", [128, KC], F32)
            self.dma("sp", gain.t[:], self.g_mix.t[1], [self.g_mix], [gain])
            x = self.sb(st, "p1_x", [128, KC, 256], F32)
            hTs = [self.sb(st, "p1_h%d" % i, [128, KC, 256], BF16) for i in range(2)]
            sq = [self.sb(st, "p1_sq%d" % i, [128, 256], F32) for i in range(2)]
            rs = self.sb(st, "p1_rs", [128, 256], F32)
            for bi, (t0, n) in enumerate(TB):
                hT = hTs[bi % 2]
                self.dma("sp", x.t[:, :, 0:n], xin.t[:, :, t0:t0 + n], [xin], [x])
                ps = self.next_pf()
                for k in range(KC):
                    q = sq[k % 2]
                    self.op("act", lambda h, k=k, q=q, n=n: h.activation(q.t[:, 0:n], x.t[:, k, 0:n], AF.Square), [x], [q])
                    self.op("pe", lambda h, k=k, q=q, n=n, ps=ps: h.matmul(ps.t[:, 0:n], self.ones_f.t[:, :], q.t[:, 0:n], start=(k == 0), stop=(k == KC - 1)), [self.ones_f, q], [ps])
                self.op("act", lambda h, n=n, ps=ps: h.activation(rs.t[:, 0:n], ps.t[:, 0:n], AF.Sqrt, bias=EPS, scale=1.0 / D), [ps], [rs])
                self.op("dve", lambda h, n=n: h.reciprocal(rs.t[:, 0:n], rs.t[:, 0:n]), [rs], [rs])
                for k in range(KC):
                    self.op("dve", lambda h, k=k, n=n, hT=hT: h.scalar_tensor_tensor(hT.t[:, k, 0:n], x.t[:, k, 0:n], gain.t[:, k:k + 1], rs.t[:, 0:n], ALU.mult, ALU.mult), [x, gain, rs], [hT])
                self.dma("sp", self.hT_own.t[:, t0:t0 + n].rearrange("(k p) t -> p k t", p=128), hT.t[:, :, 0:n], [hT], [self.hT_own])
        self.hT_pieces = self.pair_gather(self.hT_own, self.hT_pair, D, NTOK, 2)

    def phase_l1_inproj(self):
        g_in = self.inp("g_qkb", [128, 2])
        NT2 = 2 * NTOK
        self.QT = self.dram("QT", [12, 128, NT2], BF16)
        self.KTb = self.dram("KTb", [12, 128, NT2], BF16)
        self.KTf = self.dram("KTf", [12, 128, NT2], F32)
        self.Vtm = self.dram("Vtm", [3, NT2, 512], F32)
        self.U1m = self.dram("U1m", [4, 128, NT2], F32)
        blocks = [(0, 0, 512, 0), (0, 512, 512, 512), (1, 0, 512, 1024), (1, 512, 512, 1536)]
        with self.phase() as st:
            gqk = self.sb(st, "i1_g", [128, 2], F32)
            self.dma("sp", gqk.t[:], g_in.t, [g_in], [gqk])
            xT = [self.sb(st, "i1_x%d" % i, [128, KC, 512], BF16) for i in range(2)]
            wbufs = [self.sb(st, "i1_w%d" % i, [128, KC, 512], BF16) for i in range(2)]
            sq = [self.sb(st, "i1_sq%d" % i, [128, 512], F32) for i in range(2)]
            rs = [self.sb(st, "i1_rs%d" % i, [128, 512], F32) for i in range(2)]
            o16 = [self.sb(st, "i1_ob%d" % i, [128, 512], BF16) for i in range(3)]
            o32 = [self.sb(st, "i1_of%d" % i, [128, 512], F32) for i in range(3)]
            self._ci = 0

            def normed(qk, hd, pf, ntok, c0):
                i_ = self._ci
                self._ci += 1
                s_, r_ = sq[i_ % 2], rs[i_ % 2]
                self.op("act", lambda h: h.activation(s_.t[:, 0:ntok], pf.t[:, 0:ntok], AF.Square), [pf], [s_])
                ps = self.next_pf()
                self.op("pe", lambda h: h.matmul(ps.t[:, 0:ntok], self.ones_f.t[:, :], s_.t[:, 0:ntok], start=True, stop=True), [self.ones_f, s_], [ps])
                self.op("act", lambda h: h.activation(r_.t[:, 0:ntok], ps.t[:, 0:ntok], AF.Sqrt, bias=EPS, scale=1.0 / 128), [ps], [r_])
                self.op("dve", lambda h: h.reciprocal(r_.t[:, 0:ntok], r_.t[:, 0:ntok]), [r_], [r_])
                ob = o16[i_ % 3]
                if qk == 0:
                    self.op("dve", lambda h: h.scalar_tensor_tensor(ob.t[:, 0:ntok], pf.t[:, 0:ntok], gqk.t[:, 0:1], r_.t[:, 0:ntok], ALU.mult, ALU.mult), [pf, gqk, r_], [ob])
                    self.dma("sp", self.QT.t[hd, :, c0:c0 + ntok], ob.t[:, 0:ntok], [ob], [self.QT])
                else:
                    of = o32[i_ % 3]
                    self.op("dve", lambda h: h.scalar_tensor_tensor(of.t[:, 0:ntok], pf.t[:, 0:ntok], gqk.t[:, 1:2], r_.t[:, 0:ntok], ALU.mult, ALU.mult), [pf, gqk, r_], [of])
                    self.dma("sp", self.KTf.t[hd, :, c0:c0 + ntok], of.t[:, 0:ntok], [of], [self.KTf])
                    self.copy("act", ob.t[:, 0:ntok], of.t[:, 0:ntok], [of], [ob])
                    self.dma("sp", self.KTb.t[hd, :, c0:c0 + ntok], ob.t[:, 0:ntok], [ob], [self.KTb])

            def run_block(bi, loads, ntok, c0):
                x = xT[bi % 2]
                for (dst0, r, t0, n) in loads:
                    for (r0, r1) in self.hT_pieces:
                        base = 2 * r0 + r * (r1 - r0)
                        src = self.hT_pair.t[base:base + (r1 - r0), t0:t0 + n].rearrange("(k p) t -> p k t", p=128)
                        self.dma("sp", x.t[:, r0 // 128:r1 // 128, dst0:dst0 + n], src, [self.hT_pair], [x])
                for pi in range(10):
                    w = wbufs[pi % 2]
                    self.load_wpanel("sp", w, "w_in_b", KC, pi * 512, 512)
                    if 6 <= pi < 9:
                        g = pi - 6
                        for tb in range(0, ntok, 128):
                            nt = min(128, ntok - tb)
                            pf = self.next_pf()
                            for k in range(KC):
                                self.op("pe", lambda h, pf=pf, k=k, w=w, tb=tb, nt=nt: h.matmul(pf.t[0:nt, 0:512], x.t[:, k, tb:tb + nt], w.t[:, k, 0:512], start=(k == 0), stop=(k == KC - 1)), [x, w], [pf])
                            of = o32[self._ci % 3]
                            self._ci += 1
                            self.copy(self.evac_engine(), of.t[0:nt, :], pf.t[0:nt, 0:512], [pf], [of])
                            self.dma("sp", self.Vtm.t[g, c0 + tb:c0 + tb + nt, :], of.t[0:nt, :], [of], [self.Vtm])
                    else:
                        for m in range(4):
                            pf = self.next_pf()
                            for k in range(KC):
                                self.op("pe", lambda h, pf=pf, k=k, w=w, m=m: h.matmul(pf.t[:, 0:ntok], w.t[:, k, m * 128:(m + 1) * 128], x.t[:, k, 0:ntok], start=(k == 0), stop=(k == KC - 1)), [x, w], [pf])
                            if pi == 9:
                                of = o32[self._ci % 3]
                                self._ci += 1
                                self.copy(self.evac_engine(), of.t[:, 0:ntok], pf.t[:, 0:ntok], [pf], [of])
                                self.dma("sp", self.U1m.t[m, :, c0:c0 + ntok], of.t[:, 0:ntok], [of], [self.U1m])
                            else:
                                normed(0 if pi < 3 else 1, (pi % 3) * 4 + m, pf, ntok, c0)

            for bi, (r, t0, n, c0) in enumerate(blocks):
                run_block(bi, [(0, r, t0, n)], n, c0)
            run_block(4, [(0, 0, HALF, NS), (NS, 1, HALF, NS)], 2 * NS, 2048)

    def phase_dswa_prompt(self):
        cm_in = self.inp("c_amask", [128, 256])
        self.opT1 = self.dram("opT1", [1024, 2 * NTOK], BF16)
        SC = 128 ** -0.5
        with self.phase() as st:
            mk = self.sb(st, "dp_mk", [128, 256], BF16)
            self.dma("pool", mk.t[:], cm_in.t, [cm_in], [mk])
            NUM = self.sb(st, "dp_num", [128, 2048], F32)
            DEN = self.sb(st, "dp_den", [128, 2048], F32)
            qts = [self.sb(st, "dp_q%d" % i, [128, 2048], BF16) for i in range(2)]
            kts = [self.sb(st, "dp_k%d" % i, [128, 2048], BF16) for i in range(2)]
            vf = self.sb(st, "dp_vf", [128, 16, 128], F32)
            vbs = [self.sb(st, "dp_v%d" % i, [128, 16, 128], BF16) for i in range(2)]
            ers = [self.sb(st, "dp_er%d" % i, [128, 256], BF16) for i in range(2)]
            Es = [self.sb(st, "dp_E%d" % i, [128, 256], BF16) for i in range(2)]
            ob = self.sb(st, "dp_ob", [128, 2048], BF16)
            rd = self.sb(st, "dp_rd", [128, 2048], F32)
            self._ci = 0

            def block(g, q_, k_, v_, dil, nb, r, n):
                qv = q_.t[:, :].rearrange("p (n j r) -> p r n j", j=128, r=dil)
                kv = k_.t[:, :].rearrange("p (n j r) -> p r n j", j=128, r=dil)
                NUMv = NUM.t[:, :].rearrange("p (n j r) -> p r n j", j=128, r=dil)[:, r, n, :]
                DENv = DEN.t[:, :].rearrange("p (n j r) -> p r n j", j=128, r=dil)[:, r, n, :]
                er, E = ers[self._ci % 2], Es[self._ci % 2]
                self._ci += 1
                w = 256 if n > 0 else 128
                ps = self.next_pf()
                self.op("pe", lambda h: h.matmul(ps.t[:, 0:128], kv[:, r, n, :], qv[:, r, n, :], start=True, stop=True), [k_, q_], [ps])
                if n > 0:
                    self.op("pe", lambda h: h.matmul(ps.t[:, 128:256], kv[:, r, n - 1, :], qv[:, r, n, :], start=True, stop=True), [k_, q_], [ps])
                self.op("act", lambda h: h.activation(er.t[:, 0:w], ps.t[:, 0:w], AF.Exp, scale=SC), [ps], [er])
                self.op("dve", lambda h: h.tensor_tensor(E.t[:, 0:w], er.t[:, 0:w], mk.t[:, 0:w], ALU.mult), [er, mk], [E])
                pn, pd = self.next_pf(), self.next_pf()
                bi = r * nb + n
                self.op("pe", lambda h: h.matmul(pn.t[:, 0:128], v_.t[:, bi, :], E.t[:, 0:128], start=True, stop=(n == 0)), [v_, E], [pn])
                if n > 0:
                    self.op("pe", lambda h: h.matmul(pn.t[:, 0:128], v_.t[:, bi - 1, :], E.t[:, 128:256], start=False, stop=True), [v_, E], [pn])
                self.op("pe", lambda h: h.matmul(pd.t[:, 0:128], self.ones_b.t[:, :], E.t[:, 0:128], start=True, stop=(n == 0)), [self.ones_b, E], [pd])
                if n > 0:
                    self.op("pe", lambda h: h.matmul(pd.t[:, 0:128], self.ones_b.t[:, :], E.t[:, 128:256], start=False, stop=True), [self.ones_b, E], [pd])
                if g == 0:
                    self.op("act", lambda h: h.activation(NUMv, pn.t[:, 0:128], AF.Copy), [pn], [NUM])
                    self.op("dve", lambda h: h.tensor_copy(DENv, pd.t[:, 0:128]), [pd], [DEN])
                else:
                    self.op("dve", lambda h: h.tensor_tensor(NUMv, NUMv, pn.t[:, 0:128], ALU.add), [pn, NUM], [NUM])
                    self.op("dve", lambda h: h.tensor_tensor(DENv, DENv, pd.t[:, 0:128], ALU.add), [pd, DEN], [DEN])

            def head(i, g, cnt):
                hd = g * 4 + i
                dil = (1, 4, 16)[g]
                nb = 16 // dil
                q_, k_, v_ = qts[cnt % 2], kts[cnt % 2], vbs[cnt % 2]
                self.dma("sp", q_.t[:, :], self.QT.t[hd, :, 0:2048], [self.QT], [q_])
                self.dma("sp", k_.t[:, :], self.KTb.t[hd, :, 0:2048], [self.KTb], [k_])
                for r in range(dil):
                    src = self.Vtm.t[g, 0:2048, i * 128:(i + 1) * 128].rearrange("(n j r) d -> r j n d", j=128, r=dil)[r]
                    self.dma("sp", vf.t[:, r * nb:(r + 1) * nb, :], src, [self.Vtm], [vf])
                self.copy("act", v_.t[:], vf.t[:], [vf], [v_])
                for r in range(dil):
                    for n in range(nb):
                        block(g, q_, k_, v_, dil, nb, r, n)

            cnt = 0
            for i in range(4):
                for g in range(3):
                    head(i, g, cnt)
                    cnt += 1
                self.op("dve", lambda h: h.reciprocal(rd.t[:], DEN.t[:]), [DEN], [rd])
                self.op("dve", lambda h: h.tensor_tensor(ob.t[:], NUM.t[:], rd.t[:], ALU.mult), [NUM, rd], [ob])
                self.dma("sp", self.opT1.t[i * 128:(i + 1) * 128, 0:2048], ob.t[:], [ob], [self.opT1])

    def phase_kv_out_prompt(self):
        KEEP = (128, 512, 2048)
        outs = [self.outp("o_kvp%d" % g, [KEEP[g], 2, 4, 128]) for g in range(3)]
        with self.phase() as st:
            kf = self.sb(st, "ko_kf", [128, 2048], F32)
            ots = [self.sb(st, "ko_o%d" % i, [128, 4, 128], F32) for i in range(2)]
            cnt = 0
            for g in range(3):
                keep = KEEP[g]
                for r0 in range(0, keep, 512):
                    r1 = min(keep, r0 + 512)
                    self.dma("sp", outs[g].t[r0:r1, 1, :, :], self.Vtm.t[g, 2048 - keep + r0:2048 - keep + r1, :].rearrange("t (h d) -> t h d", d=128), [self.Vtm], [outs[g]])
                for i in range(4):
                    hd = g * 4 + i
                    self.dma("sp", kf.t[:, 0:keep], self.KTf.t[hd, :, 2048 - keep:2048], [self.KTf], [kf])
                    for tb in range(0, keep, 512):
                        nb_ = min(4, (keep - tb) // 128)
                        pf = self.next_pf()
                        for j in range(nb_):
                            self.op("pe", lambda h, pf=pf, j=j, tb=tb: h.transpose(pf.t[:, j * 128:(j + 1) * 128], kf.t[:, tb + j * 128:tb + (j + 1) * 128], self.identf.t[:, :]), [kf, self.identf], [pf])
                        o = ots[cnt % 2]
                        cnt += 1
                        self.copy(self.evac_engine(), o.t[:, 0:nb_, :], pf.t[:, 0:nb_ * 128].rearrange("p (j d) -> p j d", d=128), [pf], [o])
                        self.dma("sp", outs[g].t[tb:tb + nb_ * 128, 0, i, :].rearrange("(j p) d -> p j d", p=128), o.t[:, 0:nb_, :], [o], [outs[g]])

    def phase_dswa_sample(self):
        ROWS = (128, 512, 2048)
        MOFF = (0, 1, 5)
        csw = [self.inp("cswa%d" % g, [2, ROWS[g], 2, 4, 128]) for g in range(3)]
        sm_in = self.inp("c_smask", [128, 24, 8])
        outs = [self.outp("o_kvs%d" % g, [2, ROWS[g], 2, 4, 128]) for g in range(3)]
        SC = 128 ** -0.5
        self.pf_lim = 4
        accn, accd = self.pf[4], self.pf[5]
        with self.phase() as st:
            sm = self.sb(st, "ds_sm", [128, 24, 8], BF16)
            self.dma("pool", sm.t[:], sm_in.t, [sm_in], [sm])
            kf = self.sb(st, "ds_kf", [128, 16, 128], F32)
            vf = self.sb(st, "ds_vf", [128, 16, 128], F32)
            kb_ = self.sb(st, "ds_kb", [128, 16, 128], BF16)
            vb_ = self.sb(st, "ds_vb", [128, 16, 128], BF16)
            kTs = [self.sb(st, "ds_kT%d" % i, [128, 128], BF16) for i in range(2)]
            q8 = self.sb(st, "ds_q8", [128, 8], BF16)
            k8 = self.sb(st, "ds_k8", [128, 8], BF16)
            k8f = self.sb(st, "ds_k8f", [128, 8], F32)
            v8f = self.sb(st, "ds_v8f", [8, 128], F32)
            v8 = self.sb(st, "ds_v8", [8, 128], BF16)
            ers = [self.sb(st, "ds_er%d" % i, [128, 8], BF16) for i in range(2)]
            Es = [self.sb(st, "ds_E%d" % i, [128, 8], BF16) for i in range(2)]
            rd = self.sb(st, "ds_rd", [128, 8], F32)
            ob = self.sb(st, "ds_ob", [128, 8], BF16)
            k8o = self.sb(st, "ds_k8o", [8, 128], F32)
            self._ci = 0

            def group(s, i, g):
                hd = g * 4 + i
                nblk = ROWS[g] // 128
                c8 = 2048 + 8 * s
                self.dma("sp", kf.t[:, 0:nblk, :], csw[g].t[s, :, 0, i, :].rearrange("(n p) d -> p n d", p=128), [csw[g]], [kf])
                self.dma("sp", vf.t[:, 0:nblk, :], csw[g].t[s, :, 1, i, :].rearrange("(n p) d -> p n d", p=128), [csw[g]], [vf])
                self.copy("act", kb_.t[:, 0:nblk, :], kf.t[:, 0:nblk, :], [kf], [kb_])
                self.copy("dve", vb_.t[:, 0:nblk, :], vf.t[:, 0:nblk, :], [vf], [vb_])
                self.dma("sp", q8.t[:], self.QT.t[hd, :, c8:c8 + 8], [self.QT], [q8])
                self.dma("sp", k8.t[:], self.KTb.t[hd, :, c8:c8 + 8], [self.KTb], [k8])
                self.dma("sp", k8f.t[:], self.KTf.t[hd, :, c8:c8 + 8], [self.KTf], [k8f])
                self.dma("sp", v8f.t[:], self.Vtm.t[g, c8:c8 + 8, i * 128:(i + 1) * 128], [self.Vtm], [v8f])
                self.copy("dve", v8.t[:], v8f.t[:], [v8f], [v8])
                for blk in range(nblk):
                    kT = kTs[self._ci % 2]
                    er, E = ers[self._ci % 2], Es[self._ci % 2]
                    self._ci += 1
                    first = (g == 0 and blk == 0)
                    pb = self.next_pb()
                    self.op("pe", lambda h, pb=pb, blk=blk: h.transpose(pb.t[:, 0:128], kb_.t[:, blk, :], self.identb.t[:, :]), [kb_, self.identb], [pb])
                    self.copy("dve", kT.t[:], pb.t[:, 0:128], [pb], [kT])
                    ps = self.next_pf()
                    self.op("pe", lambda h, ps=ps, kT=kT: h.matmul(ps.t[:, 0:8], kT.t[:, :], q8.t[:, :], start=True, stop=True), [kT, q8], [ps])
                    self.op("act", lambda h, ps=ps, er=er: h.activation(er.t[:], ps.t[:, 0:8], AF.Exp, scale=SC), [ps], [er])
                    self.op("dve", lambda h, er=er, E=E, blk=blk: h.tensor_tensor(E.t[:], er.t[:], sm.t[:, MOFF[g] + blk, :], ALU.mult), [er, sm], [E])
                    self.op("pe", lambda h, E=E, blk=blk, first=first: h.matmul(accn.t[:, 0:8], vb_.t[:, blk, :], E.t[:], start=first, stop=False), [vb_, E], [accn])
                    self.op("pe", lambda h, E=E, first=first: h.matmul(accd.t[:, 0:8], self.ones_b.t[:, :], E.t[:], start=first, stop=False), [self.ones_b, E], [accd])
                er, E = ers[self._ci % 2], Es[self._ci % 2]
                self._ci += 1
                last = (g == 2)
                ps = self.next_pf()
                self.op("pe", lambda h: h.matmul(ps.t[0:8, 0:8], k8.t[:, :], q8.t[:, :], start=True, stop=True), [k8, q8], [ps])
                self.op("act", lambda h: h.activation(er.t[0:8, :], ps.t[0:8, 0:8], AF.Exp, scale=SC), [ps], [er])
                self.op("dve", lambda h: h.tensor_tensor(E.t[0:8, :], er.t[0:8, :], sm.t[0:8, 21 + g, :], ALU.mult), [er, sm], [E])
                self.op("pe", lambda h: h.matmul(accn.t[:, 0:8], v8.t[0:8, :], E.t[0:8, :], start=False, stop=last), [v8, E], [accn])
                self.op("pe", lambda h: h.matmul(accd.t[:, 0:8], self.ones_b.t[0:8, :], E.t[0:8, :], start=False, stop=last), [self.ones_b, E], [accd])
                rows = ROWS[g]
                pt = self.next_pf()
                self.op("pe", lambda h: h.transpose(pt.t[0:8, 0:128], k8f.t[:, 0:8], self.identf.t[:, :]), [k8f, self.identf], [pt])
                self.copy("dve", k8o.t[:], pt.t[0:8, 0:128], [pt], [k8o])
                self.dma("sp", outs[g].t[s, rows - 8:rows, 0, i, :], k8o.t[:], [k8o], [outs[g]])

            for s in range(2):
                for g in range(3):
                    rows = ROWS[g]
                    for r0 in range(8, rows, 512):
                        if "ds_nocopy" in self.phases:
                            break
                        r1 = min(rows, r0 + 512)
                        self.dma("sp", outs[g].t[s, r0 - 8:r1 - 8], csw[g].t[s, r0:r1], [csw[g]], [outs[g]])
                    c8 = 2048 + 8 * s
                    self.dma("sp", outs[g].t[s, rows - 8:rows, 1, :, :], self.Vtm.t[g, c8:c8 + 8, :].rearrange("t (h d) -> t h d", d=128), [self.Vtm], [outs[g]])
                for i in range(4):
                    if "ds_noattn" in self.phases:
                        break
                    for g in range(3):
                        group(s, i, g)
                    self.op("dve", lambda h: h.reciprocal(rd.t[:], accd.t[:, 0:8]), [accd], [rd])
                    self.op("dve", lambda h: h.tensor_tensor(ob.t[:], accn.t[:, 0:8], rd.t[:], ALU.mult), [accn, rd], [ob])
                    c8 = 2048 + 8 * s
                    self.dma("sp", self.opT1.t[i * 128:(i + 1) * 128, c8:c8 + 8], ob.t[:], [ob], [self.opT1])
        self.pf_lim = 6

    def phase_final(self, xin):
        o_y = self.outp("o_y", [NTOK, D])
        with self.phase() as st:
            xs = [self.sb(st, "fy_x%d" % i, [128, KC, 128], F32) for i in range(2)]
            yts = [self.sb(st, "fy_y%d" % i, [128, D], F32) for i in range(2)]
            for bi, t0 in enumerate(range(0, NTOK, 128)):
                n = min(128, NTOK - t0)
                x, yt = xs[bi % 2], yts[bi % 2]
                self.dma("sp", x.t[:, :, 0:n], xin.t[:, :, t0:t0 + n], [xin], [x])
                for k0 in range(0, KC, 4):
                    pf = self.next_pf()
                    for j in range(4):
                        self.op("pe", lambda h, pf=pf, j=j, k0=k0, n=n, x=x: h.transpose(pf.t[0:n, j * 128:(j + 1) * 128], x.t[:, k0 + j, 0:n], self.identf.t[:, :]), [x, self.identf], [pf])
                    self.copy(self.evac_engine(), yt.t[0:n, k0 * 128:(k0 + 4) * 128], pf.t[0:n, 0:512], [pf], [yt])
                self.dma("sp", o_y.t[t0:t0 + n, :], yt.t[0:n, :], [yt], [o_y])

    def build(self):
        ph = self.phases
        self.setup_common()
        nol0 = "nol0" in ph
        wn = []
        if not nol0:
            self.phase_weights([("w_mem0", (4096, 2048)), ("w_mem1", (4096, 2048))])
            wn = ["w_in_a"]
        if "l0out" in ph:
            wn += ["w_out_a"]
        if "ffn0" in ph:
            wn += ["w_fi0", "w_fo0"]
        if "l1" in ph:
            wn += ["w_in_b"]
        if "l1out" in ph:
            wn += ["w_out_b", "w_fi1", "w_fo1"]
        first, rest = ([w for w in wn if w == "w_in_a"], [w for w in wn if w != "w_in_a"]) if not nol0 else (wn, [])
        self.phase_weights2(first)
        self.kb.barrier()
        if not nol0:
            self.phase_memkv()
            self.phase_l0_prep()
            self.phase_l0_inproj()
            self.phase_weights2(rest)
            self.phase_rwkv()
        else:
            self.g_mix = self.inp("g_mix", [2, 128, KC])
            self.XT = [self.dram("XT%d" % i, [128, KC, NTOK], F32) for i in range(5)]
            self.hT_own = self.dram("hT_own", [D, NTOK], BF16)
            self.hT_pair = self.dram("hT_pair", [2 * D, NTOK], BF16)
        if "l0out" in ph:
            SEG = [(1, 2048, 0, 0), (2050, 8, 1, 2048), (2059, 8, 2, 2056)]
            self.phase_memattn(0, self.U0, 41, SEG, self.opT0, 1536)
            self.phase_outproj(self.opT0, 2048, "w_out_a", self.XT[0], self.XT[1], "o0")
        if "ffn0" in ph:
            self.phase_ffn(0, self.XT[1], self.XT[2])
        if "l1" in ph:
            self.phase_l1_prep(self.XT[2])
            self.phase_l1_inproj()
            if "skipkvo" not in ph:
                self.phase_kv_out_prompt()
            if "skipdp" not in ph:
                self.phase_dswa_prompt()
            else:
                self.opT1 = self.dram("opT1", [1024, 2 * NTOK], BF16)
            if "skipds" not in ph:
                self.phase_dswa_sample()
        if "l1out" in ph:
            SEG = [(0, 2048, 0, 0), (2048, 8, 1, 2048), (2056, 8, 2, 2056)]
            self.phase_memattn(1, self.U1m, 0, SEG, self.opT1, 512)
            self.phase_outproj(self.opT1, 1024, "w_out_b", self.XT[2], self.XT[3], "o1")
            self.phase_ffn(1, self.XT[3], self.XT[4])
            self.phase_final(self.XT[4])
        self.kb.finish([t.b for t in self.outs.values()] + [t.b for n, t in self.scr.items() if n in DBG_OUT])
        self.kb.replay()


def _wshard(w, G, c):
    K, N = w.shape
    rows = K // (8 * G)
    return np.ascontiguousarray(w.reshape(G, 8, rows, N)[:, c])


def _cz(c):
    return c % 4, c // 4


def _wsh2(w, name, c):
    K, N, ru, mode = WSPEC2[name]
    nr = 4 if mode == "q4" else 8
    sel = (c % 4) if mode == "q4" else UPERM[c]
    out = {}
    nu = K // ru
    out[name] = np.ascontiguousarray(w[:nu * ru].reshape(nu, nr, ru // nr, N)[:, sel])
    if K % ru:
        rt = K % ru
        out[name + "_t"] = np.ascontiguousarray(w[nu * ru:].reshape(1, nr, rt // nr, N)[:, sel])
    return out


def _pm_in(w):
    return np.ascontiguousarray(w.reshape(KC, 128, DFF // 128, 256).transpose(2, 1, 0, 3).reshape(DFF, KC * 256))


def _pm_out(w):
    return np.ascontiguousarray(w.reshape(DFF // 128, 128, KC, 128).transpose(2, 1, 0, 3).reshape(4096, DFF))


def _gather_order(nrows, ncols, esz):
    rp = max(128, ((2 << 20) // (ncols * esz)) // 128 * 128)
    rk, lr = [], []
    r0 = 0
    while r0 < nrows:
        r1 = min(nrows, r0 + rp)
        for r in range(2):
            rk.append(np.full(r1 - r0, r))
            lr.append(np.arange(r0, r1))
        r0 = r1
    return np.concatenate(rk), np.concatenate(lr)


def _cols_a(z):
    r = np.arange(z * 1536, (z + 1) * 1536)
    return np.concatenate([r, 3072 + r, 6144 + r, np.arange(9216, 9792), 9792 + np.arange(z * 512, (z + 1) * 512)])


def _cols_b(z):
    heads = np.array([g * 8 + 4 * z + i for g in range(3) for i in range(4)])
    hc = (heads[:, None] * 128 + np.arange(128)[None, :]).reshape(-1)
    return np.concatenate([hc, 3072 + hc, 6144 + hc, 9216 + np.arange(z * 512, (z + 1) * 512)])


def _smask():
    m = np.zeros((128, 24, 8), np.float32)
    t = np.arange(8)[None, :]
    off = 0
    for g, (win, dil) in enumerate(((128, 1), (512, 4), (2048, 16))):
        rows = win
        for blk in range(rows // 128):
            p = (blk * 128 + np.arange(128))[:, None]
            m[:, off + blk, :] = (((rows + t - p) % dil) == 0) & (p >= t + rows - win)
        off += rows // 128
        tp = np.arange(8)[:, None]
        m[0:8, 21 + g, :] = (((t - tp) % dil) == 0) & (tp <= t)
    return m


def _chunk_rows(v, z):
    cols = _cols_a(z)
    vv = v[cols] if v.shape[0] >= A_IN else np.concatenate([v, np.zeros(A_IN - v.shape[0], v.dtype)])[cols]
    out = np.zeros((45, 128), np.float32)
    out[0:36] = vv[0:4608].reshape(36, 128)
    out[36, :96] = vv[4608:4704]
    out[37, :96] = vv[4704:4800]
    out[38:41] = vv[4800:5184].reshape(3, 128)
    out[41:45] = vv[5184:5696].reshape(4, 128)
    return out


_WITH_RWKV = [False]


def _host_l0in(inp, c):
    b, z = _cz(c)
    m = {}
    m.update(_wsh2(np.ascontiguousarray(inp["w_in_a"][0][:, _cols_a(z)]), "w_in_a", c))
    m["x_own"] = np.ascontiguousarray(np.concatenate([inp["x_prompt"][b, z * HALF:(z + 1) * HALF], inp["x_sample"][2 * b + z]], axis=0))
    m["g_mix"] = np.ascontiguousarray(inp["norm_mix"].reshape(2, KC, 128).transpose(0, 2, 1))
    sh = np.stack([_chunk_rows(inp["state_shift"][0, 2 * b + j, 0], z) for j in range(2)], axis=-1)
    m["shcol"] = np.ascontiguousarray(sh.transpose(1, 0, 2))
    ch = slice(z * 1536, (z + 1) * 1536)
    f = lambda v: v[ch].reshape(12, 128).T
    mu = inp["mu_a"][0]
    pa = np.stack([f(mu[0:3072]), f(mu[3072:6144]), f(mu[6144:9216]), f(inp["w0_a"][0]), f(inp["a0_a"][0]), f(inp["kk_a"][0]),
                   f(inp["ka_a"][0]), f(inp["rk_a"][0].reshape(-1)), f(inp["lnx_g_a"][0]), f(inp["lnx_b_a"][0])], axis=1)
    m["pa"] = np.ascontiguousarray(pa.astype(np.float32))
    pl = np.zeros((128, 5), np.float32)
    pl[:96, 0] = mu[9216:9312]
    pl[:96, 1] = mu[9312:9408]
    pl[:, 2:5] = mu[9408:9792].reshape(3, 128).T
    m["pl"] = pl
    m["w2z"] = np.ascontiguousarray(inp["w2_a"][0][:, ch])
    m["a2z"] = np.ascontiguousarray(inp["a2_a"][0][:, ch])
    m["g2z"] = np.ascontiguousarray(inp["g2_a"][0][:, ch].reshape(3, 128, 1536).transpose(1, 0, 2))
    m["wkv_in"] = np.ascontiguousarray(inp["state_wkv"][0, 2 * b:2 * b + 2, z * HH:(z + 1) * HH])
    ii = np.arange(128)
    strict = (ii[:, None] < ii[None, :]).astype(np.float32)
    incl = (ii[:, None] <= ii[None, :]).astype(np.float32)
    bones = (ii[:, None] // 64 == ii[None, :] // 64).astype(np.float32)
    lower = (ii[:, None] > ii[None, :]).astype(np.float32)
    m["c_masks"] = np.stack([strict, incl, bones, lower])
    return m


def make_in_maps(inp, phases):
    maps = []
    shared = {}
    if "l0out" in phases:
        rk, lr = _gather_order(2048, 2 * NTOK, 2)
        perm = np.where(lr < 1536, rk * 1536 + lr, 3072 + rk * 512 + (lr - 1536))
        shared["w_out_a"] = np.ascontiguousarray(inp["w_out_a"][0][perm])
    if "ffn0" in phases:
        jj = np.arange(DFF).reshape(DFF // 128, 128)
        cols = np.concatenate([jj, DFF + jj], axis=1).reshape(-1)
        shared["w_fi0"] = _pm_in(inp["w_ffn_in"][0][:, cols])
        shared["w_fo0"] = _pm_out(inp["w_ffn_out"][0])
    if "l1out" in phases:
        rk, lr = _gather_order(1024, 2 * NTOK, 2)
        perm = np.where(lr < 512, rk * 512 + lr, 1024 + rk * 512 + (lr - 512))
        shared["w_out_b"] = np.ascontiguousarray(inp["w_out_b"][0][perm])
        jj = np.arange(DFF).reshape(DFF // 128, 128)
        cols = np.concatenate([jj, DFF + jj], axis=1).reshape(-1)
        shared["w_fi1"] = _pm_in(inp["w_ffn_in"][1][:, cols])
        shared["w_fo1"] = _pm_out(inp["w_ffn_out"][1])
    ii = np.arange(128)
    amask = np.concatenate([(ii[:, None] <= ii[None, :]), (ii[:, None] >= ii[None, :])], axis=1).astype(np.float32)
    smask = _smask()
    for c in range(NCORES):
        b, z = _cz(c)
        m = {"c_identf": np.eye(128, dtype=np.float32)}
        if "l1" in phases:
            m.update(_wsh2(np.ascontiguousarray(inp["w_in_b"][0][:, _cols_b(z)]), "w_in_b", c))
            m["g_qkb"] = np.ascontiguousarray(np.stack([inp["q_norm_b"][0], inp["k_norm_b"][0]], axis=1))
            m["c_amask"] = amask
            m["c_smask"] = smask
            for g, nm in enumerate(("cache_swa_kv1", "cache_swa_kv2", "cache_swa_kv3")):
                m["cswa%d" % g] = np.ascontiguousarray(inp[nm][0, 2 * b:2 * b + 2, :, :, 4 * z:4 * z + 4, :])
        m["c_sel"] = np.ascontiguousarray(np.broadcast_to(np.array([1.0 - z, float(z)], np.float32), (128, 2)))
        m["w_mem0"] = np.ascontiguousarray(inp["w_mem_kv"][0])
        m["w_mem1"] = np.ascontiguousarray(inp["w_mem_kv"][1])
        m["memp"] = np.ascontiguousarray(inp["mem_prompt"][b])
        m["g_mem"] = np.ascontiguousarray(inp["norm_mem"].reshape(2, KC, 128).transpose(0, 2, 1))
        m["g_kmem"] = np.ascontiguousarray(np.broadcast_to(inp["k_norm_mem"][:, None, :], (2, 128, 256)))
        m.update(_host_l0in(inp, c))
        if "l0out" in phases:
            m["cmem"] = np.ascontiguousarray(inp["cache_mem_kv"][:, 2 * b:2 * b + 2, :, :, 2 * z:2 * z + 2, :])
            m["g_qmem"] = np.ascontiguousarray(inp["q_norm_mem"].reshape(2, 2, 128).transpose(2, 0, 1))
        if "ffn0" in phases:
            m["g_ffn"] = np.ascontiguousarray(inp["norm_ffn"].reshape(2, KC, 128).transpose(0, 2, 1))
        for k, w in shared.items():
            m.update(_wsh2(w, k, c))
        maps.append(m)
    return maps


def run(inp, phases):
    nc = bass.Bass("TRN2", target_bir_lowering=False)
    with contextlib.ExitStack() as es:
        mk = MK(nc, es, phases)
        mk.build()
    maps = make_in_maps(inp, phases)
    maps = [{k: v for k, v in m.items() if k in mk.ins} for m in maps]
    missing = [k for k in mk.ins if k not in maps[0]]
    assert not missing, missing
    res = run_bass_kernel_spmd(nc, maps, core_ids=list(range(NCORES)))
    return res.results


ALL_PHASES = ("l0out", "ffn0", "l1", "l1out")

_OUT_SHAPES = [
    (4, 2048, 4096), (8, 8, 4096), (1, 4, 48, 64, 64), (1, 4, 1, A_SHIFT),
    (1, 4, 128, 2, 8, 128), (1, 4, 512, 2, 8, 128), (1, 4, 2048, 2, 8, 128),
    (2, 4, NMEM, 2, 4, 256),
    (1, 8, 48, 64, 64), (1, 8, 1, A_SHIFT),
    (1, 8, 128, 2, 8, 128), (1, 8, 512, 2, 8, 128), (1, 8, 2048, 2, 8, 128),
]


def _unchunk(a):
    return np.concatenate([a[0:36].reshape(-1), a[36, :96], a[37, :96], a[38:41].reshape(-1), a[41:45].reshape(-1)])


def kernel(**inputs):
    inp = {k: np.asarray(v) for k, v in inputs.items()}
    res = run(inp, ALL_PHASES)
    outs = [np.zeros(sh, np.float32) for sh in _OUT_SHAPES]
    for b in range(4):
        outs[7][:, b] = res[b]["o_memkv"].reshape(2, NMEM, 2, 4, 256)
        for z in range(2):
            r = res[z * 4 + b]
            hsl = slice(z * HH, (z + 1) * HH)
            outs[2][0, b, hsl] = r["o_wkv"][0]
            outs[8][0, 2 * b, hsl] = r["o_wkv"][1]
            outs[8][0, 2 * b + 1, hsl] = r["o_wkv"][2]
            o = r["o_shift0"].transpose(1, 0, 2)
            cols = _cols_a(z)
            sel = cols < A_SHIFT
            for j, dst in enumerate((outs[3][0, b, 0], outs[9][0, 2 * b, 0], outs[9][0, 2 * b + 1, 0])):
                dst[cols[sel]] = _unchunk(o[:, :, j])[sel]
            for g in range(3):
                outs[4 + g][0, b, :, :, 4 * z:4 * z + 4, :] = r["o_kvp%d" % g]
                outs[10 + g][0, 2 * b, :, :, 4 * z:4 * z + 4, :] = r["o_kvs%d" % g][0]
                outs[10 + g][0, 2 * b + 1, :, :, 4 * z:4 * z + 4, :] = r["o_kvs%d" % g][1]
            outs[0][b, z * HALF:(z + 1) * HALF] = r["o_y"][:HALF]
            outs[1][2 * b + z] = r["o_y"][HALF:]
    return tuple(outs)
```

```python
import contextlib
import numpy as np
import concourse.bass as bass
import concourse.mybir as mybir
from concourse.bass_utils import run_bass_kernel_spmd

F32 = mybir.dt.float32
BF16 = mybir.dt.bfloat16
ALU = mybir.AluOpType
AF = mybir.ActivationFunctionType
AX = mybir.AxisListType

NCORES = 8
D = 4096
KC = 32
SEQ = 2048
HALF = 1024
NS = 8
NMEM = 256
MEMW = 1024
MIXW = 3072
A_SHIFT = 9792
A_IN = 10816
B_IN = 10240
DFF = 11008
EPS = 1e-6
SHC = A_SHIFT // 8

NTOK = HALF + NS
TBLK = [(0, 512), (512, 512), (1024, NS)]
HH = 24
ZC = 4608 + 192 + 384 + 512
U0C = 2048 + 1 + 2 * 9
PAN_A = [(i * 512, [128] * 4) for i in range(9)] + [(4608, [96, 96]), (4800, [128] * 3), (5184, [128] * 4)]
Q4 = [[0, 1, 2, 3], [4, 5, 6, 7]]
P04 = [[0, 4], [1, 5], [2, 6], [3, 7]]
WSPEC2 = {
    "w_in_a": (4096, ZC, 256, "q4"), "w_out_a": (4096, 4096, 1024, "full"),
    "w_in_b": (4096, 5120, 256, "q4"), "w_out_b": (2048, 4096, 1024, "full"),
    "w_fi0": (DFF, 32 * 256, 256, "full"), "w_fi1": (DFF, 32 * 256, 256, "full"),
    "w_fo0": (4096, DFF, 128, "full"), "w_fo1": (4096, DFF, 128, "full"),
}
DBG_OUT = set()
UPERM = [0, 1, 4, 5, 2, 3, 6, 7]
WSPEC = {
    "w_mem0": (4096, 2048, 1), "w_mem1": (4096, 2048, 1),
    "w_in_a": (4096, A_IN, 2), "w_out_a": (4096, 4096, 1),
    "w_in_b": (4096, B_IN, 2), "w_out_b": (2048, 4096, 1),
    "w_fi0": (4096, 2 * DFF, 4), "w_fi1": (4096, 2 * DFF, 4),
    "w_fo0": (DFF, 4096, 2), "w_fo1": (DFF, 4096, 2),
}


class Buf:
    __slots__ = ("name", "w", "r")

    def __init__(self, name=""):
        self.name = name
        self.w = None
        self.r = []


class Eng:
    def __init__(self, name, sem):
        self.name = name
        self.sem = sem
        self.cnt = 0
        self.waited = {}
        self.ops = []


class KB:
    def __init__(self, nc, es):
        self.nc = nc
        self.es = es
        self.eng = {}
        for n in ("pe", "act", "dve", "pool", "sp"):
            s = es.enter_context(nc.semaphore("s_" + n))
            self.eng[n] = Eng(n, s)
        self.dq = {}
        for q in ("sp", "pool"):
            ring = []
            for i in range(8):
                s = es.enter_context(nc.semaphore("d_%s%d" % (q, i)))
                ring.append([s, 0])
            self.dq[q] = [ring, 0]

    def buf(self, name=""):
        return Buf(name)

    def _deps(self, reads, writes):
        deps = []
        for b in reads:
            if b.w is not None:
                deps.append(b.w)
        for b in writes:
            if b.w is not None:
                deps.append(b.w)
            deps.extend(b.r)
        return deps

    def _emit_waits(self, e, deps):
        need = {}
        for (s, v) in deps:
            if s is e.sem and e.name in ("pe", "sp"):
                continue
            k = id(s)
            if e.waited.get(k, 0) >= v:
                continue
            if need.get(k, (None, 0))[1] < v:
                need[k] = (s, v)
        for k, (s, v) in need.items():
            e.waited[k] = v
            e.ops.append(lambda h, s=s, v=v: h.wait_ge(s, v))

    def _mark(self, ev, reads, writes):
        for b in reads:
            if len(b.r) > 64:
                last = {}
                for (s, v) in b.r:
                    if last.get(id(s), (None, 0))[1] < v:
                        last[id(s)] = (s, v)
                b.r = list(last.values())
            b.r.append(ev)
        for b in writes:
            b.w = ev
            b.r = []

    def op(self, en, fn, reads=(), writes=()):
        e = self.eng[en]
        self._emit_waits(e, self._deps(reads, writes))
        e.cnt += 1
        sem = e.sem
        e.ops.append(lambda h, fn=fn, sem=sem: fn(h).then_inc(sem, 1))
        ev = (sem, e.cnt)
        self._mark(ev, reads, writes)
        return ev

    def dma(self, q, out, in_, reads=(), writes=(), **kw):
        e = self.eng[q]
        ring, idx = self.dq[q]
        slot = ring[idx]
        self.dq[q][1] = (idx + 1) % len(ring)
        deps = self._deps(reads, writes)
        if slot[1] > 0:
            deps.append((slot[0], slot[1]))
        self._emit_waits(e, deps)
        slot[1] += 16
        s, v = slot[0], slot[1]
        e.ops.append(lambda h, s=s, out=out, in_=in_, kw=kw: h.dma_start(out=out, in_=in_, **kw).then_inc(s, 16))
        ev = (s, v)
        self._mark(ev, reads, writes)
        return ev

    def coll(self, fn, reads=(), writes=()):
        return self.op("pool", fn, reads, writes)

    def finish(self, out_bufs):
        e = self.eng["sp"]
        deps = []
        for b in out_bufs:
            if b.w is not None:
                deps.append(b.w)
        for q in self.dq:
            for s, v in self.dq[q][0]:
                if v > 0:
                    deps.append((s, v))
        for n in ("pe", "act", "dve", "pool"):
            en = self.eng[n]
            if en.cnt > 0:
                deps.append((en.sem, en.cnt))
        self._emit_waits(e, deps)

    def barrier(self):
        deps = []
        for q in self.dq:
            for s, v in self.dq[q][0]:
                if v > 0:
                    deps.append((s, v))
        for n in ("pe", "act", "dve", "pool", "sp"):
            en = self.eng[n]
            if en.cnt > 0:
                deps.append((en.sem, en.cnt))
        for n in ("pe", "act", "dve", "pool", "sp"):
            self._emit_waits(self.eng[n], deps)

    def replay(self):
        nc = self.nc
        allops = {n: self.eng[n].ops for n in self.eng}
        for n in self.eng:
            self.eng[n].ops = []
        self._replay(allops)

    def _replay(self, allops):
        nc = self.nc
        with nc.Block() as block:
            @block.tensor
            def _(h):
                for f in allops["pe"]:
                    f(h)

            @block.scalar
            def _(h):
                for f in allops["act"]:
                    f(h)

            @block.vector
            def _(h):
                for f in allops["dve"]:
                    f(h)

            @block.gpsimd
            def _(h):
                for f in allops["pool"]:
                    f(h)

            @block.sync
            def _(h):
                for f in allops["sp"]:
                    f(h)


class T:
    def __init__(self, t, b):
        self.t = t
        self.b = b

    def __getitem__(self, k):
        return self.t[k]


class MK:
    def __init__(self, nc, es, phases):
        self.nc = nc
        self.es = es
        self.kb = KB(nc, es)
        self.phases = phases
        self.ins = {}
        self.outs = {}
        self.scr = {}
        self.rr = 0

    def inp(self, name, shape, dt=F32):
        ap = self.nc.dram_tensor(name, list(shape), dt, kind="ExternalInput").ap()
        t = T(ap, self.kb.buf(name))
        self.ins[name] = t
        return t

    def outp(self, name, shape, dt=F32):
        ap = self.nc.dram_tensor(name, list(shape), dt, kind="ExternalOutput").ap()
        t = T(ap, self.kb.buf(name))
        self.outs[name] = t
        return t

    def dram(self, name, shape, dt):
        kind = "ExternalOutput" if name in DBG_OUT else "Internal"
        ap = self.nc.dram_tensor(name, list(shape), dt, kind=kind).ap()
        t = T(ap, self.kb.buf(name))
        self.scr[name] = t
        return t

    def sb(self, stack, name, shape, dt):
        esz = 4 if dt == F32 else 2
        n = 1
        for d in shape[1:]:
            n *= d
        nbytes = (n * esz + 31) // 32 * 32
        nf = nbytes // 4
        if self.arena_off + nf > self.arena_n:
            raise RuntimeError("SBUF arena overflow: %s needs %d B at %d of %d" % (name, nbytes, self.arena_off * 4, self.arena_n * 4))
        v = self.arena[:, self.arena_off:self.arena_off + nf]
        self.arena_off += nf
        if dt != F32:
            v = v.bitcast(dt)
        v = v[:, 0:n]
        if len(shape) == 3:
            v = v.rearrange("p (a b) -> p a b", b=shape[2])
        elif len(shape) == 4:
            v = v.rearrange("p (a b c) -> p a b c", b=shape[2], c=shape[3])
        if shape[0] < 128:
            v = v[0:shape[0]]
        return T(v, self.kb.buf(name))

    @contextlib.contextmanager
    def phase(self):
        off = self.arena_off
        yield None
        self.kb.barrier()
        self.arena_off = off

    def ps(self, stack, name, shape, dt):
        t = stack.enter_context(self.nc.psum_tensor(name, list(shape), dt))
        return T(t, self.kb.buf(name))

    def op(self, en, fn, reads=(), writes=()):
        return self.kb.op(en, fn, [x.b for x in reads], [x.b for x in writes])

    def dma(self, q, out, in_, reads=(), writes=(), **kw):
        return self.kb.dma(q, out, in_, [x.b for x in reads], [x.b for x in writes], **kw)

    def rstd(self, t, ap, scale, eps):
        self.op("act", lambda h: h.activation(ap, ap, AF.Sqrt, bias=eps, scale=scale), [t], [t])
        self.op("dve", lambda h: h.reciprocal(ap, ap), [t], [t])

    def evac_engine(self):
        self.rr += 1
        return "dve" if self.rr % 2 else "act"

    def copy(self, en, out, in_, reads, writes):
        if en == "act":
            return self.op("act", lambda h: h.activation(out, in_, AF.Copy), reads, writes)
        return self.op(en, lambda h: h.tensor_copy(out, in_), reads, writes)

    def setup_common(self):
        es = self.es
        self.arena_n = 51 * 1024
        self.arena = es.enter_context(self.nc.sbuf_tensor("arena", [128, self.arena_n], F32))
        self.arena_off = 0
        self.c_identf = self.inp("c_identf", [128, 128])
        self.identb = self.sb(es, "identb", [128, 128], BF16)
        self.identf = self.sb(es, "identf", [128, 128], F32)
        self.dma("pool", self.identb[:], self.c_identf.t, [self.c_identf], [self.identb])
        self.dma("sp", self.identf[:], self.c_identf.t, [self.c_identf], [self.identf])
        c_sel = self.inp("c_sel", [128, 2])
        self.sel = self.sb(es, "sel", [128, 2], F32)
        self.dma("sp", self.sel[:], c_sel.t, [c_sel], [self.sel])
        self.ones_f = self.sb(es, "ones_f", [128, 128], F32)
        self.ones_b = self.sb(es, "ones_b", [128, 128], BF16)
        self.op("dve", lambda h: h.memset(self.ones_f.t[:], 1.0), [], [self.ones_f])
        self.op("dve", lambda h: h.memset(self.ones_b.t[:], 1.0), [], [self.ones_b])
        self.pf = [self.ps(es, "pf%d" % i, [128, 512], F32) for i in range(6)]
        self.pbk = [self.ps(es, "pb%d" % i, [128, 1024], BF16) for i in range(2)]
        self.pfi = 0
        self.pbi = 0

    def next_pf(self):
        p = self.pf[self.pfi % getattr(self, "pf_lim", len(self.pf))]
        self.pfi += 1
        return p

    def next_pb(self):
        p = self.pbk[self.pbi % len(self.pbk)]
        self.pbi += 1
        return p

    def phase_weights(self, names):
        self.W = getattr(self, "W", {})
        for name, (K, N) in names:
            src = self.inp(name, [K, N])
            full = self.dram(name + "_bf", [K, N], BF16)
            step = max(1, (2 << 20) // (N * 4))
            r0 = 0
            while r0 < K:
                r1 = min(K, r0 + step)
                self.dma("pool", full.t[r0:r1, :], src.t[r0:r1, :], [src], [full])
                r0 = r1
            self.W[name] = ([full], K, N)

    def wslice(self, name, k0, nk, c0, nc_):
        full, kg, N = self.W[name]
        g = k0 // kg
        assert (k0 + nk - 1) // kg == g
        return full[g], full[g].t[k0 - g * kg:k0 - g * kg + nk, c0:c0 + nc_]

    def load_wpanel(self, q, dst, name, nkc, c0, ncols, kc0=0):
        full, kg, N = self.W[name]
        kpg = kg // 128 if kg % 128 == 0 else None
        if kpg is None:
            for k in range(nkc):
                r0 = (kc0 + k) * 128
                done = 0
                while done < 128:
                    g = (r0 + done) // kg
                    lo = r0 + done - g * kg
                    n = min(128 - done, kg - lo)
                    self.dma(q, dst.t[done:done + n, k, 0:ncols], full[g].t[lo:lo + n, c0:c0 + ncols], [full[g]], [dst])
                    done += n
            return
        k = 0
        while k < nkc:
            g = (kc0 + k) // kpg
            lk = (kc0 + k) - g * kpg
            n = min(nkc - k, kpg - lk)
            src = full[g].t[lk * 128:(lk + n) * 128, c0:c0 + ncols].rearrange("(k p) n -> p k n", p=128)
            self.dma(q, dst.t[:, k:k + n, 0:ncols], src, [full[g]], [dst])
            k += n

    def rms_transpose(self, st, src_ap, src_t, ntok, gain, hT, col0, tag):
        x = self.sb(st, tag + "_x", [128, D], F32)
        xb = self.sb(st, tag + "_xb", [128, D], BF16)
        junk = self.sb(st, tag + "_j", [128, D], F32)
        ss = self.sb(st, tag + "_ss", [128, 1], F32)
        for t0 in range(0, ntok, 128):
            n = min(128, ntok - t0)
            self.dma("sp", x.t[0:n, :], src_ap[t0:t0 + n, :], [src_t], [x])
            self.op("act", lambda h, n=n: h.activation(junk.t[0:n, :], x.t[0:n, :], AF.Square), [x], [junk])
            self.op("dve", lambda h, n=n: h.reduce_sum(ss.t[0:n, :], junk.t[0:n, :], AX.X), [junk], [ss])
            self.rstd(ss, ss.t[0:n, :], 1.0 / D, EPS)
            self.op("act", lambda h, n=n: h.activation(xb.t[0:n, :], x.t[0:n, :], AF.Copy, scale=ss.t[0:n, 0:1]), [x, ss], [xb])
            for k0 in range(0, KC, 8):
                pb = self.next_pb()
                for j in range(8):
                    k = k0 + j
                    self.op("pe", lambda h, pb=pb, j=j, k=k, n=n: h.transpose(pb.t[:, j * 128:j * 128 + n], xb.t[0:n, k * 128:(k + 1) * 128], self.identb.t[0:n, 0:n]),
                            [xb, self.identb], [pb])
                o = hT.t[:, k0:k0 + 8, col0 + t0:col0 + t0 + n]
                i0 = pb.t[:, :].rearrange("p (j t) -> p j t", t=128)[:, :, 0:n]
                g = gain.t[:, k0:k0 + 8].unsqueeze(2).to_broadcast([128, 8, n])
                self.op("dve", lambda h, o=o, i0=i0, g=g: h.tensor_tensor(o, i0, g, ALU.mult), [pb, gain], [hT])

    def phase_memkv(self):
        memp = self.inp("memp", [NMEM, D])
        g_mem = self.inp("g_mem", [2, 128, KC])
        g_kmem = self.inp("g_kmem", [2, 128, 256])
        o_memkv = self.outp("o_memkv", [2, NMEM, 2 * MEMW])
        self.mkz = self.dram("mkz", [2, NMEM, 2, 512], F32)
        for li in range(2):
            with self.phase() as st:
                gain = self.sb(st, "mk_gain", [128, KC], F32)
                gk = self.sb(st, "mk_gk", [128, 256], F32)
                hT = self.sb(st, "mk_hT", [128, KC, NMEM], BF16)
                self.dma("sp", gain.t[:], g_mem.t[li], [g_mem], [gain])
                self.dma("sp", gk.t[:], g_kmem.t[li], [g_kmem], [gk])
                self.rms_transpose(st, memp.t, memp, NMEM, gain, hT, 0, "mk")
                wp = [self.sb(st, "mk_w%d" % i, [128, KC, 512], BF16) for i in range(2)]
                kv = [self.sb(st, "mk_kv%d" % i, [128, 2 * MEMW], F32) for i in range(2)]
                ssq = self.sb(st, "mk_ssq", [128, 4], F32)
                sq = self.sb(st, "mk_sq", [128, 4, 256], F32)
                zk = self.sb(st, "mk_zk", [128, 2, 512], F32)
                for pi in range(4):
                    w = wp[pi % 2]
                    self.load_wpanel("sp", w, "w_mem%d" % li, KC, pi * 512, 512)
                    for tt in range(2):
                        pf = self.next_pf()
                        for k in range(KC):
                            self.op("pe", lambda h, pf=pf, k=k, tt=tt, w=w: h.matmul(pf.t[:, :], hT.t[:, k, tt * 128:(tt + 1) * 128], w.t[:, k, :], start=(k == 0), stop=(k == KC - 1)),
                                    [hT, w], [pf])
                        self.copy("dve", kv[tt].t[:, pi * 512:(pi + 1) * 512], pf.t[:, :], [pf], [kv[tt]])
                for tt in range(2):
                    k3 = kv[tt].t[:, 0:MEMW].rearrange("p (h d) -> p h d", d=256)
                    self.op("act", lambda h, k3=k3: h.activation(sq.t[:], k3, AF.Square), [kv[tt]], [sq])
                    self.op("dve", lambda h: h.reduce_sum(ssq.t[:], sq.t[:], AX.X), [sq], [ssq])
                    self.op("act", lambda h: h.activation(ssq.t[:], ssq.t[:], AF.Sqrt, bias=EPS, scale=1.0 / 256), [ssq], [ssq])
                    self.op("dve", lambda h: h.reciprocal(ssq.t[:], ssq.t[:]), [ssq], [ssq])
                    self.op("pool", lambda h, k3=k3: h.tensor_tensor(k3, k3, ssq.t[:].unsqueeze(2).to_broadcast([128, 4, 256]), ALU.mult), [kv[tt], ssq], [kv[tt]])
                    self.op("dve", lambda h, k3=k3: h.tensor_tensor(k3, k3, gk.t[:].unsqueeze(1).to_broadcast([128, 4, 256]), ALU.mult), [kv[tt], gk], [kv[tt]])
                    self.dma("sp", o_memkv.t[li, tt * 128:(tt + 1) * 128, :], kv[tt].t[:], [kv[tt]], [o_memkv])
                    for part in range(2):
                        lo = kv[tt].t[:, part * 1024:part * 1024 + 512]
                        hi = kv[tt].t[:, part * 1024 + 512:part * 1024 + 1024]
                        self.op("dve", lambda h, part=part, lo=lo: h.tensor_scalar(zk.t[:, part, :], lo, self.sel.t[:, 0:1], None, ALU.mult), [kv[tt], self.sel], [zk])
                        self.op("dve", lambda h, part=part, hi=hi: h.scalar_tensor_tensor(zk.t[:, part, :], hi, self.sel.t[:, 1:2], zk.t[:, part, :], ALU.mult, ALU.add), [kv[tt], self.sel, zk], [zk])
                    self.dma("sp", self.mkz.t[li, tt * 128:(tt + 1) * 128], zk.t[:], [zk], [self.mkz])

    def phase_shift(self):
        NT = 12
        xlast = self.inp("xlast", [NT, D])
        g_mix0 = self.inp("g_mix0", [128, KC])
        o_shift = self.outp("o_shift", [NT, SHC])
        with self.phase() as st:
            gain = self.sb(st, "sh_gain", [128, KC], F32)
            hT = self.sb(st, "sh_hT", [128, KC, NT], BF16)
            res = self.sb(st, "sh_res", [NT, SHC], F32)
            self.dma("sp", gain.t[:], g_mix0.t, [g_mix0], [gain])
            self.rms_transpose(st, xlast.t, xlast, NT, gain, hT, 0, "sh")
            PW = SHC // 3
            wp = [self.sb(st, "sh_w%d" % i, [128, KC, PW], BF16) for i in range(2)]
            for pi in range(3):
                w = wp[pi % 2]
                self.load_wpanel("sp", w, "w_ina_c", KC, pi * PW, PW)
                pf = self.next_pf()
                for k in range(KC):
                    self.op("pe", lambda h, pf=pf, k=k, w=w: h.matmul(pf.t[0:NT, 0:PW], hT.t[:, k, 0:NT], w.t[:, k, 0:PW], start=(k == 0), stop=(k == KC - 1)),
                            [hT, w], [pf])
                self.copy("dve", res.t[:, pi * PW:(pi + 1) * PW], pf.t[0:NT, 0:PW], [pf], [res])
            self.dma("sp", o_shift.t, res.t[:], [res], [o_shift])

    def allgather(self, groups, src, dst):
        self.kb.coll(lambda h, i=src.t, o=dst.t: h.collective_compute("AllGather", ALU.bypass, replica_groups=groups, ins=[i], outs=[o]),
                     [src.b], [dst.b])

    def phase_weights2(self, names):
        self.W = getattr(self, "W", {})
        for name in names:
            K, N, ru, mode = WSPEC2[name]
            nr = 4 if mode == "q4" else 8
            full = self.dram(name + "_f", [K, N], BF16)
            parts = [(name, 0, K // ru, ru)]
            if K % ru:
                parts.append((name + "_t", (K // ru) * ru, 1, K % ru))
            for (pname, row0, nu, ru_) in parts:
                rows = ru_ // nr
                src = self.inp(pname, [nu, rows, N])
                shard = self.dram(pname + "_sh", [nu, rows, N], BF16)
                half = self.dram(pname + "_h", [nu, 4 * rows, N], BF16) if mode != "q4" else None
                step = max(1, (2 << 20) // (rows * N * 4))
                for u0 in range(0, nu, step):
                    u1 = min(nu, u0 + step)
                    self.dma("pool", shard.t[u0:u1], src.t[u0:u1], [src], [shard])
                for u in range(nu):
                    base = row0 + u * ru_
                    if mode == "q4":
                        self.kb.coll(lambda h, i=shard.t[u], o=full.t[base:base + ru_, :]: h.collective_compute(
                            "AllGather", ALU.bypass, replica_groups=Q4, ins=[i], outs=[o]), [shard.b], [full.b])
                    else:
                        self.kb.coll(lambda h, i=shard.t[u], o=half.t[u]: h.collective_compute(
                            "AllGather", ALU.bypass, replica_groups=Q4, ins=[i], outs=[o]), [shard.b], [half.b])
                        for j in range(2):
                            i_ap = half.t[u, j * 2 * rows:(j + 1) * 2 * rows, :]
                            o_ap = full.t[base + j * 4 * rows:base + (j + 1) * 4 * rows, :]
                            self.kb.coll(lambda h, i=i_ap, o=o_ap: h.collective_compute(
                                "AllGather", ALU.bypass, replica_groups=P04, ins=[i], outs=[o]), [half.b], [full.b])
            self.W[name] = ([full], K, N)

    def pair_gather(self, src, dst, rows, ncols, esz):
        rp = max(128, ((2 << 20) // (ncols * esz)) // 128 * 128)
        pieces = []
        r0 = 0
        while r0 < rows:
            r1 = min(rows, r0 + rp)
            self.kb.coll(lambda h, i=src.t[r0:r1, :], o=dst.t[2 * r0:2 * r1, :]: h.collective_compute(
                "AllGather", ALU.bypass, replica_groups=P04, ins=[i], outs=[o]), [src.b], [dst.b])
            pieces.append((r0, r1))
            r0 = r1
        return pieces

    def load_wpanel_pm(self, q, dst, name, j, nkc, ncols):
        full = self.W[name][0][0]
        self.dma(q, dst.t[:, 0:nkc, 0:ncols], full.t[j * 128:(j + 1) * 128, :].rearrange("p (k c) -> p k c", c=ncols), [full], [dst])

    def gemm_fm(self, st, xT, ntok, wname, panels, nkc, sink, tag, wbufs=None, kc0=0, pm=False):
        pw = max(sum(sz) for _, sz in panels)
        if wbufs is None:
            wbufs = [self.sb(st, "%s_w%d" % (tag, i), [128, nkc, pw], BF16) for i in range(2)]
        ci = 0
        for pi, (c0, sizes) in enumerate(panels):
            w = wbufs[pi % 2]
            if pm:
                self.load_wpanel_pm("sp", w, wname, pi, nkc, sum(sizes))
            else:
                self.load_wpanel("sp", w, wname, nkc, c0, sum(sizes), kc0=kc0)
            off = 0
            for m in sizes:
                pf = self.next_pf()
                for k in range(nkc):
                    self.op("pe", lambda h, pf=pf, k=k, w=w, off=off, m=m: h.matmul(pf.t[0:m, 0:ntok], w.t[:, k, off:off + m], xT.t[:, k, 0:ntok], start=(k == 0), stop=(k == nkc - 1)),
                            [xT, w], [pf])
                sink(ci, pf, m)
                off += m
                ci += 1

    def phase_l0_prep(self):
        x_own = self.inp("x_own", [NTOK, D])
        g_mix = self.inp("g_mix", [2, 128, KC])
        self.g_mix = g_mix
        self.XT = [self.dram("XT%d" % i, [128, KC, NTOK], F32) for i in range(5)]
        hT_own = self.hT_own = self.dram("hT_own", [D, NTOK], BF16)
        self.hT_pair = self.dram("hT_pair", [2 * D, NTOK], BF16)
        with self.phase() as st:
            gain = self.sb(st, "p0_gain", [128, KC], F32)
            self.dma("sp", gain.t[:], g_mix.t[0], [g_mix], [gain])
            x = self.sb(st, "p0_x", [128, D], F32)
            xb = self.sb(st, "p0_xb", [128, D], BF16)
            junk = self.sb(st, "p0_j", [128, D], F32)
            ss = self.sb(st, "p0_ss", [128, 1], F32)
            hst = [self.sb(st, "p0_h%d" % i, [128, KC, 128], BF16) for i in range(2)]
            xst = [self.sb(st, "p0_xs%d" % i, [128, KC, 128], F32) for i in range(2)]
            for ti, t0 in enumerate(range(0, NTOK, 128)):
                n = min(128, NTOK - t0)
                hs, xs = hst[ti % 2], xst[ti % 2]
                self.dma("sp", x.t[0:n, :], x_own.t[t0:t0 + n, :], [x_own], [x])
                self.op("act", lambda h, n=n: h.activation(junk.t[0:n, :], x.t[0:n, :], AF.Square), [x], [junk])
                self.op("dve", lambda h, n=n: h.reduce_sum(ss.t[0:n, :], junk.t[0:n, :], AX.X), [junk], [ss])
                self.rstd(ss, ss.t[0:n, :], 1.0 / D, EPS)
                self.op("act", lambda h, n=n: h.activation(xb.t[0:n, :], x.t[0:n, :], AF.Copy, scale=ss.t[0:n, 0:1]), [x, ss], [xb])
                for k0 in range(0, KC, 8):
                    pb = self.next_pb()
                    for j in range(8):
                        k = k0 + j
                        self.op("pe", lambda h, pb=pb, j=j, k=k, n=n: h.transpose(pb.t[:, j * 128:j * 128 + n], xb.t[0:n, k * 128:(k + 1) * 128], self.identb.t[0:n, 0:n]),
                                [xb, self.identb], [pb])
                    o = hs.t[:, k0:k0 + 8, 0:n]
                    i0 = pb.t[:, :].rearrange("p (j t) -> p j t", t=128)[:, :, 0:n]
                    g = gain.t[:, k0:k0 + 8].unsqueeze(2).to_broadcast([128, 8, n])
                    self.op("dve", lambda h, o=o, i0=i0, g=g: h.tensor_tensor(o, i0, g, ALU.mult), [pb, gain], [hs])
                for k0 in range(0, KC, 4):
                    pf = self.next_pf()
                    for j in range(4):
                        k = k0 + j
                        self.op("pe", lambda h, pf=pf, j=j, k=k, n=n: h.transpose(pf.t[:, j * 128:j * 128 + n], x.t[0:n, k * 128:(k + 1) * 128], self.identf.t[0:n, 0:n]),
                                [x, self.identf], [pf])
                    o = xs.t[:, k0:k0 + 4, 0:n]
                    i0 = pf.t[:, :].rearrange("p (j t) -> p j t", t=128)[:, :, 0:n]
                    self.copy("act", o, i0, [pf], [xs])
                self.dma("sp", hT_own.t[:, t0:t0 + n].rearrange("(k p) t -> p k t", p=128), hs.t[:, :, 0:n], [hs], [hT_own])
                self.dma("sp", self.XT[0].t[:, :, t0:t0 + n], xs.t[:, :, 0:n], [xs], [self.XT[0]])
        self.hT_pieces = self.pair_gather(hT_own, self.hT_pair, D, NTOK, 2)

    def phase_l0_inproj(self):
        self.U0 = self.dram("U0", [45, 128, U0C], F32)
        o_shift = self.outp("o_shift0", [128, 45, 3])
        U0 = self.U0
        blocks = [(0, 0, 512, 1), (0, 512, 512, 513), (1, 0, 512, 1025), (1, 512, 512, 1537)]
        with self.phase() as st:
            xT = [self.sb(st, "ip_x%d" % i, [128, KC, 512], BF16) for i in range(2)]
            stg = [self.sb(st, "ip_s%d" % i, [128, 512], F32) for i in range(3)]
            wbufs = [self.sb(st, "ip_w%d" % i, [128, KC, 512], BF16) for i in range(2)]
            sh = self.sb(st, "ip_sh", [128, 45, 3], F32)
            self.op("dve", lambda h: h.memset(sh.t[:], 0.0), [], [sh])
            self.sidx = 0

            def run_block(bi, loads, ntok, outs):
                x = xT[bi % 2]
                for (dst0, r, t0, n) in loads:
                    for (r0, r1) in self.hT_pieces:
                        base = 2 * r0 + r * (r1 - r0)
                        src = self.hT_pair.t[base:base + (r1 - r0), t0:t0 + n].rearrange("(k p) t -> p k t", p=128)
                        self.dma("sp", x.t[:, r0 // 128:r1 // 128, dst0:dst0 + n], src, [self.hT_pair], [x])

                def sink(ci, pf, m):
                    sg = stg[self.sidx % 3]
                    self.sidx += 1
                    self.copy(self.evac_engine(), sg.t[0:m, 0:ntok], pf.t[0:m, 0:ntok], [pf], [sg])
                    for (s0, n, c0) in outs:
                        self.dma("sp", U0.t[ci, 0:m, c0:c0 + n], sg.t[0:m, s0:s0 + n], [sg], [U0])
                        j = {2048: 0, 2057: 1, 2066: 2}.get(c0 + n - 1)
                        if j is not None:
                            self.op("dve", lambda h, j=j, ci=ci, m=m, sg=sg, e=s0 + n - 1: h.tensor_copy(sh.t[0:m, ci, j:j + 1], sg.t[0:m, e:e + 1]), [sg], [sh])
                self.gemm_fm(st, x, ntok, "w_in_a", PAN_A, KC, sink, "ip", wbufs=wbufs)

            for bi, (r, t0, n, c0) in enumerate(blocks):
                run_block(bi, [(0, r, t0, n)], n, [(0, n, c0)])
            run_block(4, [(0, 0, HALF, NS), (NS, 1, HALF, NS)], 2 * NS, [(0, NS, 2050), (NS, NS, 2059)])
            self.dma("sp", o_shift.t, sh.t[:], [sh], [o_shift])

    def phase_rwkv(self):
        U0 = self.U0
        NP = 10
        pa_in = self.inp("pa", [128, NP, 12])
        pl_in = self.inp("pl", [128, 5])
        shcol = self.inp("shcol", [128, 45, 2])
        w2_in = self.inp("w2z", [96, 1536])
        a2_in = self.inp("a2z", [96, 1536])
        g2_in = self.inp("g2z", [128, 3, 1536])
        st_in = self.inp("wkv_in", [2, HH, 64, 64])
        msk_in = self.inp("c_masks", [4, 128, 128])
        o_wkv = self.outp("o_wkv", [3, HH, 64, 64])
        self.opT0 = self.dram("opT0", [2048, 2 * NTOK], BF16)
        SEGS = [(1, 2048, 0, 0), (2050, 8, 1, 2048), (2059, 8, 2, 2056)]
        with self.phase() as st:
            pa = self.sb(st, "rw_pa", [128, NP, 12], F32)
            pl = self.sb(st, "rw_pl", [128, 5], F32)
            sc = self.sb(st, "rw_sc", [128, 45, 2], F32)
            px = self.sb(st, "rw_px", [128, 2, 12], F32)
            w2 = self.sb(st, "rw_w2", [96, 1536], BF16)
            a2 = self.sb(st, "rw_a2", [96, 1536], BF16)
            g2 = self.sb(st, "rw_g2", [128, 3, 1536], BF16)
            mk = self.sb(st, "rw_mk", [128, 4, 128], F32)
            bones = self.sb(st, "rw_bo", [128, 128], F32)
            self.dma("sp", pa.t[:], pa_in.t, [pa_in], [pa])
            self.dma("sp", pl.t[:], pl_in.t, [pl_in], [pl])
            self.dma("sp", sc.t[:], shcol.t, [shcol], [sc])
            self.dma("pool", w2.t[:], w2_in.t, [w2_in], [w2])
            self.dma("pool", a2.t[:], a2_in.t, [a2_in], [a2])
            self.dma("pool", g2.t[:], g2_in.t, [g2_in], [g2])
            self.dma("sp", mk.t[:], msk_in.t.rearrange("m p c -> p m c"), [msk_in], [mk])
            self.op("dve", lambda h: h.tensor_copy(bones.t[:], mk.t[:, 2, :]), [mk], [bones])
            self.op("dve", lambda h: h.tensor_scalar(px.t[:, 0, :], pa.t[:, 3, :], -1.0, None, ALU.mult), [pa], [px])
            self.op("dve", lambda h: h.tensor_scalar(px.t[:, 1, :], pa.t[:, 6, :], -1.0, 1.0, ALU.mult, ALU.add), [pa], [px])
            MS, MI, ML = mk.t[:, 0, :], mk.t[:, 1, :], mk.t[:, 3, :]

            def load_shift(dst, ci, mu_ap, rows, tag):
                raw = self.sb(st, tag + "_raw", [128, U0C], F32) if not hasattr(self, "_raw") else self._raw
                self._raw = raw
                d = self._dtmp
                self.dma("sp", raw.t[0:rows, :], U0.t[ci, 0:rows, :], [U0], [raw])
                self.op("dve", lambda h: h.memset(raw.t[0:rows, 0:1], 0.0), [], [raw])
                self.op("dve", lambda h: h.tensor_copy(raw.t[0:rows, 2049:2050], sc.t[0:rows, ci, 0:1]), [sc], [raw])
                self.op("dve", lambda h: h.tensor_copy(raw.t[0:rows, 2058:2059], sc.t[0:rows, ci, 1:2]), [sc], [raw])
                self.op("dve", lambda h: h.tensor_tensor(d.t[0:rows, 0:U0C - 1], raw.t[0:rows, 0:U0C - 1], raw.t[0:rows, 1:U0C], ALU.subtract), [raw], [d])
                self.op("dve", lambda h: h.scalar_tensor_tensor(dst.t[0:rows, 1:U0C], d.t[0:rows, 0:U0C - 1], mu_ap, raw.t[0:rows, 1:U0C], ALU.mult, ALU.add), [d, raw, pl, pa], [dst])

            self._dtmp = self.sb(st, "rw_d", [128, U0C], F32)
            self._raw = self.sb(st, "rw_raw", [128, U0C], F32)
            lt = self.sb(st, "rw_bon", [128, U0C], F32)
            twl = self.sb(st, "rw_twl", [96, U0C], BF16)
            tal = self.sb(st, "rw_tal", [96, U0C], BF16)
            sgl = self.sb(st, "rw_sgl", [128, 3, U0C], BF16)
            load_shift(lt, 36, pl.t[0:96, 0:1], 96, "rw")
            self.op("act", lambda h: h.activation(twl.t[:, 1:U0C], lt.t[0:96, 1:U0C], AF.Tanh), [lt], [twl])
            load_shift(lt, 37, pl.t[0:96, 1:2], 96, "rw")
            self.op("act", lambda h: h.activation(tal.t[:, 1:U0C], lt.t[0:96, 1:U0C], AF.Copy), [lt], [tal])
            for j in range(3):
                load_shift(lt, 38 + j, pl.t[:, 2 + j:3 + j], 128, "rw")
                self.op("act", lambda h, j=j: h.activation(sgl.t[:, j, 1:U0C], lt.t[:, 1:U0C], AF.Sigmoid), [lt], [sgl])

            F = lambda n: self.sb(st, "rw_" + n, [128, U0C], F32)
            r, k, v, lw, L, L2, a, g, kk, kp, t1 = [F(n) for n in ("r", "k", "v", "lw", "L", "L2", "a", "g", "kk", "kp", "t1")]
            bon = lt
            ar = self.sb(st, "rw_ar", [128, 2, U0C], BF16)
            bT = self.sb(st, "rw_bT", [128, U0C], BF16)
            kT = self.sb(st, "rw_kT", [128, U0C], BF16)
            vb = self.sb(st, "rw_vb", [128, U0C], BF16)
            pc = self._raw
            opo = self.sb(st, "rw_opo", [128, 2 * NTOK], BF16)
            Tst = self.sb(st, "rw_T", [128, 64], F32)
            Tb = self.sb(st, "rw_Tb", [128, 64], BF16)
            zs = self.sb(st, "rw_zs", [64, 128], F32)
            tm = [self.sb(st, "rw_tm%d" % i, [128, 3, 128], BF16) for i in range(2)]
            sm = {n: self.sb(st, "rw_" + n, [128, 128], BF16) for n in ("N", "NT", "N2", "NT2", "X", "X2", "Aak", "Abr", "Akr", "W1", "U")}
            ytm = self.sb(st, "rw_y", [128, 128], F32)
            gs = self.sb(st, "rw_gs", [128, 8], F32)
            tsum = self.sb(st, "rw_ts", [128, 64], F32)
            wout = self.sb(st, "rw_wo", [64, 128], F32)

            def head_pair(hp):
                P = lambda i: pa.t[:, i, hp:hp + 1]
                load_shift(r, hp, P(0), 128, "rw")
                load_shift(k, 12 + hp, P(1), 128, "rw")
                load_shift(v, 24 + hp, P(2), 128, "rw")
                for c0 in range(1, U0C, 512):
                    n = min(512, U0C - c0)
                    cs = slice(c0, c0 + n)
                    pf = self.next_pf()
                    self.op("pe", lambda h, pf=pf, cs=cs, n=n: h.matmul(pf.t[:, 0:n], w2.t[:, hp * 128:(hp + 1) * 128], twl.t[:, cs], start=True, stop=True), [w2, twl], [pf])
                    self.op("act", lambda h, pf=pf, cs=cs, n=n: h.activation(lw.t[:, cs], pf.t[:, 0:n], AF.Exp, bias=px.t[:, 0, hp:hp + 1], scale=-1.0), [pf, px], [lw])
                    self.op("act", lambda h, cs=cs: h.activation(lw.t[:, cs], lw.t[:, cs], AF.Ln, bias=1.0), [lw], [lw])
                    self.op("act", lambda h, cs=cs: h.activation(lw.t[:, cs], lw.t[:, cs], AF.Exp, bias=-0.5, scale=-1.0), [lw], [lw])
                    pf = self.next_pf()
                    self.op("pe", lambda h, pf=pf, cs=cs, n=n: h.matmul(pf.t[:, 0:n], a2.t[:, hp * 128:(hp + 1) * 128], tal.t[:, cs], start=True, stop=True), [a2, tal], [pf])
                    self.op("act", lambda h, pf=pf, cs=cs, n=n: h.activation(a.t[:, cs], pf.t[:, 0:n], AF.Sigmoid, bias=P(4)), [pf, pa], [a])
                    pf = self.next_pf()
                    for j in range(3):
                        self.op("pe", lambda h, pf=pf, cs=cs, n=n, j=j: h.matmul(pf.t[:, 0:n], g2.t[:, j, hp * 128:(hp + 1) * 128], sgl.t[:, j, cs], start=(j == 0), stop=(j == 2)), [g2, sgl], [pf])
                    self.copy("dve", g.t[:, cs], pf.t[:, 0:n], [pf], [g])
                    self.op("dve", lambda h, cs=cs: h.tensor_scalar(kk.t[:, cs], k.t[:, cs], P(5), None, ALU.mult), [k, pa], [kk])
                    self.op("dve", lambda h, cs=cs: h.tensor_tensor(t1.t[:, cs], kk.t[:, cs], kk.t[:, cs], ALU.mult), [kk], [t1])
                    pf = self.next_pf()
                    self.op("pe", lambda h, pf=pf, cs=cs, n=n: h.matmul(pf.t[:, 0:n], bones.t[:], t1.t[:, cs], start=True, stop=True), [bones, t1], [pf])
                    self.op("act", lambda h, pf=pf, cs=cs, n=n: h.activation(t1.t[:, cs], pf.t[:, 0:n], AF.Sqrt), [pf], [t1])
                    self.op("dve", lambda h, cs=cs: h.tensor_scalar(t1.t[:, cs], t1.t[:, cs], 1e-12, None, ALU.max), [t1], [t1])
                    self.op("dve", lambda h, cs=cs: h.reciprocal(t1.t[:, cs], t1.t[:, cs]), [t1], [t1])
                    self.op("dve", lambda h, cs=cs: h.tensor_tensor(kk.t[:, cs], kk.t[:, cs], t1.t[:, cs], ALU.mult), [kk, t1], [kk])
                    self.op("dve", lambda h, cs=cs: h.tensor_scalar(t1.t[:, cs], a.t[:, cs], P(6), px.t[:, 1, hp:hp + 1], ALU.mult, ALU.add), [a, pa, px], [t1])
                    self.op("dve", lambda h, cs=cs: h.tensor_tensor(kp.t[:, cs], k.t[:, cs], t1.t[:, cs], ALU.mult), [k, t1], [kp])
                    self.op("dve", lambda h, cs=cs: h.scalar_tensor_tensor(t1.t[:, cs], r.t[:, cs], P(7), kp.t[:, cs], ALU.mult, ALU.mult), [r, kp, pa], [t1])
                    pf = self.next_pf()
                    self.op("pe", lambda h, pf=pf, cs=cs, n=n: h.matmul(pf.t[:, 0:n], bones.t[:], t1.t[:, cs], start=True, stop=True), [bones, t1], [pf])
                    self.op("dve", lambda h, pf=pf, cs=cs, n=n: h.tensor_tensor(bon.t[:, cs], pf.t[:, 0:n], v.t[:, cs], ALU.mult), [pf, v], [bon])
                self.op("dve", lambda h: h.tensor_scalar(L.t[:, 1:U0C], lw.t[:, 1:U0C], -1.0, None, ALU.mult), [lw], [L])
                src, dst = L, L2
                sh_ = 1
                while sh_ < 128:
                    s3 = src.t[:, 1:2049].rearrange("p (c t) -> p c t", t=128)
                    d3 = dst.t[:, 1:2049].rearrange("p (c t) -> p c t", t=128)
                    self.op("dve", lambda h, s3=s3, d3=d3, sh_=sh_: h.tensor_tensor(d3[:, :, sh_:128], s3[:, :, sh_:128], s3[:, :, 0:128 - sh_], ALU.add), [src], [dst])
                    self.op("dve", lambda h, s3=s3, d3=d3, sh_=sh_: h.tensor_copy(d3[:, :, 0:sh_], s3[:, :, 0:sh_]), [src], [dst])
                    if sh_ < 8:
                        for c0 in (2050, 2059):
                            self.op("dve", lambda h, c0=c0, sh_=sh_, src=src, dst=dst: h.tensor_tensor(dst.t[:, c0 + sh_:c0 + 8], src.t[:, c0 + sh_:c0 + 8], src.t[:, c0:c0 + 8 - sh_], ALU.add), [src], [dst])
                            self.op("dve", lambda h, c0=c0, sh_=sh_, src=src, dst=dst: h.tensor_copy(dst.t[:, c0:c0 + sh_], src.t[:, c0:c0 + sh_]), [src], [dst])
                    else:
                        for c0 in (2050, 2059):
                            self.op("dve", lambda h, c0=c0, src=src, dst=dst: h.tensor_copy(dst.t[:, c0:c0 + 8], src.t[:, c0:c0 + 8]), [src], [dst])
                    src, dst = dst, src
                    sh_ *= 2
                Lc = src
                A_ = slice(1, U0C)
                self.op("act", lambda h: h.activation(pc.t[:, A_], Lc.t[:, A_], AF.Exp), [Lc], [pc])
                self.op("dve", lambda h: h.tensor_tensor(ar.t[:, 1, A_], r.t[:, A_], pc.t[:, A_], ALU.mult), [r, pc], [ar])
                self.op("act", lambda h: h.activation(t1.t[:, A_], Lc.t[:, A_], AF.Exp, scale=-1.0), [Lc], [t1])
                self.op("dve", lambda h: h.tensor_tensor(kT.t[:, A_], kp.t[:, A_], t1.t[:, A_], ALU.mult), [kp, t1], [kT])
                self.op("dve", lambda h: h.tensor_tensor(kp.t[:, A_], kk.t[:, A_], a.t[:, A_], ALU.mult), [kk, a], [kp])
                self.op("dve", lambda h: h.tensor_tensor(bT.t[:, A_], kp.t[:, A_], t1.t[:, A_], ALU.mult), [kp, t1], [bT])
                self.op("dve", lambda h: h.tensor_tensor(dst.t[:, A_], Lc.t[:, A_], lw.t[:, A_], ALU.add), [Lc, lw], [dst])
                self.op("act", lambda h: h.activation(t1.t[:, A_], dst.t[:, A_], AF.Exp), [dst], [t1])
                self.op("dve", lambda h: h.scalar_tensor_tensor(ar.t[:, 0, A_], kk.t[:, A_], -1.0, t1.t[:, A_], ALU.mult, ALU.mult), [kk, t1], [ar])
                self.op("act", lambda h: h.activation(vb.t[:, A_], v.t[:, A_], AF.Copy), [v], [vb])

                for (c00, ntok, si, oc0) in SEGS:
                    if si == 0:
                        self.op("dve", lambda h: h.memset(Tst.t[:], 0.0), [], [Tst])
                    else:
                        self.dma("sp", zs.t[:].rearrange("v (h k) -> v h k", h=2), st_in.t[si - 1, 2 * hp:2 * hp + 2].rearrange("h v k -> v h k"), [st_in], [zs])
                        pf = self.next_pf()
                        self.op("pe", lambda h, pf=pf: h.transpose(pf.t[:, 0:64], zs.t[:, :], self.identf.t[0:64, 0:64]), [zs, self.identf], [pf])
                        self.copy("dve", Tst.t[:], pf.t[:, 0:64], [pf], [Tst])
                    self.op("act", lambda h: h.activation(Tb.t[:], Tst.t[:], AF.Copy), [Tst], [Tb])
                    for ch in range(0, ntok, 128):
                        C = min(128, ntok - ch)
                        cs = slice(c00 + ch, c00 + ch + C)
                        tmt = tm[(ch // 128) % 2]
                        pb = self.next_pb()
                        for j, srcT in enumerate((vb, bT, kT)):
                            self.op("pe", lambda h, pb=pb, j=j, srcT=srcT, cs=cs, C=C: h.transpose(pb.t[0:C, j * 128:(j + 1) * 128], srcT.t[:, cs], self.identb.t[:, :]), [srcT, self.identb], [pb])
                        self.copy("act", tmt.t[0:C, :, :], pb.t[0:C, 0:384].rearrange("p (j c) -> p j c", c=128), [pb], [tmt])
                        for hh in range(2):
                            hs = slice(hh * 64, hh * 64 + 64)
                            Vh, Bh, Kh = tmt.t[0:C, 0, hs], tmt.t[0:C, 1, :], tmt.t[0:C, 2, :]
                            arh = ar.t[hs, :, cs]
                            p1, p2, p3 = self.next_pf(), self.next_pf(), self.next_pf()
                            self.op("pe", lambda h, p1=p1, arh=arh, hs=hs, cs=cs, C=C: h.matmul(p1.t[0:C, 0:2 * C].rearrange("p (a t) -> p a t", a=2), bT.t[hs, cs], arh, start=True, stop=True), [bT, ar], [p1])
                            self.op("pe", lambda h, p2=p2, arh=arh, hs=hs, cs=cs, C=C: h.matmul(p2.t[0:C, 0:2 * C].rearrange("p (a t) -> p a t", a=2), kT.t[hs, cs], arh, start=True, stop=True), [kT, ar], [p2])
                            self.op("pe", lambda h, p3=p3, hs=hs, cs=cs, C=C: h.matmul(p3.t[0:C, 0:C], ar.t[hs, 0, cs], bT.t[hs, cs], start=True, stop=True), [bT, ar], [p3])
                            N_, NT_, X_ = sm["N"], sm["NT"], sm["X"]
                            self.op("dve", lambda h, p1=p1, C=C: h.tensor_tensor(sm["N"].t[0:C, 0:C], p1.t[0:C, 0:C], MS[0:C, 0:C], ALU.mult), [p1, mk], [sm["N"]])
                            self.op("dve", lambda h, p1=p1, C=C: h.tensor_tensor(sm["Abr"].t[0:C, 0:C], p1.t[0:C, C:2 * C], MI[0:C, 0:C], ALU.mult), [p1, mk], [sm["Abr"]])
                            self.op("dve", lambda h, p2=p2, C=C: h.tensor_tensor(sm["Aak"].t[0:C, 0:C], p2.t[0:C, 0:C], MS[0:C, 0:C], ALU.mult), [p2, mk], [sm["Aak"]])
                            self.op("dve", lambda h, p2=p2, C=C: h.tensor_tensor(sm["Akr"].t[0:C, 0:C], p2.t[0:C, C:2 * C], MI[0:C, 0:C], ALU.mult), [p2, mk], [sm["Akr"]])
                            self.op("dve", lambda h, p3=p3, C=C: h.tensor_tensor(sm["NT"].t[0:C, 0:C], p3.t[0:C, 0:C], ML[0:C, 0:C], ALU.mult), [p3, mk], [sm["NT"]])
                            self.op("dve", lambda h, C=C: h.tensor_tensor(sm["X"].t[0:C, 0:C], sm["N"].t[0:C, 0:C], self.identb.t[0:C, 0:C], ALU.add), [sm["N"], self.identb], [sm["X"]])
                            pw, pwT, X, pw2, pwT2, X2 = sm["N"], sm["NT"], sm["X"], sm["N2"], sm["NT2"], sm["X2"]
                            nst = 0
                            while (1 << (nst + 1)) < C:
                                nst += 1
                            for it in range(nst):
                                q1, q2 = self.next_pf(), self.next_pf()
                                self.op("pe", lambda h, q1=q1, pw=pw, pwT=pwT, C=C: h.matmul(q1.t[0:C, 0:C], pwT.t[0:C, 0:C], pw.t[0:C, 0:C], start=True, stop=True), [pw, pwT], [q1])
                                self.op("pe", lambda h, q2=q2, pw=pw, pwT=pwT, C=C: h.matmul(q2.t[0:C, 0:C], pw.t[0:C, 0:C], pwT.t[0:C, 0:C], start=True, stop=True), [pw, pwT], [q2])
                                self.copy("dve", pw2.t[0:C, 0:C], q1.t[0:C, 0:C], [q1], [pw2])
                                self.copy("act", pwT2.t[0:C, 0:C], q2.t[0:C, 0:C], [q2], [pwT2])
                                q3 = self.next_pf()
                                self.op("pe", lambda h, q3=q3, X=X, C=C: h.matmul(q3.t[0:C, 0:C], self.identb.t[0:C, 0:C], X.t[0:C, 0:C], start=True, stop=False), [X, self.identb], [q3])
                                self.op("pe", lambda h, q3=q3, X=X, pwT2=pwT2, C=C: h.matmul(q3.t[0:C, 0:C], pwT2.t[0:C, 0:C], X.t[0:C, 0:C], start=False, stop=True), [X, pwT2], [q3])
                                self.copy("dve", X2.t[0:C, 0:C], q3.t[0:C, 0:C], [q3], [X2])
                                pw, pw2 = pw2, pw
                                pwT, pwT2 = pwT2, pwT
                                X, X2 = X2, X
                            q = self.next_pf()
                            self.op("pe", lambda h, q=q, hs=hs, cs=cs, C=C: h.matmul(q.t[0:C, 0:64], ar.t[hs, 0, cs], Tb.t[hs, :], start=True, stop=False), [ar, Tb], [q])
                            self.op("pe", lambda h, q=q, Vh=Vh, C=C: h.matmul(q.t[0:C, 0:64], sm["Aak"].t[0:C, 0:C], Vh, start=False, stop=True), [sm["Aak"], tmt], [q])
                            self.copy("act", sm["W1"].t[0:C, 0:64], q.t[0:C, 0:64], [q], [sm["W1"]])
                            q = self.next_pf()
                            self.op("pe", lambda h, q=q, X=X, C=C: h.matmul(q.t[0:C, 0:64], X.t[0:C, 0:C], sm["W1"].t[0:C, 0:64], start=True, stop=True), [X, sm["W1"]], [q])
                            self.copy("dve", sm["U"].t[0:C, 0:64], q.t[0:C, 0:64], [q], [sm["U"]])
                            q = self.next_pf()
                            self.op("pe", lambda h, q=q, hs=hs, cs=cs, C=C: h.matmul(q.t[0:C, 0:64], ar.t[hs, 1, cs], Tb.t[hs, :], start=True, stop=False), [ar, Tb], [q])
                            self.op("pe", lambda h, q=q, C=C: h.matmul(q.t[0:C, 0:64], sm["Abr"].t[0:C, 0:C], sm["U"].t[0:C, 0:64], start=False, stop=False), [sm["Abr"], sm["U"]], [q])
                            self.op("pe", lambda h, q=q, Vh=Vh, C=C: h.matmul(q.t[0:C, 0:64], sm["Akr"].t[0:C, 0:C], Vh, start=False, stop=True), [sm["Akr"], tmt], [q])
                            self.copy("act", ytm.t[0:C, hs], q.t[0:C, 0:64], [q], [ytm])
                            q = self.next_pf()
                            self.op("pe", lambda h, q=q, Bh=Bh, C=C: h.matmul(q.t[:, 0:64], Bh, sm["U"].t[0:C, 0:64], start=True, stop=False), [tmt, sm["U"]], [q])
                            self.op("pe", lambda h, q=q, Kh=Kh, Vh=Vh: h.matmul(q.t[:, 0:64], Kh, Vh, start=False, stop=True), [tmt], [q])
                            self.op("dve", lambda h, q=q, hs=hs: h.tensor_tensor(tsum.t[hs, :], q.t[hs, 0:64], Tst.t[hs, :], ALU.add), [q, Tst], [tsum])
                            pcol = c00 + ch + C - 1
                            self.op("act", lambda h, hs=hs, pcol=pcol: h.activation(Tst.t[hs, :], tsum.t[hs, :], AF.Copy, scale=pc.t[hs, pcol:pcol + 1]), [tsum, pc], [Tst])
                            self.op("dve", lambda h, hs=hs: h.tensor_copy(Tb.t[hs, :], Tst.t[hs, :]), [Tst], [Tb])
                        y3 = ytm.t[0:C, :].rearrange("p (h v) -> p h v", h=2)
                        self.op("dve", lambda h, y3=y3, C=C: h.reduce_sum(gs.t[0:C, 0:2], y3, AX.X), [ytm], [gs])
                        self.op("act", lambda h, C=C: h.activation(gs.t[0:C, 0:2], gs.t[0:C, 0:2], AF.Copy, scale=1.0 / 64), [gs], [gs])
                        self.op("dve", lambda h, y3=y3, C=C: h.tensor_tensor(y3, y3, gs.t[0:C, 0:2].unsqueeze(2).to_broadcast([C, 2, 64]), ALU.subtract), [ytm, gs], [ytm])
                        t3 = self._dtmp.t[0:C, 0:128].rearrange("p (h v) -> p h v", h=2)
                        self.op("act", lambda h, y3=y3, t3=t3: h.activation(t3, y3, AF.Square), [ytm], [self._dtmp])
                        self.op("dve", lambda h, t3=t3, C=C: h.reduce_sum(gs.t[0:C, 2:4], t3, AX.X), [self._dtmp], [gs])
                        self.op("act", lambda h, C=C: h.activation(gs.t[0:C, 2:4], gs.t[0:C, 2:4], AF.Sqrt, bias=64e-5, scale=1.0 / 64), [gs], [gs])
                        self.op("dve", lambda h, C=C: h.reciprocal(gs.t[0:C, 2:4], gs.t[0:C, 2:4]), [gs], [gs])
                        self.op("dve", lambda h, y3=y3, C=C: h.tensor_tensor(y3, y3, gs.t[0:C, 2:4].unsqueeze(2).to_broadcast([C, 2, 64]), ALU.mult), [ytm, gs], [ytm])
                        q = self.next_pf()
                        self.op("pe", lambda h, q=q, C=C: h.transpose(q.t[:, 0:C], ytm.t[0:C, :], self.identf.t[0:C, 0:C]), [ytm, self.identf], [q])
                        self.op("dve", lambda h, q=q, cs=cs, C=C: h.tensor_scalar(t1.t[:, cs], q.t[:, 0:C], P(8), P(9), ALU.mult, ALU.add), [q, pa], [t1])
                        self.op("dve", lambda h, cs=cs: h.tensor_tensor(t1.t[:, cs], t1.t[:, cs], bon.t[:, cs], ALU.add), [t1, bon], [t1])
                        oc = oc0 + ch
                        self.op("dve", lambda h, cs=cs, oc=oc, C=C: h.tensor_tensor(opo.t[:, oc:oc + C], t1.t[:, cs], g.t[:, cs], ALU.mult), [t1, g], [opo])
                    q = self.next_pf()
                    self.op("pe", lambda h, q=q: h.transpose(q.t[0:64, 0:128], Tst.t[:, :], self.identf.t[:, :]), [Tst, self.identf], [q])
                    self.copy("dve", wout.t[:, :], q.t[0:64, 0:128], [q], [wout])
                    self.dma("sp", o_wkv.t[si, 2 * hp:2 * hp + 2].rearrange("h v k -> v h k"), wout.t[:].rearrange("v (h k) -> v h k", h=2), [wout], [o_wkv])
                self.dma("sp", self.opT0.t[hp * 128:(hp + 1) * 128, :], opo.t[:], [opo], [self.opT0])

            for hp in range(12):
                head_pair(hp)

    def phase_memattn(self, li, qsrc, qc0, segs, opT, row0):
        cmem = self.ins["cmem"] if "cmem" in self.ins else self.inp("cmem", [2, 2, NMEM, 2, 2, 256])
        gq_in = self.ins["g_qmem"] if "g_qmem" in self.ins else self.inp("g_qmem", [128, 2, 2])
        with self.phase() as st:
            gq = self.sb(st, "ma_gq", [128, 2, 2], F32)
            self.dma("sp", gq.t[:], gq_in.t, [gq_in], [gq])
            kf = self.sb(st, "ma_kf", [128, 2, 256], F32)
            vf = self.sb(st, "ma_vf", [128, 2, 256], F32)
            kb_ = self.sb(st, "ma_kb", [128, 2, 256], BF16)
            vb_ = self.sb(st, "ma_vb", [128, 2, 256], BF16)
            KT = self.sb(st, "ma_KT", [128, 2, 256], BF16)
            qf = self.sb(st, "ma_qf", [128, 2, 512], F32)
            sq = self.sb(st, "ma_sq", [128, 2, 512], F32)
            rs = self.sb(st, "ma_rs", [128, 512], F32)
            qn = self.sb(st, "ma_qn", [128, 2, 512], BF16)
            E = self.sb(st, "ma_E", [128, 2, 512], BF16)
            rd = self.sb(st, "ma_rd", [128, 512], F32)
            ost = [self.sb(st, "ma_o%d" % i, [128, 512], BF16) for i in range(2)]
            self._oi = 0

            def block(hl, c0, n, d0):
                for dc in range(2):
                    self.dma("sp", qf.t[:, dc, 0:n], qsrc.t[qc0 + 2 * hl + dc, :, c0:c0 + n], [qsrc], [qf])
                self.op("act", lambda h: h.activation(sq.t[:, :, 0:n], qf.t[:, :, 0:n], AF.Square), [qf], [sq])
                ps = self.next_pf()
                for dc in range(2):
                    self.op("pe", lambda h, dc=dc: h.matmul(ps.t[:, 0:n], self.ones_f.t[:, :], sq.t[:, dc, 0:n], start=(dc == 0), stop=(dc == 1)), [self.ones_f, sq], [ps])
                self.op("act", lambda h: h.activation(rs.t[:, 0:n], ps.t[:, 0:n], AF.Sqrt, bias=EPS, scale=1.0 / 256), [ps], [rs])
                self.op("dve", lambda h: h.reciprocal(rs.t[:, 0:n], rs.t[:, 0:n]), [rs], [rs])
                for dc in range(2):
                    self.op("dve", lambda h, dc=dc: h.scalar_tensor_tensor(qn.t[:, dc, 0:n], qf.t[:, dc, 0:n], gq.t[:, li, dc:dc + 1], rs.t[:, 0:n], ALU.mult, ALU.mult), [qf, gq, rs], [qn])
                for mb in range(2):
                    pss = self.next_pf()
                    for dc in range(2):
                        self.op("pe", lambda h, mb=mb, dc=dc, pss=pss: h.matmul(pss.t[:, 0:n], KT.t[:, dc, mb * 128:(mb + 1) * 128], qn.t[:, dc, 0:n], start=(dc == 0), stop=(dc == 1)), [KT, qn], [pss])
                    self.op("act", lambda h, mb=mb, pss=pss: h.activation(E.t[:, mb, 0:n], pss.t[:, 0:n], AF.Exp, scale=1.0 / 16), [pss], [E])
                pd = self.next_pf()
                for mb in range(2):
                    self.op("pe", lambda h, mb=mb: h.matmul(pd.t[:, 0:n], self.ones_b.t[:, :], E.t[:, mb, 0:n], start=(mb == 0), stop=(mb == 1)), [self.ones_b, E], [pd])
                self.op("dve", lambda h: h.reciprocal(rd.t[:, 0:n], pd.t[:, 0:n]), [pd], [rd])
                for dc2 in range(2):
                    pn = self.next_pf()
                    for mb in range(2):
                        self.op("pe", lambda h, mb=mb, dc2=dc2, pn=pn: h.matmul(pn.t[:, 0:n], vb_.t[:, mb, dc2 * 128:(dc2 + 1) * 128], E.t[:, mb, 0:n], start=(mb == 0), stop=(mb == 1)), [vb_, E], [pn])
                    o = ost[self._oi % 2]
                    self._oi += 1
                    self.op("dve", lambda h, pn=pn, o=o: h.tensor_tensor(o.t[:, 0:n], pn.t[:, 0:n], rd.t[:, 0:n], ALU.mult), [pn, rd], [o])
                    r0 = row0 + hl * 256 + dc2 * 128
                    self.dma("sp", opT.t[r0:r0 + 128, d0:d0 + n], o.t[:, 0:n], [o], [opT])

            def kvset(kvi, hl):
                if kvi == 0:
                    ksrc, vsrc, srcT = self.mkz.t[li, :, 0, hl * 256:(hl + 1) * 256], self.mkz.t[li, :, 1, hl * 256:(hl + 1) * 256], self.mkz
                else:
                    ksrc, vsrc, srcT = cmem.t[li, kvi - 1, :, 0, hl, :], cmem.t[li, kvi - 1, :, 1, hl, :], cmem
                self.dma("sp", kf.t[:], ksrc.rearrange("(mb p) d -> p mb d", p=128), [srcT], [kf])
                self.dma("sp", vf.t[:], vsrc.rearrange("(mb p) d -> p mb d", p=128), [srcT], [vf])
                self.copy("act", kb_.t[:], kf.t[:], [kf], [kb_])
                self.copy("dve", vb_.t[:], vf.t[:], [vf], [vb_])
                pb = self.next_pb()
                for dc in range(2):
                    for mb in range(2):
                        self.op("pe", lambda h, dc=dc, mb=mb: h.transpose(pb.t[:, (dc * 2 + mb) * 128:(dc * 2 + mb + 1) * 128], kb_.t[:, mb, dc * 128:(dc + 1) * 128], self.identb.t[:, :]), [kb_, self.identb], [pb])
                self.copy("dve", KT.t[:].rearrange("p a b -> p (a b)"), pb.t[:, 0:512], [pb], [KT])
                for (c0, n, kv_i, d0) in segs:
                    if kv_i != kvi:
                        continue
                    for o in range(0, n, 512):
                        block(hl, c0 + o, min(512, n - o), d0 + o)

            for kvi in range(3):
                for hl in range(2):
                    kvset(kvi, hl)

    def load_own(self, x, tmp, src, nk, t0, n, srcT):
        ca, cb = (t0, HALF + t0) if t0 < HALF else (2 * HALF, 2 * HALF + NS)
        self.dma("sp", x.t[:, 0:nk, 0:n], src.t[:, ca:ca + n].rearrange("(k p) t -> p k t", p=128), [srcT], [x])
        self.dma("pool", tmp.t[:, 0:nk, 0:n], src.t[:, cb:cb + n].rearrange("(k p) t -> p k t", p=128), [srcT], [tmp])
        self.op("dve", lambda h: h.tensor_scalar(x.t[:, 0:nk, 0:n], x.t[:, 0:nk, 0:n], self.sel.t[:, 0:1], None, ALU.mult), [x, self.sel], [x])
        self.op("dve", lambda h: h.scalar_tensor_tensor(x.t[:, 0:nk, 0:n], tmp.t[:, 0:nk, 0:n], self.sel.t[:, 1:2], x.t[:, 0:nk, 0:n], ALU.mult, ALU.add), [x, tmp, self.sel], [x])

    def phase_outproj(self, opT, nrows, wname, xin, xout, tag):
        opP = self.dram(tag + "_opP", [2 * nrows, 2 * NTOK], BF16)
        self.pair_gather(opT, opP, nrows, 2 * NTOK, 2)
        nk = 2 * nrows // 128
        with self.phase() as st:
            xa = [self.sb(st, tag + "_x%d" % i, [128, nk, 512], BF16) for i in range(2)]
            xt = self.sb(st, tag + "_xt", [128, nk, 512], BF16)
            wbufs = [self.sb(st, tag + "_w%d" % i, [128, nk, 512], BF16) for i in range(2)]
            res = [self.sb(st, tag + "_r%d" % i, [128, 512], F32) for i in range(3)]
            xr = [self.sb(st, tag + "_xr%d" % i, [128, 512], F32) for i in range(3)]
            self._ri = 0
            for bi, (t0, n) in enumerate(TBLK):
                x = xa[bi % 2]
                self.load_own(x, xt, opP, nk, t0, n, opP)

                def sink(ci, pf, m, t0=t0, n=n):
                    r, xx = res[self._ri % 3], xr[self._ri % 3]
                    self._ri += 1
                    self.dma("pool", xx.t[:, 0:n], xin.t[:, ci, t0:t0 + n], [xin], [xx])
                    self.op("dve", lambda h: h.tensor_tensor(r.t[:, 0:n], pf.t[:, 0:n], xx.t[:, 0:n], ALU.add), [pf, xx], [r])
                    self.dma("sp", xout.t[:, ci, t0:t0 + n], r.t[:, 0:n], [r], [xout])
                self.gemm_fm(st, x, n, wname, [(i * 512, [128] * 4) for i in range(8)], nk, sink, tag, wbufs=wbufs)

    def phase_ffn(self, li, xin, xout):
        g_in = self.ins["g_ffn"] if "g_ffn" in self.ins else self.inp("g_ffn", [2, 128, KC])
        NJ = DFF // 128
        TB = [(0, 256), (256, 256), (512, 256), (768, 256), (1024, NS)]
        with self.phase() as st:
            gain = self.sb(st, "ff_g", [128, KC], F32)
            self.dma("sp", gain.t[:], g_in.t[li], [g_in], [gain])
            x = self.sb(st, "ff_x", [128, KC, 256], F32)
            hT = self.sb(st, "ff_h", [128, KC, 256], BF16)
            aT = self.sb(st, "ff_a", [128, NJ, 256], BF16)
            sq = [self.sb(st, "ff_sq%d" % i, [128, 256], F32) for i in range(2)]
            rs = self.sb(st, "ff_rs", [128, 256], F32)
            sg = [self.sb(st, "ff_sg%d" % i, [128, 256], F32) for i in range(2)]
            res = [self.sb(st, "ff_r%d" % i, [128, 256], F32) for i in range(3)]
            w1 = [self.sb(st, "ff_w1%d" % i, [128, KC, 256], BF16) for i in range(2)]
            w2 = [self.sb(st, "ff_w2%d" % i, [128, NJ, 128], BF16) for i in range(2)]
            self._ri = 0
            for (t0, n) in TB:
                self.dma("sp", x.t[:, :, 0:n], xin.t[:, :, t0:t0 + n], [xin], [x])
                ps = self.next_pf()
                for k in range(KC):
                    q = sq[k % 2]
                    self.op("act", lambda h, k=k, q=q, n=n: h.activation(q.t[:, 0:n], x.t[:, k, 0:n], AF.Square), [x], [q])
                    self.op("pe", lambda h, k=k, q=q, n=n, ps=ps: h.matmul(ps.t[:, 0:n], self.ones_f.t[:, :], q.t[:, 0:n], start=(k == 0), stop=(k == KC - 1)), [self.ones_f, q], [ps])
                self.op("act", lambda h, n=n, ps=ps: h.activation(rs.t[:, 0:n], ps.t[:, 0:n], AF.Sqrt, bias=EPS, scale=1.0 / D), [ps], [rs])
                self.op("dve", lambda h, n=n: h.reciprocal(rs.t[:, 0:n], rs.t[:, 0:n]), [rs], [rs])
                for k in range(KC):
                    self.op("dve", lambda h, k=k, n=n: h.scalar_tensor_tensor(hT.t[:, k, 0:n], x.t[:, k, 0:n], gain.t[:, k:k + 1], rs.t[:, 0:n], ALU.mult, ALU.mult), [x, gain, rs], [hT])

                def sink1(ci, pf, m, n=n):
                    j, up = ci // 2, ci % 2
                    s_ = sg[j % 2]
                    if not up:
                        self.op("act", lambda h: h.activation(s_.t[:, 0:n], pf.t[:, 0:n], AF.Silu), [pf], [s_])
                    else:
                        self.op("dve", lambda h: h.tensor_tensor(aT.t[:, j, 0:n], s_.t[:, 0:n], pf.t[:, 0:n], ALU.mult), [s_, pf], [aT])
                self.gemm_fm(st, hT, n, "w_fi%d" % li, [(j * 256, [128, 128]) for j in range(NJ)], KC, sink1, "ff1", wbufs=w1, pm=True)

                def sink2(ci, pf, m, t0=t0, n=n):
                    r = res[self._ri % 3]
                    self._ri += 1
                    self.op("dve", lambda h: h.tensor_tensor(r.t[:, 0:n], pf.t[:, 0:n], x.t[:, ci, 0:n], ALU.add), [pf, x], [r])
                    self.dma("sp", xout.t[:, ci, t0:t0 + n], r.t[:, 0:n], [r], [xout])
                self.gemm_fm(st, aT, n, "w_fo%d" % li, [(i * 128, [128]) for i in range(KC)], NJ, sink2, "ff2", wbufs=w2, pm=True)

    def phase_l1_prep(self, xin):
        TB = [(0, 256), (256, 256), (512, 256), (768, 256), (1024, NS)]
        with self.phase() as st:
            gain = self.sb(st, "p1_g", [128, KC], F32)
            self.dma("sp", gain.t[:], self.g_mix.t[1], [self.g_mix], [gain])
            x = self.sb(st, "p1_x", [128, KC, 256], F32)
            hTs = [self.sb(st, "p1_h%d" % i, [128, KC, 256], BF16) for i in range(2)]
            sq = [self.sb(st, "p1_sq%d" % i, [128, 256], F32) for i in range(2)]
            rs = self.sb(st, "p1_rs", [128, 256], F32)
            for bi, (t0, n) in enumerate(TB):
                hT = hTs[bi % 2]
                self.dma("sp", x.t[:, :, 0:n], xin.t[:, :, t0:t0 + n], [xin], [x])
                ps = self.next_pf()
                for k in range(KC):
                    q = sq[k % 2]
                    self.op("act", lambda h, k=k, q=q, n=n: h.activation(q.t[:, 0:n], x.t[:, k, 0:n], AF.Square), [x], [q])
                    self.op("pe", lambda h, k=k, q=q, n=n, ps=ps: h.matmul(ps.t[:, 0:n], self.ones_f.t[:, :], q.t[:, 0:n], start=(k == 0), stop=(k == KC - 1)), [self.ones_f, q], [ps])
                self.op("act", lambda h, n=n, ps=ps: h.activation(rs.t[:, 0:n], ps.t[:, 0:n], AF.Sqrt, bias=EPS, scale=1.0 / D), [ps], [rs])
                self.op("dve", lambda h, n=n: h.reciprocal(rs.t[:, 0:n], rs.t[:, 0:n]), [rs], [rs])
                for k in range(KC):
                    self.op("dve", lambda h, k=k, n=n, hT=hT: h.scalar_tensor_tensor(hT.t[:, k, 0:n], x.t[:, k, 0:n], gain.t[:, k:k + 1], rs.t[:, 0:n], ALU.mult, ALU.mult), [x, gain, rs], [hT])
                self.dma("sp", self.hT_own.t[:, t0:t0 + n].rearrange("(k p) t -> p k t", p=128), hT.t[:, :, 0:n], [hT], [self.hT_own])
        self.hT_pieces = self.pair_gather(self.hT_own, self.hT_pair, D, NTOK, 2)

    def phase_l1_inproj(self):
        g_in = self.inp("g_qkb", [128, 2])
        NT2 = 2 * NTOK
        self.QT = self.dram("QT", [12, 128, NT2], BF16)
        self.KTb = self.dram("KTb", [12, 128, NT2], BF16)
        self.KTf = self.dram("KTf", [12, 128, NT2], F32)
        self.Vtm = self.dram("Vtm", [3, NT2, 512], F32)
        self.U1m = self.dram("U1m", [4, 128, NT2], F32)
        blocks = [(0, 0, 512, 0), (0, 512, 512, 512), (1, 0, 512, 1024), (1, 512, 512, 1536)]
        with self.phase() as st:
            gqk = self.sb(st, "i1_g", [128, 2], F32)
            self.dma("sp", gqk.t[:], g_in.t, [g_in], [gqk])
            xT = [self.sb(st, "i1_x%d" % i, [128, KC, 512], BF16) for i in range(2)]
            wbufs = [self.sb(st, "i1_w%d" % i, [128, KC, 512], BF16) for i in range(2)]
            sq = [self.sb(st, "i1_sq%d" % i, [128, 512], F32) for i in range(2)]
            rs = [self.sb(st, "i1_rs%d" % i, [128, 512], F32) for i in range(2)]
            o16 = [self.sb(st, "i1_ob%d" % i, [128, 512], BF16) for i in range(3)]
            o32 = [self.sb(st, "i1_of%d" % i, [128, 512], F32) for i in range(3)]
            self._ci = 0

            def normed(qk, hd, pf, ntok, c0):
                i_ = self._ci
                self._ci += 1
                s_, r_ = sq[i_ % 2], rs[i_ % 2]
                self.op("act", lambda h: h.activation(s_.t[:, 0:ntok], pf.t[:, 0:ntok], AF.Square), [pf], [s_])
                ps = self.next_pf()
                self.op("pe", lambda h: h.matmul(ps.t[:, 0:ntok], self.ones_f.t[:, :], s_.t[:, 0:ntok], start=True, stop=True), [self.ones_f, s_], [ps])
                self.op("act", lambda h: h.activation(r_.t[:, 0:ntok], ps.t[:, 0:ntok], AF.Sqrt, bias=EPS, scale=1.0 / 128), [ps], [r_])
                self.op("dve", lambda h: h.reciprocal(r_.t[:, 0:ntok], r_.t[:, 0:ntok]), [r_], [r_])
                ob = o16[i_ % 3]
                if qk == 0:
                    self.op("dve", lambda h: h.scalar_tensor_tensor(ob.t[:, 0:ntok], pf.t[:, 0:ntok], gqk.t[:, 0:1], r_.t[:, 0:ntok], ALU.mult, ALU.mult), [pf, gqk, r_], [ob])
                    self.dma("sp", self.QT.t[hd, :, c0:c0 + ntok], ob.t[:, 0:ntok], [ob], [self.QT])
                else:
                    of = o32[i_ % 3]
                    self.op("dve", lambda h: h.scalar_tensor_tensor(of.t[:, 0:ntok], pf.t[:, 0:ntok], gqk.t[:, 1:2], r_.t[:, 0:ntok], ALU.mult, ALU.mult), [pf, gqk, r_], [of])
                    self.dma("sp", self.KTf.t[hd, :, c0:c0 + ntok], of.t[:, 0:ntok], [of], [self.KTf])
                    self.copy("act", ob.t[:, 0:ntok], of.t[:, 0:ntok], [of], [ob])
                    self.dma("sp", self.KTb.t[hd, :, c0:c0 + ntok], ob.t[:, 0:ntok], [ob], [self.KTb])

            def run_block(bi, loads, ntok, c0):
                x = xT[bi % 2]
                for (dst0, r, t0, n) in loads:
                    for (r0, r1) in self.hT_pieces:
                        base = 2 * r0 + r * (r1 - r0)
                        src = self.hT_pair.t[base:base + (r1 - r0), t0:t0 + n].rearrange("(k p) t -> p k t", p=128)
                        self.dma("sp", x.t[:, r0 // 128:r1 // 128, dst0:dst0 + n], src, [self.hT_pair], [x])
                for pi in range(10):
                    w = wbufs[pi % 2]
                    self.load_wpanel("sp", w, "w_in_b", KC, pi * 512, 512)
                    if 6 <= pi < 9:
                        g = pi - 6
                        for tb in range(0, ntok, 128):
                            nt = min(128, ntok - tb)
                            pf = self.next_pf()
                            for k in range(KC):
                                self.op("pe", lambda h, pf=pf, k=k, w=w, tb=tb, nt=nt: h.matmul(pf.t[0:nt, 0:512], x.t[:, k, tb:tb + nt], w.t[:, k, 0:512], start=(k == 0), stop=(k == KC - 1)), [x, w], [pf])
                            of = o32[self._ci % 3]
                            self._ci += 1
                            self.copy(self.evac_engine(), of.t[0:nt, :], pf.t[0:nt, 0:512], [pf], [of])
                            self.dma("sp", self.Vtm.t[g, c0 + tb:c0 + tb + nt, :], of.t[0:nt, :], [of], [self.Vtm])
                    else:
                        for m in range(4):
                            pf = self.next_pf()
                            for k in range(KC):
                                self.op("pe", lambda h, pf=pf, k=k, w=w, m=m: h.matmul(pf.t[:, 0:ntok], w.t[:, k, m * 128:(m + 1) * 128], x.t[:, k, 0:ntok], start=(k == 0), stop=(k == KC - 1)), [x, w], [pf])
                            if pi == 9:
                                of = o32[self._ci % 3]
                                self._ci += 1
                                self.copy(self.evac_engine(), of.t[:, 0:ntok], pf.t[:, 0:ntok], [pf], [of])
                                self.dma("sp", self.U1m.t[m, :, c0:c0 + ntok], of.t[:, 0:ntok], [of], [self.U1m])
                            else:
                                normed(0 if pi < 3 else 1, (pi % 3) * 4 + m, pf, ntok, c0)

            for bi, (r, t0, n, c0) in enumerate(blocks):
                run_block(bi, [(0, r, t0, n)], n, c0)
            run_block(4, [(0, 0, HALF, NS), (NS, 1, HALF, NS)], 2 * NS, 2048)

    def phase_dswa_prompt(self):
        cm_in = self.inp("c_amask", [128, 256])
        self.opT1 = self.dram("opT1", [1024, 2 * NTOK], BF16)
        SC = 128 ** -0.5
        with self.phase() as st:
            mk = self.sb(st, "dp_mk", [128, 256], BF16)
            self.dma("pool", mk.t[:], cm_in.t, [cm_in], [mk])
            NUM = self.sb(st, "dp_num", [128, 2048], F32)
            DEN = self.sb(st, "dp_den", [128, 2048], F32)
            qts = [self.sb(st, "dp_q%d" % i, [128, 2048], BF16) for i in range(2)]
            kts = [self.sb(st, "dp_k%d" % i, [128, 2048], BF16) for i in range(2)]
            vf = self.sb(st, "dp_vf", [128, 16, 128], F32)
            vbs = [self.sb(st, "dp_v%d" % i, [128, 16, 128], BF16) for i in range(2)]
            ers = [self.sb(st, "dp_er%d" % i, [128, 256], BF16) for i in range(2)]
            Es = [self.sb(st, "dp_E%d" % i, [128, 256], BF16) for i in range(2)]
            ob = self.sb(st, "dp_ob", [128, 2048], BF16)
            rd = self.sb(st, "dp_rd", [128, 2048], F32)
            self._ci = 0

            def block(g, q_, k_, v_, dil, nb, r, n):
                qv = q_.t[:, :].rearrange("p (n j r) -> p r n j", j=128, r=dil)
                kv = k_.t[:, :].rearrange("p (n j r) -> p r n j", j=128, r=dil)
                NUMv = NUM.t[:, :].rearrange("p (n j r) -> p r n j", j=128, r=dil)[:, r, n, :]
                DENv = DEN.t[:, :].rearrange("p (n j r) -> p r n j", j=128, r=dil)[:, r, n, :]
                er, E = ers[self._ci % 2], Es[self._ci % 2]
                self._ci += 1
                w = 256 if n > 0 else 128
                ps = self.next_pf()
                self.op("pe", lambda h: h.matmul(ps.t[:, 0:128], kv[:, r, n, :], qv[:, r, n, :], start=True, stop=True), [k_, q_], [ps])
                if n > 0:
                    self.op("pe", lambda h: h.matmul(ps.t[:, 128:256], kv[:, r, n - 1, :], qv[:, r, n, :], start=True, stop=True), [k_, q_], [ps])
                self.op("act", lambda h: h.activation(er.t[:, 0:w], ps.t[:, 0:w], AF.Exp, scale=SC), [ps], [er])
                self.op("dve", lambda h: h.tensor_tensor(E.t[:, 0:w], er.t[:, 0:w], mk.t[:, 0:w], ALU.mult), [er, mk], [E])
                pn, pd = self.next_pf(), self.next_pf()
                bi = r * nb + n
                self.op("pe", lambda h: h.matmul(pn.t[:, 0:128], v_.t[:, bi, :], E.t[:, 0:128], start=True, stop=(n == 0)), [v_, E], [pn])
                if n > 0:
                    self.op("pe", lambda h: h.matmul(pn.t[:, 0:128], v_.t[:, bi - 1, :], E.t[:, 128:256], start=False, stop=True), [v_, E], [pn])
                self.op("pe", lambda h: h.matmul(pd.t[:, 0:128], self.ones_b.t[:, :], E.t[:, 0:128], start=True, stop=(n == 0)), [self.ones_b, E], [pd])
                if n > 0:
                    self.op("pe", lambda h: h.matmul(pd.t[:, 0:128], self.ones_b.t[:, :], E.t[:, 128:256], start=False, stop=True), [self.ones_b, E], [pd])
                if g == 0:
                    self.op("act", lambda h: h.activation(NUMv, pn.t[:, 0:128], AF.Copy), [pn], [NUM])
                    self.op("dve", lambda h: h.tensor_copy(DENv, pd.t[:, 0:128]), [pd], [DEN])
                else:
                    self.op("dve", lambda h: h.tensor_tensor(NUMv, NUMv, pn.t[:, 0:128], ALU.add), [pn, NUM], [NUM])
                    self.op("dve", lambda h: h.tensor_tensor(DENv, DENv, pd.t[:, 0:128], ALU.add), [pd, DEN], [DEN])

            def head(i, g, cnt):
                hd = g * 4 + i
                dil = (1, 4, 16)[g]
                nb = 16 // dil
                q_, k_, v_ = qts[cnt % 2], kts[cnt % 2], vbs[cnt % 2]
                self.dma("sp", q_.t[:, :], self.QT.t[hd, :, 0:2048], [self.QT], [q_])
                self.dma("sp", k_.t[:, :], self.KTb.t[hd, :, 0:2048], [self.KTb], [k_])
                for r in range(dil):
                    src = self.Vtm.t[g, 0:2048, i * 128:(i + 1) * 128].rearrange("(n j r) d -> r j n d", j=128, r=dil)[r]
                    self.dma("sp", vf.t[:, r * nb:(r + 1) * nb, :], src, [self.Vtm], [vf])
                self.copy("act", v_.t[:], vf.t[:], [vf], [v_])
                for r in range(dil):
                    for n in range(nb):
                        block(g, q_, k_, v_, dil, nb, r, n)

            cnt = 0
            for i in range(4):
                for g in range(3):
                    head(i, g, cnt)
                    cnt += 1
                self.op("dve", lambda h: h.reciprocal(rd.t[:], DEN.t[:]), [DEN], [rd])
                self.op("dve", lambda h: h.tensor_tensor(ob.t[:], NUM.t[:], rd.t[:], ALU.mult), [NUM, rd], [ob])
                self.dma("sp", self.opT1.t[i * 128:(i + 1) * 128, 0:2048], ob.t[:], [ob], [self.opT1])

    def phase_kv_out_prompt(self):
        KEEP = (128, 512, 2048)
        outs = [self.outp("o_kvp%d" % g, [KEEP[g], 2, 4, 128]) for g in range(3)]
        with self.phase() as st:
            kf = self.sb(st, "ko_kf", [128, 2048], F32)
            ots = [self.sb(st, "ko_o%d" % i, [128, 4, 128], F32) for i in range(2)]
            cnt = 0
            for g in range(3):
                keep = KEEP[g]
                for r0 in range(0, keep, 512):
                    r1 = min(keep, r0 + 512)
                    self.dma("sp", outs[g].t[r0:r1, 1, :, :], self.Vtm.t[g, 2048 - keep + r0:2048 - keep + r1, :].rearrange("t (h d) -> t h d", d=128), [self.Vtm], [outs[g]])
                for i in range(4):
                    hd = g * 4 + i
                    self.dma("sp", kf.t[:, 0:keep], self.KTf.t[hd, :, 2048 - keep:2048], [self.KTf], [kf])
                    for tb in range(0, keep, 512):
                        nb_ = min(4, (keep - tb) // 128)
                        pf = self.next_pf()
                        for j in range(nb_):
                            self.op("pe", lambda h, pf=pf, j=j, tb=tb: h.transpose(pf.t[:, j * 128:(j + 1) * 128], kf.t[:, tb + j * 128:tb + (j + 1) * 128], self.identf.t[:, :]), [kf, self.identf], [pf])
                        o = ots[cnt % 2]
                        cnt += 1
                        self.copy(self.evac_engine(), o.t[:, 0:nb_, :], pf.t[:, 0:nb_ * 128].rearrange("p (j d) -> p j d", d=128), [pf], [o])
                        self.dma("sp", outs[g].t[tb:tb + nb_ * 128, 0, i, :].rearrange("(j p) d -> p j d", p=128), o.t[:, 0:nb_, :], [o], [outs[g]])

    def phase_dswa_sample(self):
        ROWS = (128, 512, 2048)
        MOFF = (0, 1, 5)
        csw = [self.inp("cswa%d" % g, [2, ROWS[g], 2, 4, 128]) for g in range(3)]
        sm_in = self.inp("c_smask", [128, 24, 8])
        outs = [self.outp("o_kvs%d" % g, [2, ROWS[g], 2, 4, 128]) for g in range(3)]
        SC = 128 ** -0.5
        self.pf_lim = 4
        accn, accd = self.pf[4], self.pf[5]
        with self.phase() as st:
            sm = self.sb(st, "ds_sm", [128, 24, 8], BF16)
            self.dma("pool", sm.t[:], sm_in.t, [sm_in], [sm])
            kf = self.sb(st, "ds_kf", [128, 16, 128], F32)
            vf = self.sb(st, "ds_vf", [128, 16, 128], F32)
            kb_ = self.sb(st, "ds_kb", [128, 16, 128], BF16)
            vb_ = self.sb(st, "ds_vb", [128, 16, 128], BF16)
            kTs = [self.sb(st, "ds_kT%d" % i, [128, 128], BF16) for i in range(2)]
            q8 = self.sb(st, "ds_q8", [128, 8], BF16)
            k8 = self.sb(st, "ds_k8", [128, 8], BF16)
            k8f = self.sb(st, "ds_k8f", [128, 8], F32)
            v8f = self.sb(st, "ds_v8f", [8, 128], F32)
            v8 = self.sb(st, "ds_v8", [8, 128], BF16)
            ers = [self.sb(st, "ds_er%d" % i, [128, 8], BF16) for i in range(2)]
            Es = [self.sb(st, "ds_E%d" % i, [128, 8], BF16) for i in range(2)]
            rd = self.sb(st, "ds_rd", [128, 8], F32)
            ob = self.sb(st, "ds_ob", [128, 8], BF16)
            k8o = self.sb(st, "ds_k8o", [8, 128], F32)
            self._ci = 0

            def group(s, i, g):
                hd = g * 4 + i
                nblk = ROWS[g] // 128
                c8 = 2048 + 8 * s
                self.dma("sp", kf.t[:, 0:nblk, :], csw[g].t[s, :, 0, i, :].rearrange("(n p) d -> p n d", p=128), [csw[g]], [kf])
                self.dma("sp", vf.t[:, 0:nblk, :], csw[g].t[s, :, 1, i, :].rearrange("(n p) d -> p n d", p=128), [csw[g]], [vf])
                self.copy("act", kb_.t[:, 0:nblk, :], kf.t[:, 0:nblk, :], [kf], [kb_])
                self.copy("dve", vb_.t[:, 0:nblk, :], vf.t[:, 0:nblk, :], [vf], [vb_])
                self.dma("sp", q8.t[:], self.QT.t[hd, :, c8:c8 + 8], [self.QT], [q8])
                self.dma("sp", k8.t[:], self.KTb.t[hd, :, c8:c8 + 8], [self.KTb], [k8])
                self.dma("sp", k8f.t[:], self.KTf.t[hd, :, c8:c8 + 8], [self.KTf], [k8f])
                self.dma("sp", v8f.t[:], self.Vtm.t[g, c8:c8 + 8, i * 128:(i + 1) * 128], [self.Vtm], [v8f])
                self.copy("dve", v8.t[:], v8f.t[:], [v8f], [v8])
                for blk in range(nblk):
                    kT = kTs[self._ci % 2]
                    er, E = ers[self._ci % 2], Es[self._ci % 2]
                    self._ci += 1
                    first = (g == 0 and blk == 0)
                    pb = self.next_pb()
                    self.op("pe", lambda h, pb=pb, blk=blk: h.transpose(pb.t[:, 0:128], kb_.t[:, blk, :], self.identb.t[:, :]), [kb_, self.identb], [pb])
                    self.copy("dve", kT.t[:], pb.t[:, 0:128], [pb], [kT])
                    ps = self.next_pf()
                    self.op("pe", lambda h, ps=ps, kT=kT: h.matmul(ps.t[:, 0:8], kT.t[:, :], q8.t[:, :], start=True, stop=True), [kT, q8], [ps])
                    self.op("act", lambda h, ps=ps, er=er: h.activation(er.t[:], ps.t[:, 0:8], AF.Exp, scale=SC), [ps], [er])
                    self.op("dve", lambda h, er=er, E=E, blk=blk: h.tensor_tensor(E.t[:], er.t[:], sm.t[:, MOFF[g] + blk, :], ALU.mult), [er, sm], [E])
                    self.op("pe", lambda h, E=E, blk=blk, first=first: h.matmul(accn.t[:, 0:8], vb_.t[:, blk, :], E.t[:], start=first, stop=False), [vb_, E], [accn])
                    self.op("pe", lambda h, E=E, first=first: h.matmul(accd.t[:, 0:8], self.ones_b.t[:, :], E.t[:], start=first, stop=False), [self.ones_b, E], [accd])
                er, E = ers[self._ci % 2], Es[self._ci % 2]
                self._ci += 1
                last = (g == 2)
                ps = self.next_pf()
                self.op("pe", lambda h: h.matmul(ps.t[0:8, 0:8], k8.t[:, :], q8.t[:, :], start=True, stop=True), [k8, q8], [ps])
                self.op("act", lambda h: h.activation(er.t[0:8, :], ps.t[0:8, 0:8], AF.Exp, scale=SC), [ps], [er])
                self.op("dve", lambda h: h.tensor_tensor(E.t[0:8, :], er.t[0:8, :], sm.t[0:8, 21 + g, :], ALU.mult), [er, sm], [E])
                self.op("pe", lambda h: h.matmul(accn.t[:, 0:8], v8.t[0:8, :], E.t[0:8, :], start=False, stop=last), [v8, E], [accn])
                self.op("pe", lambda h: h.matmul(accd.t[:, 0:8], self.ones_b.t[0:8, :], E.t[0:8, :], start=False, stop=last), [self.ones_b, E], [accd])
                rows = ROWS[g]
                pt = self.next_pf()
                self.op("pe", lambda h: h.transpose(pt.t[0:8, 0:128], k8f.t[:, 0:8], self.identf.t[:, :]), [k8f, self.identf], [pt])
                self.copy("dve", k8o.t[:], pt.t[0:8, 0:128], [pt], [k8o])
                self.dma("sp", outs[g].t[s, rows - 8:rows, 0, i, :], k8o.t[:], [k8o], [outs[g]])

            for s in range(2):
                for g in range(3):
                    rows = ROWS[g]
                    for r0 in range(8, rows, 512):
                        if "ds_nocopy" in self.phases:
                            break
                        r1 = min(rows, r0 + 512)
                        self.dma("sp", outs[g].t[s, r0 - 8:r1 - 8], csw[g].t[s, r0:r1], [csw[g]], [outs[g]])
                    c8 = 2048 + 8 * s
                    self.dma("sp", outs[g].t[s, rows - 8:rows, 1, :, :], self.Vtm.t[g, c8:c8 + 8, :].rearrange("t (h d) -> t h d", d=128), [self.Vtm], [outs[g]])
                for i in range(4):
                    if "ds_noattn" in self.phases:
                        break
                    for g in range(3):
                        group(s, i, g)
                    self.op("dve", lambda h: h.reciprocal(rd.t[:], accd.t[:, 0:8]), [accd], [rd])
                    self.op("dve", lambda h: h.tensor_tensor(ob.t[:], accn.t[:, 0:8], rd.t[:], ALU.mult), [accn, rd], [ob])
                    c8 = 2048 + 8 * s
                    self.dma("sp", self.opT1.t[i * 128:(i + 1) * 128, c8:c8 + 8], ob.t[:], [ob], [self.opT1])
        self.pf_lim = 6

    def phase_final(self, xin):
        o_y = self.outp("o_y", [NTOK, D])
        with self.phase() as st:
            xs = [self.sb(st, "fy_x%d" % i, [128, KC, 128], F32) for i in range(2)]
            yts = [self.sb(st, "fy_y%d" % i, [128, D], F32) for i in range(2)]
            for bi, t0 in enumerate(range(0, NTOK, 128)):
                n = min(128, NTOK - t0)
                x, yt = xs[bi % 2], yts[bi % 2]
                self.dma("sp", x.t[:, :, 0:n], xin.t[:, :, t0:t0 + n], [xin], [x])
                for k0 in range(0, KC, 4):
                    pf = self.next_pf()
                    for j in range(4):
                        self.op("pe", lambda h, pf=pf, j=j, k0=k0, n=n, x=x: h.transpose(pf.t[0:n, j * 128:(j + 1) * 128], x.t[:, k0 + j, 0:n], self.identf.t[:, :]), [x, self.identf], [pf])
                    self.copy(self.evac_engine(), yt.t[0:n, k0 * 128:(k0 + 4) * 128], pf.t[0:n, 0:512], [pf], [yt])
                self.dma("sp", o_y.t[t0:t0 + n, :], yt.t[0:n, :], [yt], [o_y])

    def build(self):
        ph = self.phases
        self.setup_common()
        nol0 = "nol0" in ph
        wn = []
        if not nol0:
            self.phase_weights([("w_mem0", (4096, 2048)), ("w_mem1", (4096, 2048))])
            wn = ["w_in_a"]
        if "l0out" in ph:
            wn += ["w_out_a"]
        if "ffn0" in ph:
            wn += ["w_fi0", "w_fo0"]
        if "l1" in ph:
            wn += ["w_in_b"]
        if "l1out" in ph:
            wn += ["w_out_b", "w_fi1", "w_fo1"]
        first, rest = ([w for w in wn if w == "w_in_a"], [w for w in wn if w != "w_in_a"]) if not nol0 else (wn, [])
        self.phase_weights2(first)
        self.kb.barrier()
        if not nol0:
            self.phase_memkv()
            self.phase_l0_prep()
            self.phase_l0_inproj()
            late = [w for w in rest if w in ("w_in_b", "w_out_b", "w_fi1", "w_fo1")] if "ffn0" in ph else []
            self.phase_weights2([w for w in rest if w not in late])
            self.phase_rwkv()
        else:
            self.g_mix = self.inp("g_mix", [2, 128, KC])
            self.XT = [self.dram("XT%d" % i, [128, KC, NTOK], F32) for i in range(5)]
            self.hT_own = self.dram("hT_own", [D, NTOK], BF16)
            self.hT_pair = self.dram("hT_pair", [2 * D, NTOK], BF16)
        if "l0out" in ph:
            SEG = [(1, 2048, 0, 0), (2050, 8, 1, 2048), (2059, 8, 2, 2056)]
            self.phase_memattn(0, self.U0, 41, SEG, self.opT0, 1536)
            self.phase_outproj(self.opT0, 2048, "w_out_a", self.XT[0], self.XT[1], "o0")
        if "ffn0" in ph:
            if not nol0:
                self.phase_weights2(late)
            self.phase_ffn(0, self.XT[1], self.XT[2])
        if "l1" in ph:
            self.phase_l1_prep(self.XT[2])
            self.phase_l1_inproj()
            if "skipkvo" not in ph:
                self.phase_kv_out_prompt()
            if "skipdp" not in ph:
                self.phase_dswa_prompt()
            else:
                self.opT1 = self.dram("opT1", [1024, 2 * NTOK], BF16)
            if "skipds" not in ph:
                self.phase_dswa_sample()
        if "l1out" in ph:
            SEG = [(0, 2048, 0, 0), (2048, 8, 1, 2048), (2056, 8, 2, 2056)]
            self.phase_memattn(1, self.U1m, 0, SEG, self.opT1, 512)
            self.phase_outproj(self.opT1, 1024, "w_out_b", self.XT[2], self.XT[3], "o1")
            self.phase_ffn(1, self.XT[3], self.XT[4])
            self.phase_final(self.XT[4])
        self.kb.finish([t.b for t in self.outs.values()] + [t.b for n, t in self.scr.items() if n in DBG_OUT])
        self.kb.replay()


def _wshard(w, G, c):
    K, N = w.shape
    rows = K // (8 * G)
    return np.ascontiguousarray(w.reshape(G, 8, rows, N)[:, c])


def _cz(c):
    return c % 4, c // 4


def _wsh2(w, name, c):
    K, N, ru, mode = WSPEC2[name]
    nr = 4 if mode == "q4" else 8
    sel = (c % 4) if mode == "q4" else UPERM[c]
    out = {}
    nu = K // ru
    out[name] = np.ascontiguousarray(w[:nu * ru].reshape(nu, nr, ru // nr, N)[:, sel])
    if K % ru:
        rt = K % ru
        out[name + "_t"] = np.ascontiguousarray(w[nu * ru:].reshape(1, nr, rt // nr, N)[:, sel])
    return out


def _pm_in(w):
    return np.ascontiguousarray(w.reshape(KC, 128, DFF // 128, 256).transpose(2, 1, 0, 3).reshape(DFF, KC * 256))


def _pm_out(w):
    return np.ascontiguousarray(w.reshape(DFF // 128, 128, KC, 128).transpose(2, 1, 0, 3).reshape(4096, DFF))


def _gather_order(nrows, ncols, esz):
    rp = max(128, ((2 << 20) // (ncols * esz)) // 128 * 128)
    rk, lr = [], []
    r0 = 0
    while r0 < nrows:
        r1 = min(nrows, r0 + rp)
        for r in range(2):
            rk.append(np.full(r1 - r0, r))
            lr.append(np.arange(r0, r1))
        r0 = r1
    return np.concatenate(rk), np.concatenate(lr)


def _cols_a(z):
    r = np.arange(z * 1536, (z + 1) * 1536)
    return np.concatenate([r, 3072 + r, 6144 + r, np.arange(9216, 9792), 9792 + np.arange(z * 512, (z + 1) * 512)])


def _cols_b(z):
    heads = np.array([g * 8 + 4 * z + i for g in range(3) for i in range(4)])
    hc = (heads[:, None] * 128 + np.arange(128)[None, :]).reshape(-1)
    return np.concatenate([hc, 3072 + hc, 6144 + hc, 9216 + np.arange(z * 512, (z + 1) * 512)])


def _smask():
    m = np.zeros((128, 24, 8), np.float32)
    t = np.arange(8)[None, :]
    off = 0
    for g, (win, dil) in enumerate(((128, 1), (512, 4), (2048, 16))):
        rows = win
        for blk in range(rows // 128):
            p = (blk * 128 + np.arange(128))[:, None]
            m[:, off + blk, :] = (((rows + t - p) % dil) == 0) & (p >= t + rows - win)
        off += rows // 128
        tp = np.arange(8)[:, None]
        m[0:8, 21 + g, :] = (((t - tp) % dil) == 0) & (tp <= t)
    return m


def _chunk_rows(v, z):
    cols = _cols_a(z)
    vv = v[cols] if v.shape[0] >= A_IN else np.concatenate([v, np.zeros(A_IN - v.shape[0], v.dtype)])[cols]
    out = np.zeros((45, 128), np.float32)
    out[0:36] = vv[0:4608].reshape(36, 128)
    out[36, :96] = vv[4608:4704]
    out[37, :96] = vv[4704:4800]
    out[38:41] = vv[4800:5184].reshape(3, 128)
    out[41:45] = vv[5184:5696].reshape(4, 128)
    return out


_WITH_RWKV = [False]


def _host_l0in(inp, c):
    b, z = _cz(c)
    m = {}
    m.update(_wsh2(np.ascontiguousarray(inp["w_in_a"][0][:, _cols_a(z)]), "w_in_a", c))
    m["x_own"] = np.ascontiguousarray(np.concatenate([inp["x_prompt"][b, z * HALF:(z + 1) * HALF], inp["x_sample"][2 * b + z]], axis=0))
    m["g_mix"] = np.ascontiguousarray(inp["norm_mix"].reshape(2, KC, 128).transpose(0, 2, 1))
    sh = np.stack([_chunk_rows(inp["state_shift"][0, 2 * b + j, 0], z) for j in range(2)], axis=-1)
    m["shcol"] = np.ascontiguousarray(sh.transpose(1, 0, 2))
    ch = slice(z * 1536, (z + 1) * 1536)
    f = lambda v: v[ch].reshape(12, 128).T
    mu = inp["mu_a"][0]
    pa = np.stack([f(mu[0:3072]), f(mu[3072:6144]), f(mu[6144:9216]), f(inp["w0_a"][0]), f(inp["a0_a"][0]), f(inp["kk_a"][0]),
                   f(inp["ka_a"][0]), f(inp["rk_a"][0].reshape(-1)), f(inp["lnx_g_a"][0]), f(inp["lnx_b_a"][0])], axis=1)
    m["pa"] = np.ascontiguousarray(pa.astype(np.float32))
    pl = np.zeros((128, 5), np.float32)
    pl[:96, 0] = mu[9216:9312]
    pl[:96, 1] = mu[9312:9408]
    pl[:, 2:5] = mu[9408:9792].reshape(3, 128).T
    m["pl"] = pl
    m["w2z"] = np.ascontiguousarray(inp["w2_a"][0][:, ch])
    m["a2z"] = np.ascontiguousarray(inp["a2_a"][0][:, ch])
    m["g2z"] = np.ascontiguousarray(inp["g2_a"][0][:, ch].reshape(3, 128, 1536).transpose(1, 0, 2))
    m["wkv_in"] = np.ascontiguousarray(inp["state_wkv"][0, 2 * b:2 * b + 2, z * HH:(z + 1) * HH])
    ii = np.arange(128)
    strict = (ii[:, None] < ii[None, :]).astype(np.float32)
    incl = (ii[:, None] <= ii[None, :]).astype(np.float32)
    bones = (ii[:, None] // 64 == ii[None, :] // 64).astype(np.float32)
    lower = (ii[:, None] > ii[None, :]).astype(np.float32)
    m["c_masks"] = np.stack([strict, incl, bones, lower])
    return m


def make_in_maps(inp, phases):
    maps = []
    shared = {}
    if "l0out" in phases:
        rk, lr = _gather_order(2048, 2 * NTOK, 2)
        perm = np.where(lr < 1536, rk * 1536 + lr, 3072 + rk * 512 + (lr - 1536))
        shared["w_out_a"] = np.ascontiguousarray(inp["w_out_a"][0][perm])
    if "ffn0" in phases:
        jj = np.arange(DFF).reshape(DFF // 128, 128)
        cols = np.concatenate([jj, DFF + jj], axis=1).reshape(-1)
        shared["w_fi0"] = _pm_in(inp["w_ffn_in"][0][:, cols])
        shared["w_fo0"] = _pm_out(inp["w_ffn_out"][0])
    if "l1out" in phases:
        rk, lr = _gather_order(1024, 2 * NTOK, 2)
        perm = np.where(lr < 512, rk * 512 + lr, 1024 + rk * 512 + (lr - 512))
        shared["w_out_b"] = np.ascontiguousarray(inp["w_out_b"][0][perm])
        jj = np.arange(DFF).reshape(DFF // 128, 128)
        cols = np.concatenate([jj, DFF + jj], axis=1).reshape(-1)
        shared["w_fi1"] = _pm_in(inp["w_ffn_in"][1][:, cols])
        shared["w_fo1"] = _pm_out(inp["w_ffn_out"][1])
    ii = np.arange(128)
    amask = np.concatenate([(ii[:, None] <= ii[None, :]), (ii[:, None] >= ii[None, :])], axis=1).astype(np.float32)
    smask = _smask()
    for c in range(NCORES):
        b, z = _cz(c)
        m = {"c_identf": np.eye(128, dtype=np.float32)}
        if "l1" in phases:
            m.update(_wsh2(np.ascontiguousarray(inp["w_in_b"][0][:, _cols_b(z)]), "w_in_b", c))
            m["g_qkb"] = np.ascontiguousarray(np.stack([inp["q_norm_b"][0], inp["k_norm_b"][0]], axis=1))
            m["c_amask"] = amask
            m["c_smask"] = smask
            for g, nm in enumerate(("cache_swa_kv1", "cache_swa_kv2", "cache_swa_kv3")):
                m["cswa%d" % g] = np.ascontiguousarray(inp[nm][0, 2 * b:2 * b + 2, :, :, 4 * z:4 * z + 4, :])
        m["c_sel"] = np.ascontiguousarray(np.broadcast_to(np.array([1.0 - z, float(z)], np.float32), (128, 2)))
        m["w_mem0"] = np.ascontiguousarray(inp["w_mem_kv"][0])
        m["w_mem1"] = np.ascontiguousarray(inp["w_mem_kv"][1])
        m["memp"] = np.ascontiguousarray(inp["mem_prompt"][b])
        m["g_mem"] = np.ascontiguousarray(inp["norm_mem"].reshape(2, KC, 128).transpose(0, 2, 1))
        m["g_kmem"] = np.ascontiguousarray(np.broadcast_to(inp["k_norm_mem"][:, None, :], (2, 128, 256)))
        m.update(_host_l0in(inp, c))
        if "l0out" in phases:
            m["cmem"] = np.ascontiguousarray(inp["cache_mem_kv"][:, 2 * b:2 * b + 2, :, :, 2 * z:2 * z + 2, :])
            m["g_qmem"] = np.ascontiguousarray(inp["q_norm_mem"].reshape(2, 2, 128).transpose(2, 0, 1))
        if "ffn0" in phases:
            m["g_ffn"] = np.ascontiguousarray(inp["norm_ffn"].reshape(2, KC, 128).transpose(0, 2, 1))
        for k, w in shared.items():
            m.update(_wsh2(w, k, c))
        maps.append(m)
    return maps


def run(inp, phases):
    nc = bass.Bass("TRN2", target_bir_lowering=False)
    with contextlib.ExitStack() as es:
        mk = MK(nc, es, phases)
        mk.build()
    maps = make_in_maps(inp, phases)
    maps = [{k: v for k, v in m.items() if k in mk.ins} for m in maps]
    missing = [k for k in mk.ins if k not in maps[0]]
    assert not missing, missing
    res = run_bass_kernel_spmd(nc, maps, core_ids=list(range(NCORES)))
    return res.results


ALL_PHASES = ("l0out", "ffn0", "l1", "l1out")

_OUT_SHAPES = [
    (4, 2048, 4096), (8, 8, 4096), (1, 4, 48, 64, 64), (1, 4, 1, A_SHIFT),
    (1, 4, 128, 2, 8, 128), (1, 4, 512, 2, 8, 128), (1, 4, 2048, 2, 8, 128),
    (2, 4, NMEM, 2, 4, 256),
    (1, 8, 48, 64, 64), (1, 8, 1, A_SHIFT),
    (1, 8, 128, 2, 8, 128), (1, 8, 512, 2, 8, 128), (1, 8, 2048, 2, 8, 128),
]


def _unchunk(a):
    return np.concatenate([a[0:36].reshape(-1), a[36, :96], a[37, :96], a[38:41].reshape(-1), a[41:45].reshape(-1)])


def kernel(**inputs):
    inp = {k: np.asarray(v) for k, v in inputs.items()}
    res = run(inp, ALL_PHASES)
    outs = [np.zeros(sh, np.float32) for sh in _OUT_SHAPES]
    for b in range(4):
        outs[7][:, b] = res[b]["o_memkv"].reshape(2, NMEM, 2, 4, 256)
        for z in range(2):
            r = res[z * 4 + b]
            hsl = slice(z * HH, (z + 1) * HH)
            outs[2][0, b, hsl] = r["o_wkv"][0]
            outs[8][0, 2 * b, hsl] = r["o_wkv"][1]
            outs[8][0, 2 * b + 1, hsl] = r["o_wkv"][2]
            o = r["o_shift0"].transpose(1, 0, 2)
            cols = _cols_a(z)
            sel = cols < A_SHIFT
            for j, dst in enumerate((outs[3][0, b, 0], outs[9][0, 2 * b, 0], outs[9][0, 2 * b + 1, 0])):
                dst[cols[sel]] = _unchunk(o[:, :, j])[sel]
            for g in range(3):
                outs[4 + g][0, b, :, :, 4 * z:4 * z + 4, :] = r["o_kvp%d" % g]
                outs[10 + g][0, 2 * b, :, :, 4 * z:4 * z + 4, :] = r["o_kvs%d" % g][0]
                outs[10 + g][0, 2 * b + 1, :, :, 4 * z:4 * z + 4, :] = r["o_kvs%d" % g][1]
            outs[0][b, z * HALF:(z + 1) * HALF] = r["o_y"][:HALF]
            outs[1][2 * b + z] = r["o_y"][HALF:]
    return tuple(outs)
```
